# Optimizing a Trainium2 kernel written in Bass

```python
import jax, jax.numpy as jnp
from jax import lax
import numpy as np

D_MODEL = 1024
BATCH = 16
SEQ = 2048
DEPTH = 1

MEM_LEN = 256
POOL_WINDOWS = (2, 4, 8, 16)
POOL_GROUPS = 4
POOL_GROUP_DIM = D_MODEL // 8
POOL_WIDTH = POOL_GROUPS * POOL_GROUP_DIM
RET_HEADS = 4
RET_QK_DIM = D_MODEL // 8
RET_V_DIM = 2 * RET_QK_DIM
RET_QK_WIDTH = RET_HEADS * RET_QK_DIM
RET_V_WIDTH = RET_HEADS * RET_V_DIM
RET_CHUNK = 128
ROPE_BASE = 10000.0
XA_HEADS = 4
XA_HEAD_DIM = D_MODEL // 8
XA_WIDTH = XA_HEADS * XA_HEAD_DIM
N_BRANCH = 3
IN_WIDTH = POOL_WIDTH + 2 * RET_QK_WIDTH + 2 * RET_V_WIDTH + XA_WIDTH + N_BRANCH * D_MODEL
FFN_HIDDEN = 2816
CONV_WIDTH = 3
EPS = 1e-6

kernel_name = "hybrid_pool_retention_memxattn_convglu"


def rmsnorm(x, g):
    x32 = x.astype(jnp.float32)
    y = x32 * lax.rsqrt(jnp.mean(x32 * x32, axis=-1, keepdims=True) + EPS)
    return y.astype(x.dtype) * g


def pool_mixer(hp, w_pool, pool_scale):
    B, S, _ = hp.shape
    h32 = hp.astype(jnp.float32)
    cs = jnp.cumsum(h32, axis=1)
    t1 = jnp.arange(1, S + 1, dtype=jnp.float32)[None, :, None]
    outs = []
    for gi, w in enumerate(POOL_WINDOWS):
        c = cs[..., gi * POOL_GROUP_DIM:(gi + 1) * POOL_GROUP_DIM]
        c_shift = jnp.pad(c, ((0, 0), (w, 0), (0, 0)))[:, :S]
        outs.append((c - c_shift) / jnp.minimum(t1, float(w)))
    pooled = jnp.concatenate(outs, axis=-1) - h32
    pooled = pooled.astype(hp.dtype).reshape(B, S, POOL_GROUPS, POOL_GROUP_DIM)
    y = jnp.einsum('bsgc,gcd->bsgd', pooled, w_pool).reshape(B, S, POOL_WIDTH)
    return y * pool_scale


def rotary(x, pos):
    half = x.shape[-1] // 2
    inv = ROPE_BASE ** (-jnp.arange(half, dtype=jnp.float32) / half)
    ang = pos[:, None] * inv[None, :]
    cos = jnp.cos(ang)[None, :, None, :]
    sin = jnp.sin(ang)[None, :, None, :]
    x1, x2 = x[..., :half], x[..., half:]
    return jnp.concatenate([x1 * cos - x2 * sin, x1 * sin + x2 * cos], axis=-1)


def chunkwise_retention(q, k, v):
    B, S, H, dk = q.shape
    dv = v.shape[-1]
    C = RET_CHUNK
    N = S // C
    log_gamma = jnp.log1p(-jnp.exp2(-5.0 - jnp.arange(H, dtype=jnp.float32)))
    lg = log_gamma[:, None, None]
    idx = jnp.arange(C, dtype=jnp.float32)
    rel = idx[:, None] - idx[None, :]
    decay_intra = jnp.where(rel >= 0, jnp.exp(jnp.maximum(rel, 0.0) * lg), 0.0)
    q_decay = jnp.exp((idx + 1.0)[None, :, None] * lg)
    k_decay = jnp.exp((C - 1.0 - idx)[None, :, None] * lg)
    chunk_decay = jnp.exp(C * lg)

    def to_chunks(a):
        d = a.shape[-1]
        return a.reshape(B, N, C, H, d).transpose(1, 0, 3, 2, 4)

    qc, kc, vc = to_chunks(q), to_chunks(k * (dk ** -0.5)), to_chunks(v)

    def step(R, inp):
        qi, ki, vi = inp
        s = jnp.einsum('bhqd,bhkd->bhqk', qi, ki) * decay_intra
        o = (jnp.einsum('bhqk,bhkv->bhqv', s, vi)
             + jnp.einsum('bhqd,bhdv->bhqv', qi * q_decay, R))
        R = chunk_decay * R + jnp.einsum('bhkd,bhkv->bhdv', ki * k_decay, vi)
        return R, o

    R0 = jnp.zeros((B, H, dk, dv), jnp.float32)
    _, o = lax.scan(step, R0, (qc, kc, vc))
    return o.transpose(1, 0, 3, 2, 4).reshape(B, S, H, dv)


def retention_branch(q, k, v, gr, g_ret, b_ret):
    B, S, _ = q.shape
    pos = jnp.arange(S, dtype=jnp.float32)
    q4 = rotary(q.astype(jnp.float32).reshape(B, S, RET_HEADS, RET_QK_DIM), pos)
    k4 = rotary(k.astype(jnp.float32).reshape(B, S, RET_HEADS, RET_QK_DIM), pos)
    v4 = v.astype(jnp.float32).reshape(B, S, RET_HEADS, RET_V_DIM)
    o = chunkwise_retention(q4, k4, v4)
    mu = jnp.mean(o, axis=-1, keepdims=True)
    var = jnp.mean(jnp.square(o - mu), axis=-1, keepdims=True)
    o = ((o - mu) * lax.rsqrt(var + EPS)).reshape(B, S, RET_V_WIDTH).astype(q.dtype)
    o = o * g_ret + b_ret
    return jax.nn.silu(gr) * o


def memory_cross_attention(qx, mem_n, w_mem_kv):
    B, S, _ = qx.shape
    M = mem_n.shape[1]
    q = qx.reshape(B, S, XA_HEADS, XA_HEAD_DIM)
    kv = mem_n @ w_mem_kv
    k, v = jnp.split(kv, 2, axis=-1)
    k = k.reshape(B, M, XA_HEADS, XA_HEAD_DIM)
    v = v.reshape(B, M, XA_HEADS, XA_HEAD_DIM)
    s = jnp.einsum('bshd,bmhd->bhsm', q, k).astype(jnp.float32) * (XA_HEAD_DIM ** -0.5)
    p = jax.nn.softmax(s, axis=-1).astype(v.dtype)
    o = jnp.einsum('bhsm,bmhd->bshd', p, v)
    return o.reshape(B, S, XA_WIDTH)


def conv_glu_ffn(h, w_up, conv_w, conv_b, w_down):
    S = h.shape[1]
    up = h @ w_up
    a, b = jnp.split(up, 2, axis=-1)
    a_pad = jnp.pad(a, ((0, 0), (CONV_WIDTH - 1, 0), (0, 0)))
    a = sum(a_pad[:, j:j + S] * conv_w[j] for j in range(CONV_WIDTH)) + conv_b
    return (jax.nn.gelu(a) * b) @ w_down


def setup_inputs(seed: int = 0) -> dict:
    key = jax.random.key(seed)
    ks = jax.random.split(key, 24)
    f32 = jnp.float32

    def nrm(k, shape, fan_in):
        return jax.random.normal(k, shape, f32) * (fan_in ** -0.5)

    def gain(k, shape):
        return 1.0 + 0.02 * jax.random.normal(k, shape, f32)

    L = DEPTH
    return {
        "x": jax.random.normal(ks[0], (BATCH, SEQ, D_MODEL), f32),
        "mem": jax.random.normal(ks[1], (BATCH, MEM_LEN, D_MODEL), f32),
        "g_mix": gain(ks[2], (L, D_MODEL)),
        "w_in": nrm(ks[3], (L, D_MODEL, IN_WIDTH), D_MODEL),
        "w_pool": nrm(ks[4], (L, POOL_GROUPS, POOL_GROUP_DIM, POOL_GROUP_DIM), POOL_GROUP_DIM),
        "pool_scale": 1.0 + 0.1 * jax.random.normal(ks[5], (L, POOL_WIDTH), f32),
        "w_a": nrm(ks[6], (L, POOL_WIDTH, D_MODEL), POOL_WIDTH),
        "g_ret": gain(ks[7], (L, RET_V_WIDTH)),
        "b_ret": 0.01 * jax.random.normal(ks[8], (L, RET_V_WIDTH), f32),
        "w_r": nrm(ks[9], (L, RET_V_WIDTH, D_MODEL), RET_V_WIDTH),
        "g_mem": gain(ks[10], (L, D_MODEL)),
        "w_mem_kv": nrm(ks[11], (L, D_MODEL, 2 * XA_WIDTH), D_MODEL),
        "w_c": nrm(ks[12], (L, XA_WIDTH, D_MODEL), XA_WIDTH),
        "w_out": nrm(ks[13], (L, D_MODEL, D_MODEL), D_MODEL),
        "g_ffn": gain(ks[14], (L, D_MODEL)),
        "w_up": nrm(ks[15], (L, D_MODEL, 2 * FFN_HIDDEN), D_MODEL),
        "conv_w": nrm(ks[16], (L, CONV_WIDTH, FFN_HIDDEN), CONV_WIDTH),
        "conv_b": 0.01 * jax.random.normal(ks[17], (L, FFN_HIDDEN), f32),
        "w_down": nrm(ks[18], (L, FFN_HIDDEN, D_MODEL), FFN_HIDDEN),
        "g_final": gain(ks[19], (D_MODEL,)),
    }


def reference(x, mem, g_mix, w_in, w_pool, pool_scale, w_a, g_ret, b_ret, w_r,
              g_mem, w_mem_kv, w_c, w_out, g_ffn, w_up, conv_w, conv_b, w_down, g_final):
    splits = list(np.cumsum([POOL_WIDTH, RET_QK_WIDTH, RET_QK_WIDTH, RET_V_WIDTH,
                             RET_V_WIDTH, XA_WIDTH]))
    for l in range(DEPTH):
        h = rmsnorm(x, g_mix[l])
        proj = h @ w_in[l]
        hp, q, k, v, gr, qx, gl = jnp.split(proj, splits, axis=-1)
        y_pool = pool_mixer(hp, w_pool[l], pool_scale[l]) @ w_a[l]
        y_ret = retention_branch(q, k, v, gr, g_ret[l], b_ret[l]) @ w_r[l]
        mem_n = rmsnorm(mem, g_mem[l])
        y_mem = memory_cross_attention(qx, mem_n, w_mem_kv[l]) @ w_c[l]
        gate_pool, gate_ret, gate_mem = jnp.split(gl, N_BRANCH, axis=-1)
        merged = (jax.nn.sigmoid(gate_pool) * y_pool
                  + jax.nn.sigmoid(gate_ret) * y_ret
                  + jax.nn.sigmoid(gate_mem) * y_mem)
        x = x + merged @ w_out[l]
        x = x + conv_glu_ffn(rmsnorm(x, g_ffn[l]), w_up[l], conv_w[l], conv_b[l], w_down[l])
    return rmsnorm(x, g_final)
```

```python
import numpy as np
import concourse.bass as bass
import concourse.mybir as mybir
from concourse.bass_utils import run_bass_kernel_spmd

F32 = mybir.dt.float32
BF16 = mybir.dt.bfloat16
AF = mybir.ActivationFunctionType
ALU = mybir.AluOpType

NCORES = 8
D = 1024
SEQ = 2048
T = 512
NT = T // 128
NG = 2 * SEQ // T
GPS = SEQ // T
MEM = 256
FH = 2816
NF = FH // 128
EPS = 1e-6
NS = 4


class Res:
    __slots__ = ("name", "writer", "readers")

    def __init__(self, name):
        self.name = name
        self.writer = None
        self.readers = []


class Op:
    __slots__ = ("eng", "fn", "deps", "signal", "sem", "val", "inc", "name")


class DmaSem:
    def __init__(self, nc, name):
        self.sem = nc.alloc_semaphore(name)
        self.count = 0
        self.ops = []


class Sched:
    ENGS = ("pe", "act", "dve", "pool", "sp")

    def __init__(self, nc):
        self.nc = nc
        self.ops = {e: [] for e in self.ENGS}
        self.esem = {e: nc.alloc_semaphore("es_" + e) for e in ("pe", "act", "dve", "pool")}

    def op(self, eng, fn, reads=(), writes=(), dma=None, name=None, nodep=()):
        o = Op()
        o.eng = eng
        o.fn = fn
        o.name = name
        o.signal = False
        o.sem = None
        o.val = None
        o.inc = 1
        deps = []
        for r in reads:
            if r.writer is not None:
                deps.append(r.writer)
        for w in writes:
            if w.writer is not None:
                deps.append(w.writer)
            deps.extend(w.readers)
        seen = set()
        fd = []
        for d in deps:
            if id(d) in seen or d is o or d in nodep:
                continue
            seen.add(id(d))
            if d.eng == "pe" and eng == "pe":
                continue
            fd.append(d)
        o.deps = fd
        for d in fd:
            d.signal = True
        if dma is not None:
            dma.count += 16
            o.sem = dma.sem
            o.val = dma.count
            o.inc = 16
            o.signal = True
            dma.ops.append(o)
        for r in reads:
            r.readers.append(o)
        for w in writes:
            w.writer = o
            w.readers = []
        self.ops[eng].append(o)
        return o

    def finalize(self):
        for e in ("pe", "act", "dve", "pool"):
            c = 0
            for o in self.ops[e]:
                if o.sem is None and o.signal:
                    c += 1
                    o.sem = self.esem[e]
                    o.val = c
                    o.inc = 1
        for e in self.ENGS:
            for o in self.ops[e]:
                if o.signal and o.sem is None:
                    raise RuntimeError("signal op without sem: %s" % o.name)

    def emit(self, block):
        self.finalize()
        sched = self

        def run(eng_name, eng):
            waited = {}
            for o in sched.ops[eng_name]:
                need = {}
                for d in o.deps:
                    k = id(d.sem)
                    if k not in need or need[k][1] < d.val:
                        need[k] = (d.sem, d.val)
                for k, (sem, val) in need.items():
                    if waited.get(k, 0) >= val:
                        continue
                    eng.wait_ge(sem, val)
                    waited[k] = val
                last = o.fn(eng)
                if o.signal:
                    if last is None:
                        raise RuntimeError("op %s returned no instruction" % o.name)
                    last.then_inc(o.sem, o.inc)

        @block.tensor
        def _(eng):
            run("pe", eng)

        @block.scalar
        def _(eng):
            run("act", eng)

        @block.vector
        def _(eng):
            run("dve", eng)

        @block.gpsimd
        def _(eng):
            run("pool", eng)

        @block.sync
        def _(eng):
            run("sp", eng)


def build_program(dbg=None):
    nc = bass.Bass("TRN2", target_bir_lowering=False)

    def din(name, shape):
        return nc.dram_tensor(name, list(shape), F32, kind="ExternalInput").ap()

    x_d = din("x", [2 * SEQ, D])
    mem_d = din("mem", [2 * MEM, D])
    w_in_d = din("w_in", [D, 7168])
    w_pool_d = din("w_pool", [4, 128, 128])
    w_a_d = din("w_a", [512, D])
    w_r_d = din("w_r", [D, D])
    w_kv_d = din("w_mem_kv", [D, D])
    w_c_d = din("w_c", [512, D])
    w_out_d = din("w_out", [D, D])
    w_up_d = din("w_up", [D, 2 * FH])
    w_down_d = din("w_down", [FH, D])
    gvec_d = din("gvec", [4, D])
    pp_d = din("pp", [128, 108])
    ropet_d = din("ropet", [16, 128, 1024])
    mask_d = din("mask4", [128, 512])
    cst_d = din("cst", [128, 14, 128])
    y_d = nc.dram_tensor("y", [2 * SEQ, D], F32, kind="ExternalOutput").ap()
    dbg_d = {}
    if dbg:
        for nm, shp in dbg.items():
            dbg_d[nm] = nc.dram_tensor("dbg_" + nm, list(shp), F32, kind="ExternalOutput").ap()

    S = Sched(nc)

    def sb(name, shape, dt):
        return nc.alloc_sbuf_tensor("s_" + name, list(shape), dt)

    xres = sb("xres", [128, NT, D], F32)
    hT = sb("hT", [128, 8, T], BF16)
    hb = [sb("hb%d" % i, [128, D], BF16) for i in range(2)]
    junk = sb("junk", [128, D], BF16)
    ss = sb("ss", [128, NT], F32)
    rstd = sb("rstd", [128, NT], F32)
    ssf = sb("ssf", [128, NT], F32)
    rstdf = sb("rstdf", [128, NT], F32)
    r1f = sb("r1f", [128, 7680], F32)
    r1b = r1f.bitcast(BF16)
    v_v = r1b[:, 0:4096].rearrange("p (a b) -> p a b", a=NT)
    retT_v = r1b[:, 0:4096].rearrange("p (a b) -> p a b", a=8)
    ktok_v = r1b[:, 4096:6144].rearrange("p (a b) -> p a b", a=NT)
    kT_v = r1b[:, 6144:8192].rearrange("p (a b) -> p a b", a=4)
    qT_v = r1b[:, 8192:10240].rearrange("p (a b) -> p a b", a=4)
    on_v = r1b[:, 10240:14336].rearrange("p (a b) -> p a b", a=NT)
    gT_v = r1b[:, 0:NF * T].rearrange("p (a b) -> p a b", a=NF)
    acc_v = [r1f[:, 5632 + i * 512: 5632 + (i + 1) * 512] for i in range(2)]
    g1_v = [r1f[:, 6656 + i * 512: 6656 + (i + 1) * 512] for i in range(2)]
    mergedF = sb("mergedT", [128, 8 * T], F32)
    mergedT = mergedF[:, :].rearrange("p (a b) -> p a b", a=8)
    xn_v = mergedF[:, :].rearrange("p (a b) -> p a b", a=NT)
    r2 = sb("r2", [128, 4096], BF16)
    pooledT_v = r2[:, 0:2048].rearrange("p (a b) -> p a b", a=4)
    ypT_v = r2[:, 2048:4096].rearrange("p (a b) -> p a b", a=4)
    mbf_v = r2[:, 0:4096].rearrange("p (a b) -> p a b", a=8)
    hp_tok = sb("hp_tok", [128, NT + 1, 512], BF16)
    qxT = sb("qxT", [128, 4, T], BF16)
    oT = sb("oT", [128, 4, T], BF16)
    pT = [sb("pT%d" % i, [128, T], BF16) for i in range(4)]
    rden = sb("rden", [128, T], F32)
    th = [sb("th%d" % i, [128, T], F32) for i in range(2)]
    tt = [sb("tt%d" % i, [128, T], F32) for i in range(2)]
    rot = [sb("rot%d" % i, [128, 4, 4, 64], F32) for i in range(1)]
    krot = [sb("krot%d" % i, [128, 4, 128], BF16) for i in range(2)]
    ssb = [sb("ssb%d" % i, [128, 4, 128], BF16) for i in range(3)]
    ropes = [sb("ropes%d" % i, [128, 2, 4, 64], F32) for i in range(2)]
    carry = sb("carry", [128, NF, 2], F32)
    bnd = sb("bnd", [128, NF, 2], F32)
    btmp = sb("btmp", [128, 2, NF], F32)
    memT = sb("memT", [128, 8, MEM], BF16)
    kmT = sb("kmT", [128, 4, MEM], BF16)
    vm = sb("vm", [128, 2, 512], BF16)
    W32 = sb("W32", [128, 4, 256], F32)
    Rbf = sb("Rbf", [128, 4, 256], BF16)
    bnst = sb("bnst", [128, 4, 6], F32)
    mv = sb("mv", [128, 4, 2], F32)
    grs = sb("grs", [128, 4], F32)
    gnb = sb("gnb", [128, 4], F32)
    gt_mix = sb("gt_mix", [128, D], F32)
    gt_ffn = sb("gt_ffn", [128, D], F32)
    gt_fin = sb("gt_fin", [128, D], F32)
    yst = [sb("yst%d" % i, [128, D], F32) for i in range(4)]
    pp = sb("pp", [128, 108], F32)
    mask4 = sb("mask4", [128, 4, 128], F32)
    cst = sb("cst", [128, 14, 128], BF16)
    wpool = sb("wpool", [128, 4, 128], BF16)
    slots = [sb("wslot%d" % i, [128, 4096], BF16) for i in range(NS)]
    psf = [nc.alloc_psum_tensor("ps%d" % i, [128, 512], F32) for i in range(8)]
    psb = [p.bitcast(BF16) for p in psf]

    ident = cst[:, 12, :]
    ones = cst[:, 13, :]
    PS_OFF, GR_OFF, BR_OFF, CW0, CW1, CW2, CB = 0, 4, 12, 20, 42, 64, 86

    def R(n):
        return Res(n)

    r_x = [R("x%d" % i) for i in range(NT)]
    r_xn = [R("xn%d" % i) for i in range(NT)]
    r_hT = R("hT")
    r_hb = [R("hb0"), R("hb1")]
    r_junk = R("junk")
    r_ss = R("ss")
    r_rstd = R("rstd")
    r_ss2 = [R("ss2_%d" % i) for i in range(NT)]
    r_rstd2 = [R("rstd2_%d" % i) for i in range(NT)]
    r_ssf = R("ssf")
    r_rstdf = R("rstdf")
    r_v = R("v")
    r_ktok = R("ktok")
    r_kT = R("kT")
    r_qT = R("qT")
    r_on = R("on")
    r_gTk = [R("gT%d" % k) for k in range(3)]
    r_acc = [R("acc0"), R("acc1")]
    r_g1 = [R("g10"), R("g11")]
    r_merged = [R("mg%d" % j) for j in range(8)]
    r_pooledT = R("pooledT")
    r_ypT = R("ypT")
    r_mbf = R("mbf")
    r_hp = [R("hp%d" % i) for i in range(NT + 1)]
    r_qxT = R("qxT")
    r_oT = R("oT")
    r_pT = [R("pT%d" % i) for i in range(4)]
    r_rden = R("rden")
    r_th = [R("th0"), R("th1")]
    r_tt = [R("tt0"), R("tt1")]
    r_rot = [R("rot0")]
    r_krot = [R("krot0"), R("krot1")]
    r_ssb = [R("ssb0"), R("ssb1"), R("ssb2")]
    r_ropes = [R("ropes0"), R("ropes1")]
    r_carry = R("carry")
    r_bnd = R("bnd")
    r_btmp = R("btmp")
    r_memT = R("memT")
    r_kmT = R("kmT")
    r_vm = R("vm")
    r_W32 = R("W32")
    r_Rbf = R("Rbf")
    r_bn = R("bn")
    r_mv = R("mv")
    r_grs = R("grs")
    r_gnb = R("gnb")
    r_gt = R("gtiles")
    r_yst = [R("yst%d" % i) for i in range(4)]
    r_const = R("const")
    r_const2 = R("const2")
    r_ps = [R("ps%d" % i) for i in range(8)]
    r_slot = [R("slot%d" % i) for i in range(NS)]

    d_x = [DmaSem(nc, "d_x%d" % i) for i in range(NT)]
    d_yst = [DmaSem(nc, "d_y%d" % i) for i in range(4)]
    d_ropes = [DmaSem(nc, "d_r%d" % i) for i in range(2)]
    d_const = DmaSem(nc, "d_c")
    d_const2 = DmaSem(nc, "d_c2")
    d_slot = [DmaSem(nc, "d_s%d" % i) for i in range(NS)]
    d_dbg = DmaSem(nc, "d_dbg")

    bank_ctr = [0]

    def bank():
        b = bank_ctr[0]
        bank_ctr[0] = (b + 1) % 8
        return b

    NUW = 37
    wsc_d = nc.dram_tensor("wsc", [NUW, 128, 4096], BF16, kind="Internal").ap()
    r_wsc = [Res("wsc%d" % u) for u in range(NUW)]
    d_wst = [DmaSem(nc, "d_w%d" % i) for i in range(NS)]

    class WStream:
        def __init__(self):
            self.units = []
            self.issued = 0
            self.slot_of = {}
            self.free = list(range(NS))

        def add(self, name, pieces):
            uid, grp = self.cur
            self.units.append((name, pieces, uid, grp))

        def pump(self):
            while self.issued < len(self.units) and self.free:
                j = self.issued
                s = self.free.pop(0)
                self.slot_of[j] = s
                name, pieces, uid, grp = self.units[j]
                if uid is None or grp == 0:
                    prev = []
                    for (dstf, src) in pieces:
                        o = S.op("pool", lambda e, dstf=dstf, src=src, s=s: e.dma_start(out=dstf(slots[s]), in_=src),
                                 writes=[r_slot[s]], dma=d_slot[s], nodep=tuple(prev), name="wload")
                        prev.append(o)
                    if uid is not None:
                        S.op("sp", lambda e, s=s, uid=uid: e.dma_start(out=wsc_d[uid, :, :], in_=slots[s][:, :]),
                             reads=[r_slot[s]], writes=[r_wsc[uid]], dma=d_wst[s], name="wstore")
                else:
                    S.op("pool", lambda e, s=s, uid=uid: e.dma_start(out=slots[s][:, :], in_=wsc_d[uid, :, :]),
                         reads=[r_wsc[uid]], writes=[r_slot[s]], dma=d_slot[s], name="wload2")
                self.issued += 1

        def take(self, j, name):
            assert self.units[j][0] == name, (self.units[j][0], name)
            self.pump()
            assert self.issued > j, ("weight unit not issued", j, name)
            s = self.slot_of[j]
            return slots[s], r_slot[s]

        def release(self, j):
            self.free.append(self.slot_of[j])
            self.pump()

    W = WStream()
    w_in_v = w_in_d.rearrange("(k p) n -> p k n", p=128)
    w_r_v = w_r_d.rearrange("(k p) n -> p k n", p=128)
    w_kv_v = w_kv_d.rearrange("(k p) n -> p k n", p=128)
    w_out_v = w_out_d.rearrange("(k p) n -> p k n", p=128)
    w_up_v = w_up_d.rearrange("(k p) n -> p k n", p=128)
    w_a_v = w_a_d.rearrange("(k p) n -> p k n", p=128)
    w_c_v = w_c_d.rearrange("(k p) n -> p k n", p=128)
    w_down_v = w_down_d.rearrange("(f p) n -> p f n", p=128)

    def k8(slot):
        return slot[:, 0:4096].rearrange("p (k n) -> p k n", k=8)

    def k4(slot):
        return slot[:, 0:4096].rearrange("p (k n) -> p k n", k=4)

    def unit_k8(src_v, c0):
        return [(lambda sl: k8(sl), src_v[:, :, c0:c0 + 512])]

    IN_COL = {"hp": 0, "q": 512, "k": 1024, "v0": 1536, "v1": 2048, "gr0": 2560, "gr1": 3072, "qx": 3584,
              "gp0": 4096, "gp1": 4608, "gret0": 5120, "gret1": 5632, "gm0": 6144, "gm1": 6656}
    order = []
    for g in range(NG):
        gl = []
        if g % GPS == 0:
            order += [("kvk", None, g), ("kvv", None, g)]
        gl += ["v0", "k", "v1", "q", "gr0", "gr1", "hp", "qx", "a", "gp0", "gp1", "r0", "gret0", "r1", "gret1",
                  "c", "gm0", "gm1", "out0", "out1"]
        gl += ["up%d" % u for u in range(11)]
        gl += ["dn%d_%d" % (c2, k) for c2 in range(2) for k in range(3)]
        assert len(gl) == NUW
        order += [(nm, u, g) for u, nm in enumerate(gl)]
    for (nm, uid, grp) in order:
        W.cur = (uid, grp)
        if nm in IN_COL:
            W.add(nm, unit_k8(w_in_v, IN_COL[nm]))
        elif nm == "kvk":
            W.add(nm, unit_k8(w_kv_v, 0))
        elif nm == "kvv":
            W.add(nm, unit_k8(w_kv_v, 512))
        elif nm in ("r0", "r1"):
            W.add(nm, unit_k8(w_r_v, 512 * int(nm[1])))
        elif nm in ("out0", "out1"):
            W.add(nm, unit_k8(w_out_v, 512 * int(nm[3])))
        elif nm == "a":
            W.add(nm, [(lambda sl: k4(sl), w_a_v)])
        elif nm == "c":
            W.add(nm, [(lambda sl: k4(sl), w_c_v)])
        elif nm.startswith("up"):
            u = int(nm[2:])
            W.add(nm, [(lambda sl: k8(sl)[:, :, 0:256], w_up_v[:, :, u * 256:(u + 1) * 256]),
                       (lambda sl: k8(sl)[:, :, 256:512], w_up_v[:, :, FH + u * 256:FH + (u + 1) * 256])])
        elif nm.startswith("dn"):
            c2 = int(nm[2])
            k = int(nm[4])
            nf = min(8, NF - 8 * k)
            W.add(nm, [(lambda sl, nf=nf: k8(sl)[:, 0:nf, :],
                        w_down_v[:, 8 * k:8 * k + nf, c2 * 512:(c2 + 1) * 512])])
        else:
            raise ValueError(nm)
    wpos = [0]

    def wtake(name):
        j = wpos[0]
        wpos[0] += 1
        sl, rs = W.take(j, name)
        return j, sl, rs

    def mm_group(e, out_ap, pairs):
        n = len(pairs)
        last = None
        for i, (l, r) in enumerate(pairs):
            last = e.matmul(out_ap, lhsT=l, rhs=r, start=(i == 0), stop=(i == n - 1))
        return last

    def dump(name, ap, res):
        if dbg and name in dbg_d and name not in dumped:
            dumped.add(name)
            S.op("pool", lambda e: e.dma_start(out=dbg_d[name], in_=ap), reads=res, dma=d_dbg, name="dbg")
    dumped = set()

    S.op("sp", lambda e: e.dma_start(out=pp[:], in_=pp_d), writes=[r_const], dma=d_const)
    S.op("sp", lambda e: e.dma_start(out=mask4[:].rearrange("p a b -> p (a b)"), in_=mask_d), writes=[r_const],
         dma=d_const, nodep=tuple(d_const.ops))
    S.op("sp", lambda e: e.dma_start(out=gt_mix[:], in_=gvec_d[0, :].partition_broadcast(128)), writes=[r_gt],
         dma=d_const)
    S.op("sp", lambda e: e.dma_start(out=gt_ffn[:], in_=gvec_d[1, :].partition_broadcast(128)), writes=[r_gt],
         dma=d_const, nodep=tuple(d_const.ops))
    S.op("sp", lambda e: e.dma_start(out=gt_fin[:], in_=gvec_d[2, :].partition_broadcast(128)), writes=[r_gt],
         dma=d_const, nodep=tuple(d_const.ops))
    S.op("pool", lambda e: e.dma_start(out=cst[:], in_=cst_d), writes=[r_const2], dma=d_const2)
    S.op("pool", lambda e: e.dma_start(out=wpool[:], in_=w_pool_d.rearrange("g c d -> c g d")), writes=[r_const2],
         dma=d_const2, nodep=tuple(d_const2.ops))
    for o in d_const.ops:
        o.val = d_const.count
    for o in d_const2.ops:
        o.val = d_const2.count

    def norm_stats(src_fn, r_src, ntiles):
        for i in range(ntiles):
            S.op("act", lambda e, i=i: e.activation(out=junk[:], in_=src_fn(i), func=AF.Square,
                                                    accum_out=ss[:, i:i + 1]),
                 reads=[r_src[i]], writes=[r_junk, r_ss], name="sq")
        S.op("dve", lambda e: e.tensor_scalar(out=rstd[:, 0:ntiles], in0=ss[:, 0:ntiles], scalar1=1.0 / D,
                                              scalar2=EPS, op0=ALU.mult, op1=ALU.add),
             reads=[r_ss], writes=[r_rstd])
        S.op("act", lambda e: e.activation(out=rstd[:, 0:ntiles], in_=rstd[:, 0:ntiles], func=AF.Sqrt),
             reads=[r_rstd], writes=[r_rstd])
        S.op("dve", lambda e: e.reciprocal(out=rstd[:, 0:ntiles], in_=rstd[:, 0:ntiles]),
             reads=[r_rstd], writes=[r_rstd])

    def norm_tile(i, src_fn, r_src, gtile, dstT, dst_res):
        hbi = i % 2
        S.op("dve", lambda e, i=i, hbi=hbi: e.scalar_tensor_tensor(
            out=hb[hbi][:], in0=src_fn(i), scalar=rstd[:, i:i + 1], in1=gtile[:], op0=ALU.mult, op1=ALU.mult),
            reads=[r_src[i], r_rstd, r_gt], writes=[r_hb[hbi]])
        b = bank()

        def tr(e, hbi=hbi, b=b):
            last = None
            for k in range(8):
                last = e.transpose(out=psb[b][:, k * 128:(k + 1) * 128], in_=hb[hbi][:, k * 128:(k + 1) * 128],
                                   identity=ident)
            return last
        S.op("pe", tr, reads=[r_hb[hbi], r_const, r_const2], writes=[r_ps[b]])
        S.op("act", lambda e, i=i, b=b: e.activation(
            out=dstT[:, :, i * 128:(i + 1) * 128],
            in_=psb[b][:, 0:1024].rearrange("p (k t) -> p k t", k=8), func=AF.Copy),
            reads=[r_ps[b]], writes=[dst_res])

    def norm_to_T(src_fn, r_src, gtile, dstT, dst_res, ntiles, tok_stride):
        norm_stats(src_fn, r_src, ntiles)
        for i in range(ntiles):
            norm_tile(i, src_fn, r_src, gtile, dstT, dst_res)

    def load_x(g):
        for i in range(NT):
            S.op("sp", lambda e, i=i, g=g: e.dma_start(out=xn_v[:, i, :],
                                                      in_=x_d[g * T + i * 128: g * T + (i + 1) * 128, :]),
                 writes=[r_xn[i], r_merged[2 * i], r_merged[2 * i + 1]], dma=d_x[i], name="xload")

    def copy_x():
        for i in range(NT):
            S.op("act", lambda e, i=i: e.activation(out=xres[:, i, :], in_=xn_v[:, i, :], func=AF.Copy),
                 reads=[r_xn[i]], writes=[r_x[i]])

    out_ops = []

    def do_group(g):
        seq = g // GPS
        gi = g % GPS
        tok0 = g * T
        first = (gi == 0)

        if first:
            S.op("sp", lambda e: e.dma_start(out=yst[1][:], in_=gvec_d[3, :].partition_broadcast(128)),
                 writes=[r_yst[1]], dma=d_yst[1])
            r_memtile = [r_yst[0], r_yst[0]]
            r_gt_save = r_gt
            for mi in range(2):
                S.op("sp", lambda e, mi=mi: e.dma_start(
                    out=yst[0][:], in_=mem_d[seq * MEM + mi * 128: seq * MEM + (mi + 1) * 128, :]),
                    writes=[r_yst[0]], dma=d_yst[0])
                S.op("act", lambda e: e.activation(out=junk[:], in_=yst[0][:], func=AF.Square, accum_out=ss[:, 0:1]),
                     reads=[r_yst[0]], writes=[r_junk, r_ss])
                S.op("dve", lambda e: e.tensor_scalar(out=rstd[:, 0:1], in0=ss[:, 0:1], scalar1=1.0 / D, scalar2=EPS,
                                                      op0=ALU.mult, op1=ALU.add), reads=[r_ss], writes=[r_rstd])
                S.op("act", lambda e: e.activation(out=rstd[:, 0:1], in_=rstd[:, 0:1], func=AF.Sqrt),
                     reads=[r_rstd], writes=[r_rstd])
                S.op("dve", lambda e: e.reciprocal(out=rstd[:, 0:1], in_=rstd[:, 0:1]), reads=[r_rstd],
                     writes=[r_rstd])
                S.op("dve", lambda e: e.scalar_tensor_tensor(out=hb[0][:], in0=yst[0][:], scalar=rstd[:, 0:1],
                                                             in1=yst[1][:], op0=ALU.mult, op1=ALU.mult),
                     reads=[r_yst[0], r_yst[1], r_rstd], writes=[r_hb[0]])
                b = bank()

                def trm(e, b=b):
                    last = None
                    for k in range(8):
                        last = e.transpose(out=psb[b][:, k * 128:(k + 1) * 128], in_=hb[0][:, k * 128:(k + 1) * 128],
                                           identity=ident)
                    return last
                S.op("pe", trm, reads=[r_hb[0], r_const, r_const2], writes=[r_ps[b]])
                S.op("act", lambda e, mi=mi, b=b: e.activation(
                    out=memT[:, :, mi * 128:(mi + 1) * 128],
                    in_=psb[b][:, 0:1024].rearrange("p (k t) -> p k t", k=8), func=AF.Copy),
                    reads=[r_ps[b]], writes=[r_memT])
            j, sl, rs = wtake("kvk")
            for h in range(4):
                b = bank()
                S.op("pe", lambda e, h=h, b=b, sl=sl: mm_group(
                    e, psf[b][:, 0:MEM], [(k8(sl)[:, kc, h * 128:(h + 1) * 128], memT[:, kc, :]) for kc in range(8)]),
                    reads=[rs, r_memT], writes=[r_ps[b]])
                S.op("act", lambda e, h=h, b=b: e.activation(out=kmT[:, h, :], in_=psf[b][:, 0:MEM], func=AF.Copy),
                     reads=[r_ps[b]], writes=[r_kmT])
            W.release(j)
            j, sl, rs = wtake("kvv")
            for mc in range(2):
                b = bank()
                S.op("pe", lambda e, mc=mc, b=b, sl=sl: mm_group(
                    e, psf[b][:], [(memT[:, kc, mc * 128:(mc + 1) * 128], k8(sl)[:, kc, :]) for kc in range(8)]),
                    reads=[rs, r_memT], writes=[r_ps[b]])
                S.op("act", lambda e, mc=mc, b=b: e.activation(out=vm[:, mc, :], in_=psf[b][:], func=AF.Copy),
                     reads=[r_ps[b]], writes=[r_vm])
            W.release(j)
            S.op("dve", lambda e: e.memset(W32[:], 0.0), writes=[r_W32])
            S.op("dve", lambda e: e.memset(Rbf[:], 0.0), writes=[r_Rbf])
            S.op("dve", lambda e: e.memset(carry[:], 0.0), writes=[r_carry])

        dump("hT", hT[:].rearrange("p a b -> p (a b)"), [r_hT])

        fence1 = r_gTk + [r_acc[0], r_acc[1], r_g1[0], r_g1[1]]
        for vu, which in ((0, "k"), (1, "q")):
            jv, slv, rsv = wtake("v%d" % vu)
            j, sl, rs = wtake(which)
            pend = None
            for i in range(NT):
                c = gi * NT + i
                rb = i % 2
                tcos = 0 if which == "q" else 2
                S.op("sp", lambda e, c=c, rb=rb, tcos=tcos: e.dma_start(
                    out=ropes[rb][:, 0:2, :, :].rearrange("p a b c -> p (a b c)"),
                    in_=ropet_d[c, :, tcos * 256:(tcos + 2) * 256]),
                    writes=[r_ropes[rb]], dma=d_ropes[rb])
                b = bank()
                S.op("pe", lambda e, i=i, b=b, sl=sl: mm_group(
                    e, psf[b][:], [(hT[:, kc, i * 128:(i + 1) * 128], k8(sl)[:, kc, :]) for kc in range(8)]),
                    reads=[rs, r_hT], writes=[r_ps[b]])
                bv = bank()
                S.op("pe", lambda e, i=i, bv=bv, slv=slv: mm_group(
                    e, psf[bv][:], [(hT[:, kc, i * 128:(i + 1) * 128], k8(slv)[:, kc, :]) for kc in range(8)]),
                    reads=[rsv, r_hT], writes=[r_ps[bv]])
                S.op("act", lambda e, i=i, bv=bv, vu=vu: e.activation(out=v_v[:, i, vu * 512:(vu + 1) * 512],
                                                                       in_=psf[bv][:], func=AF.Copy),
                     reads=[r_ps[bv]], writes=[r_v] + (fence1 if (vu == 0 and i == 0) else []))
                pv = psf[b][:, :].rearrange("p (h d) -> p h d", h=4)
                x1 = pv[:, :, 0:64]
                x2 = pv[:, :, 64:128]
                kb = i % 2
                rt = rot[0]

                def rotary(e, x1=x1, x2=x2, rb=rb, rt=rt):
                    e.tensor_tensor(out=rt[:, 0, :, :], in0=x1, in1=ropes[rb][:, 0, :, :], op=ALU.mult)
                    e.tensor_tensor(out=rt[:, 1, :, :], in0=x2, in1=ropes[rb][:, 1, :, :], op=ALU.mult)
                    e.tensor_tensor(out=rt[:, 2, :, :], in0=x1, in1=ropes[rb][:, 1, :, :], op=ALU.mult)
                    return e.tensor_tensor(out=rt[:, 3, :, :], in0=x2, in1=ropes[rb][:, 0, :, :], op=ALU.mult)
                S.op("dve", rotary, reads=[r_ps[b], r_ropes[rb]], writes=[r_rot[0]])
                if which == "k":
                    dst = ktok_v[:, i, :].rearrange("p (h d) -> p h d", h=4)
                    dres = r_ktok
                else:
                    dst = krot[kb][:]
                    dres = r_krot[kb]

                def rotary2(e, dst=dst, rt=rt):
                    e.tensor_tensor(out=dst[:, :, 0:64], in0=rt[:, 0, :, :], in1=rt[:, 1, :, :], op=ALU.subtract)
                    return e.tensor_tensor(out=dst[:, :, 64:128], in0=rt[:, 2, :, :], in1=rt[:, 3, :, :], op=ALU.add)
                S.op("dve", rotary2, reads=[r_rot[0]], writes=[dres])
                src = ktok_v[:, i, :] if which == "k" else krot[kb][:].rearrange("p h d -> p (h d)")
                dT = kT_v if which == "k" else qT_v
                dTres = r_kT if which == "k" else r_qT

                def emit_tr(i=i, src=src, dres=dres, dT=dT, dTres=dTres):
                    b2 = bank()

                    def trq(e, b2=b2, src=src):
                        last = None
                        for h in range(4):
                            last = e.transpose(out=psb[b2][:, h * 128:(h + 1) * 128],
                                               in_=src[:, h * 128:(h + 1) * 128], identity=ident)
                        return last
                    S.op("pe", trq, reads=[dres, r_const, r_const2], writes=[r_ps[b2]])
                    S.op("act", lambda e, i=i, b2=b2, dT=dT: e.activation(
                        out=dT[:, :, i * 128:(i + 1) * 128],
                        in_=psb[b2][:, 0:512].rearrange("p (h t) -> p h t", h=4), func=AF.Copy),
                        reads=[r_ps[b2]], writes=[dTres])
                if pend is not None:
                    pend()
                pend = emit_tr
            pend()
            W.release(jv)
            W.release(j)
        dump("qT", qT_v.rearrange("p a b -> p (a b)"), [r_qT])
        dump("kT", kT_v.rearrange("p a b -> p (a b)"), [r_kT])
        dump("v", v_v.rearrange("p a b -> p (a b)"), [r_v])

        GC = [float(np.exp(float(T) * np.log1p(-2.0 ** (-5.0 - h)))) for h in range(4)]
        sbufs = [(ssb[0], r_ssb[0]), (ssb[1], r_ssb[1]), (ssb[2], r_ssb[2])] + [(pT[q_][:, :].rearrange("p (h t) -> p h t", h=4), r_pT[q_])
                                                           for q_ in range(4)]
        sb_free = list(range(7))
        S_blk = {}
        o_banks = {}

        def rec_scores(i):
            for jt in range(i + 1):
                bs = bank()

                def scores(e, i=i, jt=jt, bs=bs):
                    last = None
                    for h in range(4):
                        last = e.matmul(psf[bs][:, h * 128:(h + 1) * 128], lhsT=kT_v[:, h, jt * 128:(jt + 1) * 128],
                                        rhs=qT_v[:, h, i * 128:(i + 1) * 128], start=True, stop=True)
                    return last
                S.op("pe", scores, reads=[r_kT, r_qT], writes=[r_ps[bs]])
                bi = sb_free.pop(0)
                buf, rbuf = sbufs[bi]
                bufap = buf[:] if bi < 3 else buf
                if jt == i:
                    S.op("dve", lambda e, bs=bs, bufap=bufap: e.tensor_tensor(
                        out=bufap, in0=psf[bs][:, :].rearrange("p (h t) -> p h t", h=4), in1=mask4[:], op=ALU.mult),
                        reads=[r_ps[bs], r_const, r_const2], writes=[rbuf])
                else:
                    S.op("act", lambda e, bs=bs, bufap=bufap: e.activation(
                        out=bufap, in_=psf[bs][:, :].rearrange("p (h t) -> p h t", h=4), func=AF.Copy),
                        reads=[r_ps[bs]], writes=[rbuf])
                S_blk[(jt, i)] = (bi, bufap, rbuf)

        def rec_o(i):
            bo = [bank(), bank()]
            blks = [S_blk[(jt, i)] for jt in range(i + 1)]

            def omm(e, i=i, bo=bo, blks=blks):
                last = None
                for h in range(4):
                    o_ap = psf[bo[h // 2]][:, (h % 2) * 256:(h % 2 + 1) * 256]
                    for jt, (bi, bufap, rbuf) in enumerate(blks):
                        e.matmul(o_ap, lhsT=bufap[:, h, :], rhs=v_v[:, jt, h * 256:(h + 1) * 256], start=(jt == 0),
                                 stop=False)
                    last = e.matmul(o_ap, lhsT=qT_v[:, h, i * 128:(i + 1) * 128], rhs=Rbf[:, h, :], start=False,
                                    stop=True)
                return last
            S.op("pe", omm, reads=[rb for (_, _, rb) in blks] + [r_v, r_qT, r_Rbf],
                 writes=[r_ps[bo[0]], r_ps[bo[1]]])
            for (bi, _, _) in blks:
                sb_free.append(bi)
            o_banks[i] = bo

        def rec_gn(i):
            bo = o_banks[i]

            def bst(e, bo=bo):
                last = None
                for h in range(4):
                    last = e.bn_stats(out=bnst[:, h, :], in_=psf[bo[h // 2]][:, (h % 2) * 256:(h % 2 + 1) * 256])
                return last
            S.op("dve", bst, reads=[r_ps[bo[0]], r_ps[bo[1]]], writes=[r_bn])

            def bag(e):
                last = None
                for h in range(4):
                    last = e.bn_aggr(out=mv[:, h, :], in_=bnst[:, h, :])
                return last
            S.op("dve", bag, reads=[r_bn], writes=[r_mv])
            S.op("dve", lambda e: e.tensor_scalar(out=grs[:], in0=mv[:, :, 1], scalar1=EPS, scalar2=None,
                                                  op0=ALU.add), reads=[r_mv], writes=[r_grs])
            S.op("act", lambda e: e.activation(out=grs[:], in_=grs[:], func=AF.Sqrt), reads=[r_grs], writes=[r_grs])
            S.op("dve", lambda e: e.reciprocal(out=grs[:], in_=grs[:]), reads=[r_grs], writes=[r_grs])
            S.op("dve", lambda e: e.scalar_tensor_tensor(out=gnb[:], in0=mv[:, :, 0], scalar=-1.0, in1=grs[:],
                                                         op0=ALU.mult, op1=ALU.mult),
                 reads=[r_mv, r_grs], writes=[r_gnb])

            def onorm(e, i=i, bo=bo):
                last = None
                for h in range(4):
                    last = e.activation(out=on_v[:, i, h * 256:(h + 1) * 256],
                                        in_=psf[bo[h // 2]][:, (h % 2) * 256:(h % 2 + 1) * 256],
                                        func=AF.Identity, scale=grs[:, h:h + 1], bias=gnb[:, h:h + 1])
                return last
            S.op("act", onorm, reads=[r_ps[bo[0]], r_ps[bo[1]], r_grs, r_gnb], writes=[r_on])

        rec_scores(0)
        rec_scores(1)
        rec_scores(2)
        rec_o(0)
        rec_o(1)
        rec_scores(3)
        rec_gn(0)
        rec_o(2)
        rec_gn(1)
        rec_o(3)
        rec_gn(2)
        rec_gn(3)

        bd = [bank(), bank()]

        def dmm(e, bd=bd):
            last = None
            for h in range(4):
                d_ap = psf[bd[h // 2]][:, (h % 2) * 256:(h % 2 + 1) * 256]
                for jt in range(NT):
                    last = e.matmul(d_ap, lhsT=ktok_v[:, jt, h * 128:(h + 1) * 128],
                                    rhs=v_v[:, jt, h * 256:(h + 1) * 256], start=(jt == 0), stop=(jt == NT - 1))
            return last
        S.op("pe", dmm, reads=[r_ktok, r_v], writes=[r_ps[bd[0]], r_ps[bd[1]]])

        def wupd(e, bd=bd):
            last = None
            for h in range(4):
                d_ap = psf[bd[h // 2]][:, (h % 2) * 256:(h % 2 + 1) * 256]
                last = e.scalar_tensor_tensor(out=W32[:, h, :], in0=W32[:, h, :], scalar=GC[h], in1=d_ap,
                                              op0=ALU.mult, op1=ALU.add)
            return last
        S.op("dve", wupd, reads=[r_ps[bd[0]], r_ps[bd[1]]], writes=[r_W32])

        def rupd(e):
            last = None
            for h in range(4):
                last = e.activation(out=Rbf[:, h, :], in_=W32[:, h, :], func=AF.Copy, scale=GC[h])
            return last
        S.op("act", rupd, reads=[r_W32], writes=[r_Rbf])
        dump("on", on_v.rearrange("p a b -> p (a b)"), [r_on])

        for i in range(NT):
            b = bank()

            def tro(e, i=i, b=b):
                last = None
                for fc in range(8):
                    last = e.transpose(out=psb[b][:, fc * 128:(fc + 1) * 128], in_=on_v[:, i, fc * 128:(fc + 1) * 128],
                                       identity=ident)
                return last
            S.op("pe", tro, reads=[r_on, r_const, r_const2], writes=[r_ps[b]])

            def aff(e, i=i, b=b):
                last = None
                for fc in range(8):
                    last = e.activation(out=retT_v[:, fc, i * 128:(i + 1) * 128], in_=psb[b][:, fc * 128:(fc + 1) * 128],
                                        func=AF.Identity, scale=pp[:, GR_OFF + fc:GR_OFF + fc + 1],
                                        bias=pp[:, BR_OFF + fc:BR_OFF + fc + 1])
                return last
            S.op("act", aff, reads=[r_ps[b], r_const, r_const2], writes=[r_v])

        cnt = 0
        for u in range(2):
            j, sl, rs = wtake("gr%d" % u)
            for fl in range(4):
                fc = u * 4 + fl
                b = bank()
                ti = cnt % 2
                cnt += 1
                S.op("pe", lambda e, fl=fl, b=b, sl=sl: mm_group(
                    e, psf[b][:], [(k8(sl)[:, kc, fl * 128:(fl + 1) * 128], hT[:, kc, :]) for kc in range(8)]),
                    reads=[rs, r_hT], writes=[r_ps[b]])
                S.op("act", lambda e, b=b, ti=ti: e.activation(out=th[ti][:], in_=psf[b][:], func=AF.Tanh, scale=0.5),
                     reads=[r_ps[b]], writes=[r_th[ti]])
                S.op("dve", lambda e, b=b, ti=ti: e.scalar_tensor_tensor(
                    out=tt[ti][:], in0=th[ti][:], scalar=1.0, in1=psf[b][:], op0=ALU.add, op1=ALU.mult),
                    reads=[r_th[ti], r_ps[b]], writes=[r_tt[ti]])
                S.op("dve", lambda e, fc=fc, ti=ti: e.scalar_tensor_tensor(
                    out=retT_v[:, fc, :], in0=tt[ti][:], scalar=0.5, in1=retT_v[:, fc, :], op0=ALU.mult, op1=ALU.mult),
                    reads=[r_tt[ti], r_v], writes=[r_v])
            W.release(j)
        dump("retT", retT_v.rearrange("p a b -> p (a b)"), [r_v])

        def branch_merge(jl_range, j0, y_pairs_fn, y_reads, gate_sl, gate_rs, mode):
            nonlocal cnt
            for jl in jl_range:
                jd = j0 + jl
                by = bank()
                S.op("pe", lambda e, jl=jl, by=by: mm_group(e, psf[by][:], y_pairs_fn(jl)),
                     reads=y_reads, writes=[r_ps[by]])
                bg = bank()
                S.op("pe", lambda e, jl=jl, bg=bg: mm_group(
                    e, psf[bg][:], [(k8(gate_sl)[:, kc, jl * 128:(jl + 1) * 128], hT[:, kc, :]) for kc in range(8)]),
                    reads=[gate_rs, r_hT], writes=[r_ps[bg]])
                ti = cnt % 2
                cnt += 1
                S.op("act", lambda e, bg=bg, ti=ti: e.activation(out=th[ti][:], in_=psf[bg][:], func=AF.Tanh,
                                                                 scale=0.5),
                     reads=[r_ps[bg]], writes=[r_th[ti]])
                if mode == "set":
                    S.op("dve", lambda e, by=by, ti=ti, jd=jd: e.scalar_tensor_tensor(
                        out=mergedT[:, jd, :], in0=th[ti][:], scalar=1.0, in1=psf[by][:], op0=ALU.add, op1=ALU.mult),
                        reads=[r_th[ti], r_ps[by]], writes=[r_merged[jd], r_xn[jd // 2]])
                else:
                    S.op("dve", lambda e, by=by, ti=ti: e.scalar_tensor_tensor(
                        out=tt[ti][:], in0=th[ti][:], scalar=1.0, in1=psf[by][:], op0=ALU.add, op1=ALU.mult),
                        reads=[r_th[ti], r_ps[by]], writes=[r_tt[ti]])
                    if mode == "add":
                        S.op("dve", lambda e, ti=ti, jd=jd: e.tensor_tensor(
                            out=mergedT[:, jd, :], in0=mergedT[:, jd, :], in1=tt[ti][:], op=ALU.add),
                            reads=[r_tt[ti], r_merged[jd]], writes=[r_merged[jd]])
                    else:
                        S.op("dve", lambda e, ti=ti, jd=jd: e.tensor_tensor(
                            out=mbf_v[:, jd, :], in0=mergedT[:, jd, :], in1=tt[ti][:], op=ALU.add),
                            reads=[r_tt[ti], r_merged[jd]], writes=[r_mbf, r_pooledT, r_ypT])

        jh, slh, rsh = wtake("hp")
        jq, slq, rsq = wtake("qx")
        for i in range(NT):
            b = bank()
            S.op("pe", lambda e, i=i, b=b, slh=slh: mm_group(
                e, psf[b][:], [(hT[:, kc, i * 128:(i + 1) * 128], k8(slh)[:, kc, :]) for kc in range(8)]),
                reads=[rsh, r_hT], writes=[r_ps[b]])
            S.op("act", lambda e, i=i, b=b: e.activation(out=hp_tok[:, i + 1, :], in_=psf[b][:], func=AF.Copy),
                 reads=[r_ps[b]], writes=[r_hp[i + 1]])
            h = i
            b = bank()
            S.op("pe", lambda e, h=h, b=b, slq=slq: mm_group(
                e, psf[b][:], [(k8(slq)[:, kc, h * 128:(h + 1) * 128], hT[:, kc, :]) for kc in range(8)]),
                reads=[rsq, r_hT], writes=[r_ps[b]])
            S.op("act", lambda e, h=h, b=b: e.activation(out=qxT[:, h, :], in_=psf[b][:], func=AF.Copy),
                 reads=[r_ps[b]], writes=[r_qxT])
        W.release(jh)
        W.release(jq)
        pcnt = 0
        for step in range(4):
            if pending:
                pending[-1].mul(step)
                if step > 0:
                    pending[-1].copy(step - 1)
            gq = step
            b = bank()

            def poolmm(e, gq=gq, b=b):
                last = None
                for i in range(NT):
                    o_ap = psf[b][:, i * 128:(i + 1) * 128]
                    cur = hp_tok[:, i + 1, gq * 128:(gq + 1) * 128]
                    if first and i == 0:
                        last = e.matmul(o_ap, lhsT=cur, rhs=cst[:, 8 + gq, :], start=True, stop=True)
                    else:
                        e.matmul(o_ap, lhsT=cur, rhs=cst[:, gq, :], start=True, stop=False)
                        last = e.matmul(o_ap, lhsT=hp_tok[:, i, gq * 128:(gq + 1) * 128], rhs=cst[:, 4 + gq, :],
                                        start=False, stop=True)
                return last
            S.op("pe", poolmm, reads=r_hp + [r_const, r_const2], writes=[r_ps[b]])
            S.op("act", lambda e, gq=gq, b=b: e.activation(out=pooledT_v[:, gq, :], in_=psf[b][:], func=AF.Copy),
                 reads=[r_ps[b]], writes=[r_pooledT] + ([r_mbf] if gq == 0 else []))
            h = step
            pis = []
            for mc in range(2):
                bsx = bank()
                pi = pcnt % 4
                pcnt += 1
                pis.append(pi)
                S.op("pe", lambda e, h=h, mc=mc, bsx=bsx: e.matmul(
                    psf[bsx][:], lhsT=kmT[:, h, mc * 128:(mc + 1) * 128], rhs=qxT[:, h, :], start=True, stop=True),
                    reads=[r_kmT, r_qxT], writes=[r_ps[bsx]])
                S.op("act", lambda e, bsx=bsx, pi=pi: e.activation(out=pT[pi][:], in_=psf[bsx][:], func=AF.Exp,
                                                                   scale=float(128.0 ** -0.5)),
                     reads=[r_ps[bsx]], writes=[r_pT[pi]])
            b2 = bank()
            S.op("pe", lambda e, gq=gq, b2=b2: e.matmul(psf[b2][:], lhsT=wpool[:, gq, :], rhs=pooledT_v[:, gq, :],
                                                        start=True, stop=True),
                 reads=[r_pooledT, r_const, r_const2], writes=[r_ps[b2]])
            S.op("act", lambda e, gq=gq, b2=b2: e.activation(out=ypT_v[:, gq, :], in_=psf[b2][:], func=AF.Copy,
                                                             scale=pp[:, PS_OFF + gq:PS_OFF + gq + 1]),
                 reads=[r_ps[b2], r_const, r_const2], writes=[r_ypT])
            bo_ = bank()
            S.op("pe", lambda e, h=h, bo_=bo_, pis=tuple(pis): mm_group(
                e, psf[bo_][:], [(vm[:, mc, h * 128:(h + 1) * 128], pT[pis[mc]][:]) for mc in range(2)]),
                reads=[r_vm, r_pT[pis[0]], r_pT[pis[1]]], writes=[r_ps[bo_]])
            bden = bank()
            S.op("pe", lambda e, bden=bden, pis=tuple(pis): mm_group(
                e, psf[bden][:], [(ones, pT[pis[mc]][:]) for mc in range(2)]),
                reads=[r_const, r_const2, r_pT[pis[0]], r_pT[pis[1]]], writes=[r_ps[bden]])
            S.op("dve", lambda e, bden=bden: e.reciprocal(out=rden[:], in_=psf[bden][:]),
                 reads=[r_ps[bden]], writes=[r_rden])
            S.op("dve", lambda e, h=h, bo_=bo_: e.tensor_tensor(out=oT[:, h, :], in0=psf[bo_][:], in1=rden[:],
                                                               op=ALU.mult),
                 reads=[r_ps[bo_], r_rden], writes=[r_oT])
        if pending:
            pending[-1].copy(NT - 1)
            for i_ in range(NT):
                pending[-1].store(i_)
            pending.pop()
        S.op("act", lambda e: e.activation(out=hp_tok[:, 0, :], in_=hp_tok[:, NT, :], func=AF.Copy),
             reads=[r_hp[NT]], writes=[r_hp[0]])
        dump("pooledT", pooledT_v.rearrange("p a b -> p (a b)"), [r_pooledT])
        dump("oT", oT[:].rearrange("p a b -> p (a b)"), [r_oT])

        ja, sla, rsa = wtake("a")
        for u in range(2):
            jg, slg, rsg = wtake("gp%d" % u)
            branch_merge(range(4), u * 4,
                         lambda jl, u=u, sla=sla: [(k4(sla)[:, q4, (u * 4 + jl) * 128:(u * 4 + jl + 1) * 128],
                                                   ypT_v[:, q4, :]) for q4 in range(4)],
                         [rsa, r_ypT], slg, rsg, "set")
            W.release(jg)
        W.release(ja)
        for u in range(2):
            jr, slr, rsr = wtake("r%d" % u)
            jg, slg, rsg = wtake("gret%d" % u)
            branch_merge(range(4), u * 4,
                         lambda jl, slr=slr: [(k8(slr)[:, kc, jl * 128:(jl + 1) * 128], retT_v[:, kc, :])
                                              for kc in range(8)],
                         [rsr, r_v], slg, rsg, "add")
            W.release(jr)
            W.release(jg)
        jc, slc, rsc = wtake("c")
        for u in range(2):
            jg, slg, rsg = wtake("gm%d" % u)
            branch_merge(range(4), u * 4,
                         lambda jl, u=u, slc=slc: [(k4(slc)[:, q4, (u * 4 + jl) * 128:(u * 4 + jl + 1) * 128],
                                                   oT[:, q4, :]) for q4 in range(4)],
                         [rsc, r_oT], slg, rsg, "final")
            W.release(jg)
        W.release(jc)
        dump("mbf", mbf_v.rearrange("p a b -> p (a b)"), [r_mbf])

        jo0, slo0, rso0 = wtake("out0")
        jo1, slo1, rso1 = wtake("out1")

        def norm2_stats(i):
            S.op("act", lambda e, i=i: e.activation(out=junk[:], in_=xres[:, i, :], func=AF.Square,
                                                    accum_out=ss[:, i:i + 1]),
                 reads=[r_x[i]], writes=[r_junk, r_ss2[i]])
            S.op("dve", lambda e, i=i: e.tensor_scalar(out=rstd[:, i:i + 1], in0=ss[:, i:i + 1], scalar1=1.0 / D,
                                                       scalar2=EPS, op0=ALU.mult, op1=ALU.add),
                 reads=[r_ss2[i]], writes=[r_rstd2[i]])
            S.op("act", lambda e, i=i: e.activation(out=rstd[:, i:i + 1], in_=rstd[:, i:i + 1], func=AF.Sqrt),
                 reads=[r_rstd2[i]], writes=[r_rstd2[i]])
            S.op("dve", lambda e, i=i: e.reciprocal(out=rstd[:, i:i + 1], in_=rstd[:, i:i + 1]),
                 reads=[r_rstd2[i]], writes=[r_rstd2[i]])

        def norm2_tile(i):
            hbi = i % 2
            S.op("dve", lambda e, i=i, hbi=hbi: e.scalar_tensor_tensor(
                out=hb[hbi][:], in0=xres[:, i, :], scalar=rstd[:, i:i + 1], in1=gt_ffn[:], op0=ALU.mult,
                op1=ALU.mult),
                reads=[r_x[i], r_rstd2[i], r_gt], writes=[r_hb[hbi]])
            b = bank()

            def tr(e, hbi=hbi, b=b):
                last = None
                for k in range(8):
                    last = e.transpose(out=psb[b][:, k * 128:(k + 1) * 128], in_=hb[hbi][:, k * 128:(k + 1) * 128],
                                       identity=ident)
                return last
            S.op("pe", tr, reads=[r_hb[hbi], r_const, r_const2], writes=[r_ps[b]])
            S.op("act", lambda e, i=i, b=b: e.activation(
                out=hT[:, :, i * 128:(i + 1) * 128],
                in_=psb[b][:, 0:1024].rearrange("p (k t) -> p k t", k=8), func=AF.Copy),
                reads=[r_ps[b]], writes=[r_hT])

        S.op("dve", lambda e: e.memset(ss[:], 0.0), reads=[], writes=[r_ss, r_rstd] + r_ss2 + r_rstd2)
        for i in range(NT):
            for c2, slo, rso in ((0, slo0, rso0), (1, slo1, rso1)):
                b = bank()
                S.op("pe", lambda e, i=i, b=b, slo=slo: mm_group(
                    e, psf[b][:], [(mbf_v[:, kc, i * 128:(i + 1) * 128], k8(slo)[:, kc, :]) for kc in range(8)]),
                    reads=[rso, r_mbf], writes=[r_ps[b]])
                S.op("dve", lambda e, i=i, b=b, c2=c2: e.scalar_tensor_tensor(
                    out=xres[:, i, c2 * 512:(c2 + 1) * 512], in0=psf[b][:], scalar=0.5,
                    in1=xres[:, i, c2 * 512:(c2 + 1) * 512], op0=ALU.mult, op1=ALU.add),
                    reads=[r_ps[b], r_x[i]], writes=[r_x[i]])
            norm2_stats(i)
            if i > 0:
                norm2_tile(i - 1)
        norm2_tile(NT - 1)
        S.op("dve", lambda e: e.memset(junk[:, 0:2], 0.0), reads=r_ss2 + r_rstd2, writes=[r_ss, r_rstd, r_junk])
        W.release(jo0)
        W.release(jo1)
        dump("x2", xres[:].rearrange("p a b -> p (a b)"), r_x)
        if g + 1 < NG:
            load_x(g + 1)

        fence2 = [r_v, r_ktok, r_kT, r_qT, r_on]
        acnt = 0
        if not first:
            S.op("dve", lambda e: e.tensor_tensor(out=btmp[:, 0, :], in0=carry[:, :, 1], in1=pp[:, CW1:CW1 + NF],
                                                  op=ALU.mult), reads=[r_carry, r_const, r_const2], writes=[r_btmp])
            S.op("dve", lambda e: e.tensor_tensor(out=btmp[:, 1, :], in0=carry[:, :, 0], in1=pp[:, CW0:CW0 + NF],
                                                  op=ALU.mult), reads=[r_carry, r_const, r_const2], writes=[r_btmp])
            S.op("dve", lambda e: e.tensor_tensor(out=bnd[:, :, 1], in0=carry[:, :, 1], in1=pp[:, CW0:CW0 + NF],
                                                  op=ALU.mult), reads=[r_carry, r_const, r_const2], writes=[r_bnd])
            S.op("dve", lambda e: e.tensor_tensor(out=bnd[:, :, 0], in0=btmp[:, 0, :], in1=btmp[:, 1, :],
                                                  op=ALU.add), reads=[r_btmp, r_bnd], writes=[r_bnd])
        for u in range(11):
            j, sl, rs = wtake("up%d" % u)
            for fl in range(2):
                f = 2 * u + fl
                ai = acnt % 2
                acnt += 1
                ba = bank()
                S.op("pe", lambda e, fl=fl, ba=ba, sl=sl: mm_group(
                    e, psf[ba][:], [(k8(sl)[:, kc, fl * 128:(fl + 1) * 128], hT[:, kc, :]) for kc in range(8)]),
                    reads=[rs, r_hT], writes=[r_ps[ba]])
                bb = bank()
                S.op("pe", lambda e, fl=fl, bb=bb, sl=sl: mm_group(
                    e, psf[bb][:], [(k8(sl)[:, kc, 256 + fl * 128:256 + (fl + 1) * 128], hT[:, kc, :])
                                    for kc in range(8)]),
                    reads=[rs, r_hT], writes=[r_ps[bb]])
                S.op("act", lambda e, f=f, ba=ba, ai=ai: e.activation(
                    out=acc_v[ai], in_=psf[ba][:], func=AF.Identity, scale=pp[:, CW2 + f:CW2 + f + 1],
                    bias=pp[:, CB + f:CB + f + 1]),
                    reads=[r_ps[ba], r_const, r_const2], writes=[r_acc[ai]] + (fence2 + r_gTk if (u == 0 and fl == 0) else []))

                S.op("dve", lambda e, f=f, ba=ba, ai=ai: e.scalar_tensor_tensor(
                    out=acc_v[ai][:, 1:T], in0=psf[ba][:, 0:T - 1], scalar=pp[:, CW1 + f:CW1 + f + 1],
                    in1=acc_v[ai][:, 1:T], op0=ALU.mult, op1=ALU.add),
                    reads=[r_ps[ba], r_acc[ai], r_const, r_const2], writes=[r_acc[ai]])
                S.op("dve", lambda e, f=f, ba=ba, ai=ai: e.scalar_tensor_tensor(
                    out=acc_v[ai][:, 2:T], in0=psf[ba][:, 0:T - 2], scalar=pp[:, CW0 + f:CW0 + f + 1],
                    in1=acc_v[ai][:, 2:T], op0=ALU.mult, op1=ALU.add),
                    reads=[r_ps[ba], r_acc[ai], r_const, r_const2], writes=[r_acc[ai]])
                if not first:
                    S.op("dve", lambda e, f=f, ai=ai: e.tensor_tensor(
                        out=acc_v[ai][:, 0:2], in0=acc_v[ai][:, 0:2], in1=bnd[:, f, :], op=ALU.add),
                        reads=[r_acc[ai], r_bnd], writes=[r_acc[ai]])
                S.op("act", lambda e, f=f, ba=ba: e.activation(out=carry[:, f, :], in_=psf[ba][:, T - 2:T],
                                                               func=AF.Copy),
                     reads=[r_ps[ba]], writes=[r_carry])
                S.op("act", lambda e, ai=ai: e.activation(out=g1_v[ai], in_=acc_v[ai], func=AF.Gelu_apprx_tanh),
                     reads=[r_acc[ai]], writes=[r_g1[ai]])
                S.op("dve", lambda e, f=f, bb=bb, ai=ai: e.tensor_tensor(out=gT_v[:, f, :], in0=g1_v[ai],
                                                                         in1=psf[bb][:], op=ALU.mult),
                     reads=[r_g1[ai], r_ps[bb]], writes=[r_gTk[f // 8]])
            W.release(j)
        dump("gT", gT_v.rearrange("p a b -> p (a b)"), r_gTk)

        nxt = (g + 1 < NG)
        if nxt:
            norm_stats(lambda i: xn_v[:, i, :], r_xn, NT)

        for c2 in range(2):
            bks = [bank() for _ in range(NT)]
            for k in range(3):
                j, sl, rs = wtake("dn%d_%d" % (c2, k))
                nf = min(8, NF - 8 * k)

                def down(e, k=k, nf=nf, sl=sl, bks=bks):
                    last = None
                    for i in range(NT):
                        for fl in range(nf):
                            f = 8 * k + fl
                            last = e.matmul(psf[bks[i]][:], lhsT=gT_v[:, f, i * 128:(i + 1) * 128],
                                            rhs=k8(sl)[:, fl, :], start=(f == 0), stop=(f == NF - 1))
                    return last
                S.op("pe", down, reads=[rs, r_gTk[k]], writes=[r_ps[bk] for bk in bks])
                W.release(j)
                if nxt and c2 == 1 and k == 0:
                    for i in range(NT):
                        norm_tile(i, lambda i: xn_v[:, i, :], r_xn, gt_mix, hT, r_hT)
            for i in range(NT):
                S.op("dve", lambda e, i=i, c2=c2, bks=bks: e.tensor_tensor(
                    out=xres[:, i, c2 * 512:(c2 + 1) * 512], in0=psf[bks[i]][:],
                    in1=xres[:, i, c2 * 512:(c2 + 1) * 512], op=ALU.add),
                    reads=[r_ps[bks[i]], r_x[i]], writes=[r_x[i]])

        for i in range(NT):
            S.op("act", lambda e, i=i: e.activation(out=junk[:], in_=xres[:, i, :], func=AF.Square,
                                                    accum_out=ssf[:, i:i + 1]),
                 reads=[r_x[i]], writes=[r_junk, r_ssf])
        S.op("dve", lambda e: e.tensor_scalar(out=rstdf[:], in0=ssf[:], scalar1=1.0 / D, scalar2=EPS, op0=ALU.mult,
                                              op1=ALU.add), reads=[r_ssf], writes=[r_rstdf])
        S.op("act", lambda e: e.activation(out=rstdf[:], in_=rstdf[:], func=AF.Sqrt), reads=[r_rstdf],
             writes=[r_rstdf])
        S.op("dve", lambda e: e.reciprocal(out=rstdf[:], in_=rstdf[:]), reads=[r_rstdf], writes=[r_rstdf])

        def tail_mul(i):
            yi = i % 4
            S.op("dve", lambda e, i=i, yi=yi: e.scalar_tensor_tensor(
                out=yst[yi][:], in0=xres[:, i, :], scalar=rstdf[:, i:i + 1], in1=gt_fin[:], op0=ALU.mult,
                op1=ALU.mult),
                reads=[r_x[i], r_rstdf, r_gt], writes=[r_yst[yi]])

        def tail_copy(i, nxt=nxt):
            if nxt:
                S.op("act", lambda e, i=i: e.activation(out=xres[:, i, :], in_=xn_v[:, i, :], func=AF.Copy),
                     reads=[r_xn[i]], writes=[r_x[i]])

        def tail_store(i, tok0=tok0):
            yi = i % 4
            o = S.op("sp", lambda e, i=i, yi=yi, tok0=tok0: e.dma_start(
                out=y_d[tok0 + i * 128: tok0 + (i + 1) * 128, :], in_=yst[yi][:]),
                reads=[r_yst[yi]], dma=d_yst[yi], name="ystore")
            out_ops.append(o)

        def tail(tail_mul=tail_mul, tail_copy=tail_copy, tail_store=tail_store):
            for i in range(NT):
                tail_mul(i)
                tail_copy(i)
                tail_store(i)
        tail.mul = tail_mul
        tail.copy = tail_copy
        tail.store = tail_store
        pending.append(tail)

    pending = []
    load_x(0)
    norm_to_T(lambda i: xn_v[:, i, :], r_xn, gt_mix, hT, r_hT, NT, 128)
    copy_x()
    for g in range(NG):
        do_group(g)
    pending.pop()()

    fin = S.op("sp", lambda e: None, name="final_wait")
    fin.deps = list(out_ops[-4:]) + list(d_dbg.ops)
    for o in fin.deps:
        o.signal = True
    with nc.Block() as block:
        S.emit(block)
    return nc


def make_consts():
    f32 = np.float32
    half = 64
    inv = (np.float32(10000.0) ** (-np.arange(half, dtype=f32) / np.float32(half))).astype(f32)
    pos = np.arange(SEQ, dtype=f32)
    ang = (pos[:, None] * inv[None, :]).astype(f32).astype(np.float64)
    cos = np.cos(ang)
    sin = np.sin(ang)
    lg = np.log1p(-np.exp2(-5.0 - np.arange(4, dtype=np.float64)))
    p1 = ((np.arange(16)[:, None] % NT) * 128 + np.arange(1, 129)[None, :]).astype(np.float64)
    qd = np.exp(p1[:, :, None] * lg[None, None, :])
    kd = np.exp(-p1[:, :, None] * lg[None, None, :]) * (128.0 ** -0.5)
    ropet = np.zeros((16, 128, 4, 4, 64), np.float64)
    cosr = cos.reshape(16, 128, 1, 64)
    sinr = sin.reshape(16, 128, 1, 64)
    ropet[:, :, 0] = cosr * qd[:, :, :, None]
    ropet[:, :, 1] = sinr * qd[:, :, :, None]
    ropet[:, :, 2] = cosr * kd[:, :, :, None]
    ropet[:, :, 3] = sinr * kd[:, :, :, None]
    ropet = ropet.reshape(16, 128, 1024).astype(f32)
    kk = np.arange(128)[:, None]
    qq = np.arange(128)[None, :]
    m = (kk <= qq).astype(f32)
    mask4 = np.repeat(m[:, None, :], 4, axis=1).reshape(128, 512).astype(f32)
    cst = np.zeros((128, 14, 128), np.float64)
    tp = np.arange(128)[:, None]
    t = np.arange(128)[None, :]
    for gq, w in enumerate((2, 4, 8, 16)):
        cst[:, gq, :] = ((tp <= t) & (tp > t - w)) / w - (tp == t)
        cst[:, 4 + gq, :] = ((tp - 128) > (t - w)) / w
        cst[:, 8 + gq, :] = ((tp <= t) & (tp > t - w)) / np.minimum(t + 1, w) - (tp == t)
    cst[:, 12, :] = np.eye(128)
    cst[:, 13, :] = 1.0
    return ropet, mask4, cst.astype(f32)


_PROGRAM = {}


def kernel(x, mem, g_mix, w_in, w_pool, pool_scale, w_a, g_ret, b_ret, w_r, g_mem, w_mem_kv, w_c, w_out,
           g_ffn, w_up, conv_w, conv_b, w_down, g_final, _dbg=None):
    f32 = np.float32
    x = np.asarray(x, f32)
    mem = np.asarray(mem, f32)
    ropet, mask4, cst = make_consts()
    gvec = np.ascontiguousarray(np.stack([np.asarray(g_mix, f32)[0], np.asarray(g_ffn, f32)[0],
                                          np.asarray(g_final, f32), np.asarray(g_mem, f32)[0]]))
    cw = np.asarray(conv_w, f32)[0]
    pp = np.concatenate([
        np.asarray(pool_scale, f32)[0].reshape(4, 128).T,
        np.asarray(g_ret, f32)[0].reshape(8, 128).T,
        np.asarray(b_ret, f32)[0].reshape(8, 128).T,
        cw[0].reshape(NF, 128).T, cw[1].reshape(NF, 128).T, cw[2].reshape(NF, 128).T,
        np.asarray(conv_b, f32)[0].reshape(NF, 128).T], axis=1)
    pp = np.ascontiguousarray(pp, dtype=f32)
    shared = {
        "w_in": np.ascontiguousarray(np.asarray(w_in, f32)[0]),
        "w_pool": np.ascontiguousarray(np.asarray(w_pool, f32)[0]),
        "w_a": np.ascontiguousarray(np.asarray(w_a, f32)[0]),
        "w_r": np.ascontiguousarray(np.asarray(w_r, f32)[0]),
        "w_mem_kv": np.ascontiguousarray(np.asarray(w_mem_kv, f32)[0]),
        "w_c": np.ascontiguousarray(np.asarray(w_c, f32)[0]),
        "w_out": np.ascontiguousarray(np.asarray(w_out, f32)[0]),
        "w_up": np.ascontiguousarray(np.asarray(w_up, f32)[0]),
        "w_down": np.ascontiguousarray(np.asarray(w_down, f32)[0]),
        "gvec": gvec, "pp": pp, "ropet": ropet, "mask4": mask4, "cst": cst,
    }
    in_maps = []
    for c in range(NCORES):
        m = dict(shared)
        m["x"] = np.ascontiguousarray(x[2 * c:2 * c + 2].reshape(2 * SEQ, D))
        m["mem"] = np.ascontiguousarray(mem[2 * c:2 * c + 2].reshape(2 * MEM, D))
        in_maps.append(m)
    key = tuple(sorted(_dbg.items())) if _dbg else None
    if key not in _PROGRAM:
        _PROGRAM[key] = build_program(_dbg)
    nc = _PROGRAM[key]
    res = run_bass_kernel_spmd(nc, in_maps, core_ids=list(range(NCORES)))
    out = np.concatenate([np.asarray(r["y"], f32).reshape(2, SEQ, D) for r in res.results], axis=0)
    if _dbg:
        return out, res.results
    return out
```

```python
import numpy as np
import concourse.bass as bass
import concourse.mybir as mybir
from concourse.bass_utils import run_bass_kernel_spmd

F32 = mybir.dt.float32
BF16 = mybir.dt.bfloat16
AF = mybir.ActivationFunctionType
ALU = mybir.AluOpType

NCORES = 8
D = 1024
SEQ = 2048
T = 512
NT = T // 128
NG = 2 * SEQ // T
GPS = SEQ // T
MEM = 256
FH = 2816
NF = FH // 128
EPS = 1e-6
NS = 4


class Res:
    __slots__ = ("name", "writer", "readers")

    def __init__(self, name):
        self.name = name
        self.writer = None
        self.readers = []


class Op:
    __slots__ = ("eng", "fn", "deps", "signal", "sem", "val", "inc", "name")


class DmaSem:
    def __init__(self, nc, name):
        self.sem = nc.alloc_semaphore(name)
        self.count = 0
        self.ops = []


class Sched:
    ENGS = ("pe", "act", "dve", "pool", "sp")

    def __init__(self, nc):
        self.nc = nc
        self.ops = {e: [] for e in self.ENGS}
        self.esem = {e: nc.alloc_semaphore("es_" + e) for e in ("pe", "act", "dve", "pool")}

    def op(self, eng, fn, reads=(), writes=(), dma=None, name=None, nodep=()):
        o = Op()
        o.eng = eng
        o.fn = fn
        o.name = name
        o.signal = False
        o.sem = None
        o.val = None
        o.inc = 1
        deps = []
        for r in reads:
            if r.writer is not None:
                deps.append(r.writer)
        for w in writes:
            if w.writer is not None:
                deps.append(w.writer)
            deps.extend(w.readers)
        seen = set()
        fd = []
        for d in deps:
            if id(d) in seen or d is o or d in nodep:
                continue
            seen.add(id(d))
            if d.eng == "pe" and eng == "pe":
                continue
            fd.append(d)
        o.deps = fd
        for d in fd:
            d.signal = True
        if dma is not None:
            dma.count += 16
            o.sem = dma.sem
            o.val = dma.count
            o.inc = 16
            o.signal = True
            dma.ops.append(o)
        for r in reads:
            r.readers.append(o)
        for w in writes:
            w.writer = o
            w.readers = []
        self.ops[eng].append(o)
        return o

    def finalize(self):
        for e in ("pe", "act", "dve", "pool"):
            c = 0
            for o in self.ops[e]:
                if o.sem is None and o.signal:
                    c += 1
                    o.sem = self.esem[e]
                    o.val = c
                    o.inc = 1
        for e in self.ENGS:
            for o in self.ops[e]:
                if o.signal and o.sem is None:
                    raise RuntimeError("signal op without sem: %s" % o.name)

    def emit(self, block):
        self.finalize()
        sched = self

        def run(eng_name, eng):
            waited = {}
            for o in sched.ops[eng_name]:
                need = {}
                for d in o.deps:
                    k = id(d.sem)
                    if k not in need or need[k][1] < d.val:
                        need[k] = (d.sem, d.val)
                for k, (sem, val) in need.items():
                    if waited.get(k, 0) >= val:
                        continue
                    eng.wait_ge(sem, val)
                    waited[k] = val
                last = o.fn(eng)
                if o.signal:
                    if last is None:
                        raise RuntimeError("op %s returned no instruction" % o.name)
                    last.then_inc(o.sem, o.inc)

        @block.tensor
        def _(eng):
            run("pe", eng)

        @block.scalar
        def _(eng):
            run("act", eng)

        @block.vector
        def _(eng):
            run("dve", eng)

        @block.gpsimd
        def _(eng):
            run("pool", eng)

        @block.sync
        def _(eng):
            run("sp", eng)


def build_program(dbg=None):
    nc = bass.Bass("TRN2", target_bir_lowering=False)

    def din(name, shape):
        return nc.dram_tensor(name, list(shape), F32, kind="ExternalInput").ap()

    x_d = din("x", [2 * SEQ, D])
    mem_d = din("mem", [2 * MEM, D])
    w_in_d = din("w_in", [D, 7168])
    w_pool_d = din("w_pool", [4, 128, 128])
    w_a_d = din("w_a", [512, D])
    w_r_d = din("w_r", [D, D])
    w_kv_d = din("w_mem_kv", [D, D])
    w_c_d = din("w_c", [512, D])
    w_out_d = din("w_out", [D, D])
    w_up_d = din("w_up", [D, 2 * FH])
    w_down_d = din("w_down", [FH, D])
    gvec_d = din("gvec", [4, D])
    pp_d = din("pp", [128, 108])
    ropet_d = din("ropet", [16, 128, 1024])
    mask_d = din("mask4", [128, 512])
    cst_d = din("cst", [128, 14, 128])
    y_d = nc.dram_tensor("y", [2 * SEQ, D], F32, kind="ExternalOutput").ap()
    dbg_d = {}
    if dbg:
        for nm, shp in dbg.items():
            dbg_d[nm] = nc.dram_tensor("dbg_" + nm, list(shp), F32, kind="ExternalOutput").ap()

    S = Sched(nc)

    def sb(name, shape, dt):
        return nc.alloc_sbuf_tensor("s_" + name, list(shape), dt)

    xres = sb("xres", [128, NT, D], F32)
    hT = sb("hT", [128, 8, T], BF16)
    hb = [sb("hb%d" % i, [128, D], BF16) for i in range(2)]
    junk = sb("junk", [128, D], BF16)
    ss = sb("ss", [128, NT], F32)
    rstd = sb("rstd", [128, NT], F32)
    ssf = sb("ssf", [128, NT], F32)
    rstdf = sb("rstdf", [128, NT], F32)
    r1f = sb("r1f", [128, 7680], F32)
    r1b = r1f.bitcast(BF16)
    v_v = r1b[:, 0:4096].rearrange("p (a b) -> p a b", a=NT)
    retT_v = r1b[:, 0:4096].rearrange("p (a b) -> p a b", a=8)
    ktok_v = r1b[:, 4096:6144].rearrange("p (a b) -> p a b", a=NT)
    kT_v = r1b[:, 6144:8192].rearrange("p (a b) -> p a b", a=4)
    qT_v = r1b[:, 8192:10240].rearrange("p (a b) -> p a b", a=4)
    on_v = r1b[:, 10240:14336].rearrange("p (a b) -> p a b", a=NT)
    gT_v = r1b[:, 0:NF * T].rearrange("p (a b) -> p a b", a=NF)
    acc_v = [r1f[:, 5632 + i * 512: 5632 + (i + 1) * 512] for i in range(2)]
    g1_v = [r1f[:, 6656 + i * 512: 6656 + (i + 1) * 512] for i in range(2)]
    mergedF = sb("mergedT", [128, 8 * T], F32)
    mergedT = mergedF[:, :].rearrange("p (a b) -> p a b", a=8)
    xn_v = mergedF[:, :].rearrange("p (a b) -> p a b", a=NT)
    r2 = sb("r2", [128, 4096], BF16)
    pooledT_v = r2[:, 0:2048].rearrange("p (a b) -> p a b", a=4)
    ypT_v = r2[:, 2048:4096].rearrange("p (a b) -> p a b", a=4)
    mbf_v = r2[:, 0:4096].rearrange("p (a b) -> p a b", a=8)
    hp_tok = sb("hp_tok", [128, NT + 1, 512], BF16)
    qxT = sb("qxT", [128, 4, T], BF16)
    oT = sb("oT", [128, 4, T], BF16)
    pT = [sb("pT%d" % i, [128, T], BF16) for i in range(4)]
    rden = sb("rden", [128, T], F32)
    th = [sb("th%d" % i, [128, T], F32) for i in range(2)]
    tt = [sb("tt%d" % i, [128, T], F32) for i in range(2)]
    rot = [sb("rot%d" % i, [128, 4, 4, 64], F32) for i in range(1)]
    krot = [sb("krot%d" % i, [128, 4, 128], BF16) for i in range(2)]
    ssb = [sb("ssb%d" % i, [128, 4, 128], BF16) for i in range(2)]
    ropes = [sb("ropes%d" % i, [128, 2, 4, 64], F32) for i in range(2)]
    carry = sb("carry", [128, NF, 2], F32)
    bnd = sb("bnd", [128, NF, 2], F32)
    btmp = sb("btmp", [128, 2, NF], F32)
    memT = sb("memT", [128, 8, MEM], BF16)
    kmT = sb("kmT", [128, 4, MEM], BF16)
    vm = sb("vm", [128, 2, 512], BF16)
    W32 = sb("W32", [128, 4, 256], F32)
    Rbf = sb("Rbf", [128, 4, 256], BF16)
    bnst = sb("bnst", [128, 4, 6], F32)
    mv = sb("mv", [128, 4, 2], F32)
    grs = sb("grs", [128, 4], F32)
    gnb = sb("gnb", [128, 4], F32)
    gt_mix = sb("gt_mix", [128, D], F32)
    gt_ffn = sb("gt_ffn", [128, D], F32)
    gt_fin = sb("gt_fin", [128, D], F32)
    yst = [sb("yst%d" % i, [128, D], F32) for i in range(4)]
    pp = sb("pp", [128, 108], F32)
    mask4 = sb("mask4", [128, 4, 128], F32)
    cst = sb("cst", [128, 14, 128], BF16)
    wpool = sb("wpool", [128, 4, 128], BF16)
    slots = [sb("wslot%d" % i, [128, 4096], BF16) for i in range(NS)]
    psf = [nc.alloc_psum_tensor("ps%d" % i, [128, 512], F32) for i in range(8)]
    psb = [p.bitcast(BF16) for p in psf]

    ident = cst[:, 12, :]
    ones = cst[:, 13, :]
    PS_OFF, GR_OFF, BR_OFF, CW0, CW1, CW2, CB = 0, 4, 12, 20, 42, 64, 86

    def R(n):
        return Res(n)

    r_x = [R("x%d" % i) for i in range(NT)]
    r_xn = [R("xn%d" % i) for i in range(NT)]
    r_hT = R("hT")
    r_hb = [R("hb0"), R("hb1")]
    r_junk = R("junk")
    r_ss = R("ss")
    r_rstd = R("rstd")
    r_ssf = R("ssf")
    r_rstdf = R("rstdf")
    r_v = R("v")
    r_ktok = R("ktok")
    r_kT = R("kT")
    r_qT = R("qT")
    r_on = R("on")
    r_gTk = [R("gT%d" % k) for k in range(3)]
    r_acc = [R("acc0"), R("acc1")]
    r_g1 = [R("g10"), R("g11")]
    r_merged = [R("mg%d" % j) for j in range(8)]
    r_pooledT = R("pooledT")
    r_ypT = R("ypT")
    r_mbf = R("mbf")
    r_hp = [R("hp%d" % i) for i in range(NT + 1)]
    r_qxT = R("qxT")
    r_oT = R("oT")
    r_pT = [R("pT%d" % i) for i in range(4)]
    r_rden = R("rden")
    r_th = [R("th0"), R("th1")]
    r_tt = [R("tt0"), R("tt1")]
    r_rot = [R("rot0")]
    r_krot = [R("krot0"), R("krot1")]
    r_ssb = [R("ssb0"), R("ssb1")]
    r_ropes = [R("ropes0"), R("ropes1")]
    r_carry = R("carry")
    r_bnd = R("bnd")
    r_btmp = R("btmp")
    r_memT = R("memT")
    r_kmT = R("kmT")
    r_vm = R("vm")
    r_W32 = R("W32")
    r_Rbf = R("Rbf")
    r_bn = R("bn")
    r_mv = R("mv")
    r_grs = R("grs")
    r_gnb = R("gnb")
    r_gt = R("gtiles")
    r_yst = [R("yst%d" % i) for i in range(4)]
    r_const = R("const")
    r_const2 = R("const2")
    r_ps = [R("ps%d" % i) for i in range(8)]
    r_slot = [R("slot%d" % i) for i in range(NS)]

    d_x = [DmaSem(nc, "d_x%d" % i) for i in range(NT)]
    d_yst = [DmaSem(nc, "d_y%d" % i) for i in range(4)]
    d_ropes = [DmaSem(nc, "d_r%d" % i) for i in range(2)]
    d_const = DmaSem(nc, "d_c")
    d_const2 = DmaSem(nc, "d_c2")
    d_slot = [DmaSem(nc, "d_s%d" % i) for i in range(NS)]
    d_dbg = DmaSem(nc, "d_dbg")

    bank_ctr = [0]

    def bank():
        b = bank_ctr[0]
        bank_ctr[0] = (b + 1) % 8
        return b

    NUW = 37
    wsc_d = nc.dram_tensor("wsc", [NUW, 128, 4096], BF16, kind="Internal").ap()
    r_wsc = [Res("wsc%d" % u) for u in range(NUW)]
    d_wst = [DmaSem(nc, "d_w%d" % i) for i in range(NS)]

    class WStream:
        def __init__(self):
            self.units = []
            self.issued = 0
            self.slot_of = {}
            self.free = list(range(NS))

        def add(self, name, pieces):
            uid, grp = self.cur
            self.units.append((name, pieces, uid, grp))

        def pump(self):
            while self.issued < len(self.units) and self.free:
                j = self.issued
                s = self.free.pop(0)
                self.slot_of[j] = s
                name, pieces, uid, grp = self.units[j]
                if uid is None or grp == 0:
                    prev = []
                    for (dstf, src) in pieces:
                        o = S.op("pool", lambda e, dstf=dstf, src=src, s=s: e.dma_start(out=dstf(slots[s]), in_=src),
                                 writes=[r_slot[s]], dma=d_slot[s], nodep=tuple(prev), name="wload")
                        prev.append(o)
                    if uid is not None:
                        S.op("sp", lambda e, s=s, uid=uid: e.dma_start(out=wsc_d[uid, :, :], in_=slots[s][:, :]),
                             reads=[r_slot[s]], writes=[r_wsc[uid]], dma=d_wst[s], name="wstore")
                else:
                    S.op("pool", lambda e, s=s, uid=uid: e.dma_start(out=slots[s][:, :], in_=wsc_d[uid, :, :]),
                         reads=[r_wsc[uid]], writes=[r_slot[s]], dma=d_slot[s], name="wload2")
                self.issued += 1

        def take(self, j, name):
            assert self.units[j][0] == name, (self.units[j][0], name)
            self.pump()
            assert self.issued > j, ("weight unit not issued", j, name)
            s = self.slot_of[j]
            return slots[s], r_slot[s]

        def release(self, j):
            self.free.append(self.slot_of[j])
            self.pump()

    W = WStream()
    w_in_v = w_in_d.rearrange("(k p) n -> p k n", p=128)
    w_r_v = w_r_d.rearrange("(k p) n -> p k n", p=128)
    w_kv_v = w_kv_d.rearrange("(k p) n -> p k n", p=128)
    w_out_v = w_out_d.rearrange("(k p) n -> p k n", p=128)
    w_up_v = w_up_d.rearrange("(k p) n -> p k n", p=128)
    w_a_v = w_a_d.rearrange("(k p) n -> p k n", p=128)
    w_c_v = w_c_d.rearrange("(k p) n -> p k n", p=128)
    w_down_v = w_down_d.rearrange("(f p) n -> p f n", p=128)

    def k8(slot):
        return slot[:, 0:4096].rearrange("p (k n) -> p k n", k=8)

    def k4(slot):
        return slot[:, 0:4096].rearrange("p (k n) -> p k n", k=4)

    def unit_k8(src_v, c0):
        return [(lambda sl: k8(sl), src_v[:, :, c0:c0 + 512])]

    IN_COL = {"hp": 0, "q": 512, "k": 1024, "v0": 1536, "v1": 2048, "gr0": 2560, "gr1": 3072, "qx": 3584,
              "gp0": 4096, "gp1": 4608, "gret0": 5120, "gret1": 5632, "gm0": 6144, "gm1": 6656}
    order = []
    for g in range(NG):
        gl = []
        gl += ["v0", "k", "v1", "q", "gr0", "gr1", "hp", "qx", "a", "gp0", "gp1", "r0", "gret0", "r1", "gret1",
                  "c", "gm0", "gm1", "out0", "out1"]
        gl += ["up%d" % u for u in range(11)]
        gl += ["dn%d_%d" % (c2, k) for c2 in range(2) for k in range(3)]
        assert len(gl) == NUW
        for u, nm in enumerate(gl):
            if nm == "hp" and g % GPS == 0:
                order += [("kvk", None, g), ("kvv", None, g)]
            order.append((nm, u, g))
    for (nm, uid, grp) in order:
        W.cur = (uid, grp)
        if nm in IN_COL:
            W.add(nm, unit_k8(w_in_v, IN_COL[nm]))
        elif nm == "kvk":
            W.add(nm, unit_k8(w_kv_v, 0))
        elif nm == "kvv":
            W.add(nm, unit_k8(w_kv_v, 512))
        elif nm in ("r0", "r1"):
            W.add(nm, unit_k8(w_r_v, 512 * int(nm[1])))
        elif nm in ("out0", "out1"):
            W.add(nm, unit_k8(w_out_v, 512 * int(nm[3])))
        elif nm == "a":
            W.add(nm, [(lambda sl: k4(sl), w_a_v)])
        elif nm == "c":
            W.add(nm, [(lambda sl: k4(sl), w_c_v)])
        elif nm.startswith("up"):
            u = int(nm[2:])
            W.add(nm, [(lambda sl: k8(sl)[:, :, 0:256], w_up_v[:, :, u * 256:(u + 1) * 256]),
                       (lambda sl: k8(sl)[:, :, 256:512], w_up_v[:, :, FH + u * 256:FH + (u + 1) * 256])])
        elif nm.startswith("dn"):
            c2 = int(nm[2])
            k = int(nm[4])
            nf = min(8, NF - 8 * k)
            W.add(nm, [(lambda sl, nf=nf: k8(sl)[:, 0:nf, :],
                        w_down_v[:, 8 * k:8 * k + nf, c2 * 512:(c2 + 1) * 512])])
        else:
            raise ValueError(nm)
    wpos = [0]

    def wtake(name):
        j = wpos[0]
        wpos[0] += 1
        sl, rs = W.take(j, name)
        return j, sl, rs

    def mm_group(e, out_ap, pairs):
        n = len(pairs)
        last = None
        for i, (l, r) in enumerate(pairs):
            last = e.matmul(out_ap, lhsT=l, rhs=r, start=(i == 0), stop=(i == n - 1))
        return last

    def dump(name, ap, res):
        if dbg and name in dbg_d and name not in dumped:
            dumped.add(name)
            S.op("pool", lambda e: e.dma_start(out=dbg_d[name], in_=ap), reads=res, dma=d_dbg, name="dbg")
    dumped = set()

    S.op("sp", lambda e: e.dma_start(out=pp[:], in_=pp_d), writes=[r_const], dma=d_const)
    S.op("sp", lambda e: e.dma_start(out=mask4[:].rearrange("p a b -> p (a b)"), in_=mask_d), writes=[r_const],
         dma=d_const, nodep=tuple(d_const.ops))
    S.op("sp", lambda e: e.dma_start(out=gt_mix[:], in_=gvec_d[0, :].partition_broadcast(128)), writes=[r_gt],
         dma=d_const)
    S.op("sp", lambda e: e.dma_start(out=gt_ffn[:], in_=gvec_d[1, :].partition_broadcast(128)), writes=[r_gt],
         dma=d_const, nodep=tuple(d_const.ops))
    S.op("sp", lambda e: e.dma_start(out=gt_fin[:], in_=gvec_d[2, :].partition_broadcast(128)), writes=[r_gt],
         dma=d_const, nodep=tuple(d_const.ops))
    S.op("pool", lambda e: e.dma_start(out=cst[:], in_=cst_d), writes=[r_const2], dma=d_const2)
    S.op("pool", lambda e: e.dma_start(out=wpool[:], in_=w_pool_d.rearrange("g c d -> c g d")), writes=[r_const2],
         dma=d_const2, nodep=tuple(d_const2.ops))
    for o in d_const.ops:
        o.val = d_const.count
    for o in d_const2.ops:
        o.val = d_const2.count

    def norm_stats(src_fn, r_src, ntiles):
        for i in range(ntiles):
            S.op("act", lambda e, i=i: e.activation(out=junk[:], in_=src_fn(i), func=AF.Square,
                                                    accum_out=ss[:, i:i + 1]),
                 reads=[r_src[i]], writes=[r_junk, r_ss], name="sq")
        S.op("dve", lambda e: e.tensor_scalar(out=rstd[:, 0:ntiles], in0=ss[:, 0:ntiles], scalar1=1.0 / D,
                                              scalar2=EPS, op0=ALU.mult, op1=ALU.add),
             reads=[r_ss], writes=[r_rstd])
        S.op("act", lambda e: e.activation(out=rstd[:, 0:ntiles], in_=rstd[:, 0:ntiles], func=AF.Sqrt),
             reads=[r_rstd], writes=[r_rstd])
        S.op("dve", lambda e: e.reciprocal(out=rstd[:, 0:ntiles], in_=rstd[:, 0:ntiles]),
             reads=[r_rstd], writes=[r_rstd])

    def norm_tile(i, src_fn, r_src, gtile, dstT, dst_res):
        hbi = i % 2
        S.op("dve", lambda e, i=i, hbi=hbi: e.scalar_tensor_tensor(
            out=hb[hbi][:], in0=src_fn(i), scalar=rstd[:, i:i + 1], in1=gtile[:], op0=ALU.mult, op1=ALU.mult),
            reads=[r_src[i], r_rstd, r_gt], writes=[r_hb[hbi]])
        b = bank()

        def tr(e, hbi=hbi, b=b):
            last = None
            for k in range(8):
                last = e.transpose(out=psb[b][:, k * 128:(k + 1) * 128], in_=hb[hbi][:, k * 128:(k + 1) * 128],
                                   identity=ident)
            return last
        S.op("pe", tr, reads=[r_hb[hbi], r_const, r_const2], writes=[r_ps[b]])
        S.op("act", lambda e, i=i, b=b: e.activation(
            out=dstT[:, :, i * 128:(i + 1) * 128],
            in_=psb[b][:, 0:1024].rearrange("p (k t) -> p k t", k=8), func=AF.Copy),
            reads=[r_ps[b]], writes=[dst_res])

    def norm_to_T(src_fn, r_src, gtile, dstT, dst_res, ntiles, tok_stride):
        norm_stats(src_fn, r_src, ntiles)
        for i in range(ntiles):
            norm_tile(i, src_fn, r_src, gtile, dstT, dst_res)

    def load_x(g):
        for i in range(NT):
            S.op("sp", lambda e, i=i, g=g: e.dma_start(out=xn_v[:, i, :],
                                                      in_=x_d[g * T + i * 128: g * T + (i + 1) * 128, :]),
                 writes=[r_xn[i], r_merged[2 * i], r_merged[2 * i + 1]], dma=d_x[i], name="xload")

    def copy_x():
        for i in range(NT):
            S.op("act", lambda e, i=i: e.activation(out=xres[:, i, :], in_=xn_v[:, i, :], func=AF.Copy),
                 reads=[r_xn[i]], writes=[r_x[i]])

    out_ops = []

    def do_group(g):
        seq = g // GPS
        gi = g % GPS
        tok0 = g * T
        first = (gi == 0)

        if first:
            S.op("dve", lambda e: e.memset(W32[:], 0.0), writes=[r_W32])
            S.op("dve", lambda e: e.memset(Rbf[:], 0.0), writes=[r_Rbf])
            S.op("dve", lambda e: e.memset(carry[:], 0.0), writes=[r_carry])

        dump("hT", hT[:].rearrange("p a b -> p (a b)"), [r_hT])

        fence1 = r_gTk + [r_acc[0], r_acc[1], r_g1[0], r_g1[1]]
        for vu, which in ((0, "k"), (1, "q")):
            jv, slv, rsv = wtake("v%d" % vu)
            j, sl, rs = wtake(which)
            pend = None
            for i in range(NT):
                c = gi * NT + i
                rb = i % 2
                tcos = 0 if which == "q" else 2
                S.op("sp", lambda e, c=c, rb=rb, tcos=tcos: e.dma_start(
                    out=ropes[rb][:, 0:2, :, :].rearrange("p a b c -> p (a b c)"),
                    in_=ropet_d[c, :, tcos * 256:(tcos + 2) * 256]),
                    writes=[r_ropes[rb]], dma=d_ropes[rb])
                b = bank()
                S.op("pe", lambda e, i=i, b=b, sl=sl: mm_group(
                    e, psf[b][:], [(hT[:, kc, i * 128:(i + 1) * 128], k8(sl)[:, kc, :]) for kc in range(8)]),
                    reads=[rs, r_hT], writes=[r_ps[b]])
                bv = bank()
                S.op("pe", lambda e, i=i, bv=bv, slv=slv: mm_group(
                    e, psf[bv][:], [(hT[:, kc, i * 128:(i + 1) * 128], k8(slv)[:, kc, :]) for kc in range(8)]),
                    reads=[rsv, r_hT], writes=[r_ps[bv]])
                S.op("act", lambda e, i=i, bv=bv, vu=vu: e.activation(out=v_v[:, i, vu * 512:(vu + 1) * 512],
                                                                       in_=psf[bv][:], func=AF.Copy),
                     reads=[r_ps[bv]], writes=[r_v] + (fence1 if (vu == 0 and i == 0) else []))
                pv = psf[b][:, :].rearrange("p (h d) -> p h d", h=4)
                x1 = pv[:, :, 0:64]
                x2 = pv[:, :, 64:128]
                kb = i % 2
                rt = rot[0]

                def rotary(e, x1=x1, x2=x2, rb=rb, rt=rt):
                    e.tensor_tensor(out=rt[:, 0, :, :], in0=x1, in1=ropes[rb][:, 0, :, :], op=ALU.mult)
                    e.tensor_tensor(out=rt[:, 1, :, :], in0=x2, in1=ropes[rb][:, 1, :, :], op=ALU.mult)
                    e.tensor_tensor(out=rt[:, 2, :, :], in0=x1, in1=ropes[rb][:, 1, :, :], op=ALU.mult)
                    return e.tensor_tensor(out=rt[:, 3, :, :], in0=x2, in1=ropes[rb][:, 0, :, :], op=ALU.mult)
                S.op("dve", rotary, reads=[r_ps[b], r_ropes[rb]], writes=[r_rot[0]])
                if which == "k":
                    dst = ktok_v[:, i, :].rearrange("p (h d) -> p h d", h=4)
                    dres = r_ktok
                else:
                    dst = krot[kb][:]
                    dres = r_krot[kb]

                def rotary2(e, dst=dst, rt=rt):
                    e.tensor_tensor(out=dst[:, :, 0:64], in0=rt[:, 0, :, :], in1=rt[:, 1, :, :], op=ALU.subtract)
                    return e.tensor_tensor(out=dst[:, :, 64:128], in0=rt[:, 2, :, :], in1=rt[:, 3, :, :], op=ALU.add)
                S.op("dve", rotary2, reads=[r_rot[0]], writes=[dres])
                if which == "q" and pending and i < NT - 1:
                    pending[-1].part(i)
                src = ktok_v[:, i, :] if which == "k" else krot[kb][:].rearrange("p h d -> p (h d)")
                dT = kT_v if which == "k" else qT_v
                dTres = r_kT if which == "k" else r_qT

                def emit_tr(i=i, src=src, dres=dres, dT=dT, dTres=dTres):
                    b2 = bank()

                    def trq(e, b2=b2, src=src):
                        last = None
                        for h in range(4):
                            last = e.transpose(out=psb[b2][:, h * 128:(h + 1) * 128],
                                               in_=src[:, h * 128:(h + 1) * 128], identity=ident)
                        return last
                    S.op("pe", trq, reads=[dres, r_const, r_const2], writes=[r_ps[b2]])
                    S.op("act", lambda e, i=i, b2=b2, dT=dT: e.activation(
                        out=dT[:, :, i * 128:(i + 1) * 128],
                        in_=psb[b2][:, 0:512].rearrange("p (h t) -> p h t", h=4), func=AF.Copy),
                        reads=[r_ps[b2]], writes=[dTres])
                if pend is not None:
                    pend()
                pend = emit_tr
            pend()
            if which == "q" and pending:
                pending[-1].part(NT - 1)
            W.release(jv)
            W.release(j)
        if pending:
            pending.pop()
        dump("qT", qT_v.rearrange("p a b -> p (a b)"), [r_qT])
        dump("kT", kT_v.rearrange("p a b -> p (a b)"), [r_kT])
        dump("v", v_v.rearrange("p a b -> p (a b)"), [r_v])

        GC = [float(np.exp(float(T) * np.log1p(-2.0 ** (-5.0 - h)))) for h in range(4)]
        sbufs = [(ssb[0], r_ssb[0]), (ssb[1], r_ssb[1])] + [(pT[q_][:, :].rearrange("p (h t) -> p h t", h=4), r_pT[q_])
                                                           for q_ in range(4)]
        sb_free = list(range(6))
        S_blk = {}
        o_banks = {}

        def rec_scores(i):
            for jt in range(i + 1):
                bs = bank()

                def scores(e, i=i, jt=jt, bs=bs):
                    last = None
                    for h in range(4):
                        last = e.matmul(psf[bs][:, h * 128:(h + 1) * 128], lhsT=kT_v[:, h, jt * 128:(jt + 1) * 128],
                                        rhs=qT_v[:, h, i * 128:(i + 1) * 128], start=True, stop=True)
                    return last
                S.op("pe", scores, reads=[r_kT, r_qT], writes=[r_ps[bs]])
                bi = sb_free.pop(0)
                buf, rbuf = sbufs[bi]
                bufap = buf[:] if bi < 2 else buf
                if jt == i:
                    S.op("dve", lambda e, bs=bs, bufap=bufap: e.tensor_tensor(
                        out=bufap, in0=psf[bs][:, :].rearrange("p (h t) -> p h t", h=4), in1=mask4[:], op=ALU.mult),
                        reads=[r_ps[bs], r_const, r_const2], writes=[rbuf])
                else:
                    S.op("act", lambda e, bs=bs, bufap=bufap: e.activation(
                        out=bufap, in_=psf[bs][:, :].rearrange("p (h t) -> p h t", h=4), func=AF.Copy),
                        reads=[r_ps[bs]], writes=[rbuf])
                S_blk[(jt, i)] = (bi, bufap, rbuf)

        def rec_o(i):
            bo = [bank(), bank()]
            blks = [S_blk[(jt, i)] for jt in range(i + 1)]

            def omm(e, i=i, bo=bo, blks=blks):
                last = None
                for h in range(4):
                    o_ap = psf[bo[h // 2]][:, (h % 2) * 256:(h % 2 + 1) * 256]
                    for jt, (bi, bufap, rbuf) in enumerate(blks):
                        e.matmul(o_ap, lhsT=bufap[:, h, :], rhs=v_v[:, jt, h * 256:(h + 1) * 256], start=(jt == 0),
                                 stop=False)
                    last = e.matmul(o_ap, lhsT=qT_v[:, h, i * 128:(i + 1) * 128], rhs=Rbf[:, h, :], start=False,
                                    stop=True)
                return last
            S.op("pe", omm, reads=[rb for (_, _, rb) in blks] + [r_v, r_qT, r_Rbf],
                 writes=[r_ps[bo[0]], r_ps[bo[1]]])
            for (bi, _, _) in blks:
                sb_free.append(bi)
            o_banks[i] = bo

        def rec_gn(i):
            bo = o_banks[i]

            def bst(e, bo=bo):
                last = None
                for h in range(4):
                    last = e.bn_stats(out=bnst[:, h, :], in_=psf[bo[h // 2]][:, (h % 2) * 256:(h % 2 + 1) * 256])
                return last
            S.op("dve", bst, reads=[r_ps[bo[0]], r_ps[bo[1]]], writes=[r_bn])

            def bag(e):
                last = None
                for h in range(4):
                    last = e.bn_aggr(out=mv[:, h, :], in_=bnst[:, h, :])
                return last
            S.op("dve", bag, reads=[r_bn], writes=[r_mv])
            S.op("dve", lambda e: e.tensor_scalar(out=grs[:], in0=mv[:, :, 1], scalar1=EPS, scalar2=None,
                                                  op0=ALU.add), reads=[r_mv], writes=[r_grs])
            S.op("act", lambda e: e.activation(out=grs[:], in_=grs[:], func=AF.Sqrt), reads=[r_grs], writes=[r_grs])
            S.op("dve", lambda e: e.reciprocal(out=grs[:], in_=grs[:]), reads=[r_grs], writes=[r_grs])
            S.op("dve", lambda e: e.scalar_tensor_tensor(out=gnb[:], in0=mv[:, :, 0], scalar=-1.0, in1=grs[:],
                                                         op0=ALU.mult, op1=ALU.mult),
                 reads=[r_mv, r_grs], writes=[r_gnb])

            def onorm(e, i=i, bo=bo):
                last = None
                for h in range(4):
                    last = e.activation(out=on_v[:, i, h * 256:(h + 1) * 256],
                                        in_=psf[bo[h // 2]][:, (h % 2) * 256:(h % 2 + 1) * 256],
                                        func=AF.Identity, scale=grs[:, h:h + 1], bias=gnb[:, h:h + 1])
                return last
            S.op("act", onorm, reads=[r_ps[bo[0]], r_ps[bo[1]], r_grs, r_gnb], writes=[r_on])

        rec_scores(0)
        rec_scores(1)
        rec_scores(2)
        rec_o(0)
        rec_o(1)
        rec_gn(0)
        rec_o(2)
        rec_gn(1)
        rec_scores(3)
        rec_gn(2)
        rec_o(3)
        rec_gn(3)

        bd = [bank(), bank()]

        def dmm(e, bd=bd):
            last = None
            for h in range(4):
                d_ap = psf[bd[h // 2]][:, (h % 2) * 256:(h % 2 + 1) * 256]
                for jt in range(NT):
                    last = e.matmul(d_ap, lhsT=ktok_v[:, jt, h * 128:(h + 1) * 128],
                                    rhs=v_v[:, jt, h * 256:(h + 1) * 256], start=(jt == 0), stop=(jt == NT - 1))
            return last
        S.op("pe", dmm, reads=[r_ktok, r_v], writes=[r_ps[bd[0]], r_ps[bd[1]]])

        def wupd(e, bd=bd):
            last = None
            for h in range(4):
                d_ap = psf[bd[h // 2]][:, (h % 2) * 256:(h % 2 + 1) * 256]
                last = e.scalar_tensor_tensor(out=W32[:, h, :], in0=W32[:, h, :], scalar=GC[h], in1=d_ap,
                                              op0=ALU.mult, op1=ALU.add)
            return last
        S.op("dve", wupd, reads=[r_ps[bd[0]], r_ps[bd[1]]], writes=[r_W32])

        def rupd(e):
            last = None
            for h in range(4):
                last = e.activation(out=Rbf[:, h, :], in_=W32[:, h, :], func=AF.Copy, scale=GC[h])
            return last
        S.op("act", rupd, reads=[r_W32], writes=[r_Rbf])
        dump("on", on_v.rearrange("p a b -> p (a b)"), [r_on])

        for i in range(NT):
            b = bank()

            def tro(e, i=i, b=b):
                last = None
                for fc in range(8):
                    last = e.transpose(out=psb[b][:, fc * 128:(fc + 1) * 128], in_=on_v[:, i, fc * 128:(fc + 1) * 128],
                                       identity=ident)
                return last
            S.op("pe", tro, reads=[r_on, r_const, r_const2], writes=[r_ps[b]])

            def aff(e, i=i, b=b):
                last = None
                for fc in range(8):
                    last = e.activation(out=retT_v[:, fc, i * 128:(i + 1) * 128], in_=psb[b][:, fc * 128:(fc + 1) * 128],
                                        func=AF.Identity, scale=pp[:, GR_OFF + fc:GR_OFF + fc + 1],
                                        bias=pp[:, BR_OFF + fc:BR_OFF + fc + 1])
                return last
            S.op("act", aff, reads=[r_ps[b], r_const, r_const2], writes=[r_v])

        cnt = 0
        for u in range(2):
            j, sl, rs = wtake("gr%d" % u)
            for fl in range(4):
                fc = u * 4 + fl
                b = bank()
                ti = cnt % 2
                cnt += 1
                S.op("pe", lambda e, fl=fl, b=b, sl=sl: mm_group(
                    e, psf[b][:], [(k8(sl)[:, kc, fl * 128:(fl + 1) * 128], hT[:, kc, :]) for kc in range(8)]),
                    reads=[rs, r_hT], writes=[r_ps[b]])
                S.op("act", lambda e, b=b, ti=ti: e.activation(out=th[ti][:], in_=psf[b][:], func=AF.Tanh, scale=0.5),
                     reads=[r_ps[b]], writes=[r_th[ti]])
                S.op("dve", lambda e, b=b, ti=ti: e.scalar_tensor_tensor(
                    out=tt[ti][:], in0=th[ti][:], scalar=1.0, in1=psf[b][:], op0=ALU.add, op1=ALU.mult),
                    reads=[r_th[ti], r_ps[b]], writes=[r_tt[ti]])
                S.op("dve", lambda e, fc=fc, ti=ti: e.scalar_tensor_tensor(
                    out=retT_v[:, fc, :], in0=tt[ti][:], scalar=0.5, in1=retT_v[:, fc, :], op0=ALU.mult, op1=ALU.mult),
                    reads=[r_tt[ti], r_v], writes=[r_v])
            W.release(j)
        dump("retT", retT_v.rearrange("p a b -> p (a b)"), [r_v])

        def branch_merge(jl_range, j0, y_pairs_fn, y_reads, gate_sl, gate_rs, mode):
            nonlocal cnt
            for jl in jl_range:
                jd = j0 + jl
                by = bank()
                S.op("pe", lambda e, jl=jl, by=by: mm_group(e, psf[by][:], y_pairs_fn(jl)),
                     reads=y_reads, writes=[r_ps[by]])
                bg = bank()
                S.op("pe", lambda e, jl=jl, bg=bg: mm_group(
                    e, psf[bg][:], [(k8(gate_sl)[:, kc, jl * 128:(jl + 1) * 128], hT[:, kc, :]) for kc in range(8)]),
                    reads=[gate_rs, r_hT], writes=[r_ps[bg]])
                ti = cnt % 2
                cnt += 1
                S.op("act", lambda e, bg=bg, ti=ti: e.activation(out=th[ti][:], in_=psf[bg][:], func=AF.Tanh,
                                                                 scale=0.5),
                     reads=[r_ps[bg]], writes=[r_th[ti]])
                if mode == "set":
                    S.op("dve", lambda e, by=by, ti=ti, jd=jd: e.scalar_tensor_tensor(
                        out=mergedT[:, jd, :], in0=th[ti][:], scalar=1.0, in1=psf[by][:], op0=ALU.add, op1=ALU.mult),
                        reads=[r_th[ti], r_ps[by]], writes=[r_merged[jd], r_xn[jd // 2]])
                else:
                    S.op("dve", lambda e, by=by, ti=ti: e.scalar_tensor_tensor(
                        out=tt[ti][:], in0=th[ti][:], scalar=1.0, in1=psf[by][:], op0=ALU.add, op1=ALU.mult),
                        reads=[r_th[ti], r_ps[by]], writes=[r_tt[ti]])
                    if mode == "add":
                        S.op("dve", lambda e, ti=ti, jd=jd: e.tensor_tensor(
                            out=mergedT[:, jd, :], in0=mergedT[:, jd, :], in1=tt[ti][:], op=ALU.add),
                            reads=[r_tt[ti], r_merged[jd]], writes=[r_merged[jd]])
                    else:
                        S.op("dve", lambda e, ti=ti, jd=jd: e.tensor_tensor(
                            out=mbf_v[:, jd, :], in0=mergedT[:, jd, :], in1=tt[ti][:], op=ALU.add),
                            reads=[r_tt[ti], r_merged[jd]], writes=[r_mbf, r_pooledT, r_ypT])

        if first:
            S.op("sp", lambda e: e.dma_start(out=yst[1][:], in_=gvec_d[3, :].partition_broadcast(128)),
                 writes=[r_yst[1]], dma=d_yst[1])
            r_memtile = [r_yst[0], r_yst[0]]
            r_gt_save = r_gt
            for mi in range(2):
                S.op("sp", lambda e, mi=mi: e.dma_start(
                    out=yst[0][:], in_=mem_d[seq * MEM + mi * 128: seq * MEM + (mi + 1) * 128, :]),
                    writes=[r_yst[0]], dma=d_yst[0])
                S.op("act", lambda e: e.activation(out=junk[:], in_=yst[0][:], func=AF.Square, accum_out=ss[:, 0:1]),
                     reads=[r_yst[0]], writes=[r_junk, r_ss])
                S.op("dve", lambda e: e.tensor_scalar(out=rstd[:, 0:1], in0=ss[:, 0:1], scalar1=1.0 / D, scalar2=EPS,
                                                      op0=ALU.mult, op1=ALU.add), reads=[r_ss], writes=[r_rstd])
                S.op("act", lambda e: e.activation(out=rstd[:, 0:1], in_=rstd[:, 0:1], func=AF.Sqrt),
                     reads=[r_rstd], writes=[r_rstd])
                S.op("dve", lambda e: e.reciprocal(out=rstd[:, 0:1], in_=rstd[:, 0:1]), reads=[r_rstd],
                     writes=[r_rstd])
                S.op("dve", lambda e: e.scalar_tensor_tensor(out=hb[0][:], in0=yst[0][:], scalar=rstd[:, 0:1],
                                                             in1=yst[1][:], op0=ALU.mult, op1=ALU.mult),
                     reads=[r_yst[0], r_yst[1], r_rstd], writes=[r_hb[0]])
                b = bank()

                def trm(e, b=b):
                    last = None
                    for k in range(8):
                        last = e.transpose(out=psb[b][:, k * 128:(k + 1) * 128], in_=hb[0][:, k * 128:(k + 1) * 128],
                                           identity=ident)
                    return last
                S.op("pe", trm, reads=[r_hb[0], r_const, r_const2], writes=[r_ps[b]])
                S.op("act", lambda e, mi=mi, b=b: e.activation(
                    out=memT[:, :, mi * 128:(mi + 1) * 128],
                    in_=psb[b][:, 0:1024].rearrange("p (k t) -> p k t", k=8), func=AF.Copy),
                    reads=[r_ps[b]], writes=[r_memT])
            j, sl, rs = wtake("kvk")
            for h in range(4):
                b = bank()
                S.op("pe", lambda e, h=h, b=b, sl=sl: mm_group(
                    e, psf[b][:, 0:MEM], [(k8(sl)[:, kc, h * 128:(h + 1) * 128], memT[:, kc, :]) for kc in range(8)]),
                    reads=[rs, r_memT], writes=[r_ps[b]])
                S.op("act", lambda e, h=h, b=b: e.activation(out=kmT[:, h, :], in_=psf[b][:, 0:MEM], func=AF.Copy),
                     reads=[r_ps[b]], writes=[r_kmT])
            W.release(j)
            j, sl, rs = wtake("kvv")
            for mc in range(2):
                b = bank()
                S.op("pe", lambda e, mc=mc, b=b, sl=sl: mm_group(
                    e, psf[b][:], [(memT[:, kc, mc * 128:(mc + 1) * 128], k8(sl)[:, kc, :]) for kc in range(8)]),
                    reads=[rs, r_memT], writes=[r_ps[b]])
                S.op("act", lambda e, mc=mc, b=b: e.activation(out=vm[:, mc, :], in_=psf[b][:], func=AF.Copy),
                     reads=[r_ps[b]], writes=[r_vm])
            W.release(j)

        jh, slh, rsh = wtake("hp")
        jq, slq, rsq = wtake("qx")
        for i in range(NT):
            b = bank()
            S.op("pe", lambda e, i=i, b=b, slh=slh: mm_group(
                e, psf[b][:], [(hT[:, kc, i * 128:(i + 1) * 128], k8(slh)[:, kc, :]) for kc in range(8)]),
                reads=[rsh, r_hT], writes=[r_ps[b]])
            S.op("act", lambda e, i=i, b=b: e.activation(out=hp_tok[:, i + 1, :], in_=psf[b][:], func=AF.Copy),
                 reads=[r_ps[b]], writes=[r_hp[i + 1]])
            h = i
            b = bank()
            S.op("pe", lambda e, h=h, b=b, slq=slq: mm_group(
                e, psf[b][:], [(k8(slq)[:, kc, h * 128:(h + 1) * 128], hT[:, kc, :]) for kc in range(8)]),
                reads=[rsq, r_hT], writes=[r_ps[b]])
            S.op("act", lambda e, h=h, b=b: e.activation(out=qxT[:, h, :], in_=psf[b][:], func=AF.Copy),
                 reads=[r_ps[b]], writes=[r_qxT])
        W.release(jh)
        W.release(jq)
        pcnt = 0
        for step in range(4):
            gq = step
            b = bank()

            def poolmm(e, gq=gq, b=b):
                last = None
                for i in range(NT):
                    o_ap = psf[b][:, i * 128:(i + 1) * 128]
                    cur = hp_tok[:, i + 1, gq * 128:(gq + 1) * 128]
                    if first and i == 0:
                        last = e.matmul(o_ap, lhsT=cur, rhs=cst[:, 8 + gq, :], start=True, stop=True)
                    else:
                        e.matmul(o_ap, lhsT=cur, rhs=cst[:, gq, :], start=True, stop=False)
                        last = e.matmul(o_ap, lhsT=hp_tok[:, i, gq * 128:(gq + 1) * 128], rhs=cst[:, 4 + gq, :],
                                        start=False, stop=True)
                return last
            S.op("pe", poolmm, reads=r_hp + [r_const, r_const2], writes=[r_ps[b]])
            S.op("act", lambda e, gq=gq, b=b: e.activation(out=pooledT_v[:, gq, :], in_=psf[b][:], func=AF.Copy),
                 reads=[r_ps[b]], writes=[r_pooledT] + ([r_mbf] if gq == 0 else []))
            h = step
            pis = []
            for mc in range(2):
                bsx = bank()
                pi = pcnt % 4
                pcnt += 1
                pis.append(pi)
                S.op("pe", lambda e, h=h, mc=mc, bsx=bsx: e.matmul(
                    psf[bsx][:], lhsT=kmT[:, h, mc * 128:(mc + 1) * 128], rhs=qxT[:, h, :], start=True, stop=True),
                    reads=[r_kmT, r_qxT], writes=[r_ps[bsx]])
                S.op("act", lambda e, bsx=bsx, pi=pi: e.activation(out=pT[pi][:], in_=psf[bsx][:], func=AF.Exp,
                                                                   scale=float(128.0 ** -0.5)),
                     reads=[r_ps[bsx]], writes=[r_pT[pi]])
            b2 = bank()
            S.op("pe", lambda e, gq=gq, b2=b2: e.matmul(psf[b2][:], lhsT=wpool[:, gq, :], rhs=pooledT_v[:, gq, :],
                                                        start=True, stop=True),
                 reads=[r_pooledT, r_const, r_const2], writes=[r_ps[b2]])
            S.op("act", lambda e, gq=gq, b2=b2: e.activation(out=ypT_v[:, gq, :], in_=psf[b2][:], func=AF.Copy,
                                                             scale=pp[:, PS_OFF + gq:PS_OFF + gq + 1]),
                 reads=[r_ps[b2], r_const, r_const2], writes=[r_ypT])
            bo_ = bank()
            S.op("pe", lambda e, h=h, bo_=bo_, pis=tuple(pis): mm_group(
                e, psf[bo_][:], [(vm[:, mc, h * 128:(h + 1) * 128], pT[pis[mc]][:]) for mc in range(2)]),
                reads=[r_vm, r_pT[pis[0]], r_pT[pis[1]]], writes=[r_ps[bo_]])
            bden = bank()
            S.op("pe", lambda e, bden=bden, pis=tuple(pis): mm_group(
                e, psf[bden][:], [(ones, pT[pis[mc]][:]) for mc in range(2)]),
                reads=[r_const, r_const2, r_pT[pis[0]], r_pT[pis[1]]], writes=[r_ps[bden]])
            S.op("dve", lambda e, bden=bden: e.reciprocal(out=rden[:], in_=psf[bden][:]),
                 reads=[r_ps[bden]], writes=[r_rden])
            S.op("dve", lambda e, h=h, bo_=bo_: e.tensor_tensor(out=oT[:, h, :], in0=psf[bo_][:], in1=rden[:],
                                                               op=ALU.mult),
                 reads=[r_ps[bo_], r_rden], writes=[r_oT])
        S.op("act", lambda e: e.activation(out=hp_tok[:, 0, :], in_=hp_tok[:, NT, :], func=AF.Copy),
             reads=[r_hp[NT]], writes=[r_hp[0]])
        dump("pooledT", pooledT_v.rearrange("p a b -> p (a b)"), [r_pooledT])
        dump("oT", oT[:].rearrange("p a b -> p (a b)"), [r_oT])

        ja, sla, rsa = wtake("a")
        for u in range(2):
            jg, slg, rsg = wtake("gp%d" % u)
            branch_merge(range(4), u * 4,
                         lambda jl, u=u, sla=sla: [(k4(sla)[:, q4, (u * 4 + jl) * 128:(u * 4 + jl + 1) * 128],
                                                   ypT_v[:, q4, :]) for q4 in range(4)],
                         [rsa, r_ypT], slg, rsg, "set")
            W.release(jg)
        W.release(ja)
        for u in range(2):
            jr, slr, rsr = wtake("r%d" % u)
            jg, slg, rsg = wtake("gret%d" % u)
            branch_merge(range(4), u * 4,
                         lambda jl, slr=slr: [(k8(slr)[:, kc, jl * 128:(jl + 1) * 128], retT_v[:, kc, :])
                                              for kc in range(8)],
                         [rsr, r_v], slg, rsg, "add")
            W.release(jr)
            W.release(jg)
        jc, slc, rsc = wtake("c")
        for u in range(2):
            jg, slg, rsg = wtake("gm%d" % u)
            branch_merge(range(4), u * 4,
                         lambda jl, u=u, slc=slc: [(k4(slc)[:, q4, (u * 4 + jl) * 128:(u * 4 + jl + 1) * 128],
                                                   oT[:, q4, :]) for q4 in range(4)],
                         [rsc, r_oT], slg, rsg, "final")
            W.release(jg)
        W.release(jc)
        dump("mbf", mbf_v.rearrange("p a b -> p (a b)"), [r_mbf])

        jo0, slo0, rso0 = wtake("out0")
        jo1, slo1, rso1 = wtake("out1")
        for i in range(NT):
            for c2, slo, rso in ((0, slo0, rso0), (1, slo1, rso1)):
                b = bank()
                S.op("pe", lambda e, i=i, b=b, slo=slo: mm_group(
                    e, psf[b][:], [(mbf_v[:, kc, i * 128:(i + 1) * 128], k8(slo)[:, kc, :]) for kc in range(8)]),
                    reads=[rso, r_mbf], writes=[r_ps[b]])
                S.op("dve", lambda e, i=i, b=b, c2=c2: e.scalar_tensor_tensor(
                    out=xres[:, i, c2 * 512:(c2 + 1) * 512], in0=psf[b][:], scalar=0.5,
                    in1=xres[:, i, c2 * 512:(c2 + 1) * 512], op0=ALU.mult, op1=ALU.add),
                    reads=[r_ps[b], r_x[i]], writes=[r_x[i]])
        W.release(jo0)
        W.release(jo1)
        dump("x2", xres[:].rearrange("p a b -> p (a b)"), r_x)
        if g + 1 < NG:
            load_x(g + 1)

        norm_to_T(lambda i: xres[:, i, :], r_x, gt_ffn, hT, r_hT, NT, 128)
        fence2 = [r_v, r_ktok, r_kT, r_qT, r_on]
        acnt = 0
        if not first:
            S.op("dve", lambda e: e.tensor_tensor(out=btmp[:, 0, :], in0=carry[:, :, 1], in1=pp[:, CW1:CW1 + NF],
                                                  op=ALU.mult), reads=[r_carry, r_const, r_const2], writes=[r_btmp])
            S.op("dve", lambda e: e.tensor_tensor(out=btmp[:, 1, :], in0=carry[:, :, 0], in1=pp[:, CW0:CW0 + NF],
                                                  op=ALU.mult), reads=[r_carry, r_const, r_const2], writes=[r_btmp])
            S.op("dve", lambda e: e.tensor_tensor(out=bnd[:, :, 1], in0=carry[:, :, 1], in1=pp[:, CW0:CW0 + NF],
                                                  op=ALU.mult), reads=[r_carry, r_const, r_const2], writes=[r_bnd])
            S.op("dve", lambda e: e.tensor_tensor(out=bnd[:, :, 0], in0=btmp[:, 0, :], in1=btmp[:, 1, :],
                                                  op=ALU.add), reads=[r_btmp, r_bnd], writes=[r_bnd])
        for u in range(11):
            j, sl, rs = wtake("up%d" % u)
            for fl in range(2):
                f = 2 * u + fl
                ai = acnt % 2
                acnt += 1
                ba = bank()
                S.op("pe", lambda e, fl=fl, ba=ba, sl=sl: mm_group(
                    e, psf[ba][:], [(k8(sl)[:, kc, fl * 128:(fl + 1) * 128], hT[:, kc, :]) for kc in range(8)]),
                    reads=[rs, r_hT], writes=[r_ps[ba]])
                bb = bank()
                S.op("pe", lambda e, fl=fl, bb=bb, sl=sl: mm_group(
                    e, psf[bb][:], [(k8(sl)[:, kc, 256 + fl * 128:256 + (fl + 1) * 128], hT[:, kc, :])
                                    for kc in range(8)]),
                    reads=[rs, r_hT], writes=[r_ps[bb]])
                S.op("act", lambda e, f=f, ba=ba, ai=ai: e.activation(
                    out=acc_v[ai], in_=psf[ba][:], func=AF.Identity, scale=pp[:, CW2 + f:CW2 + f + 1],
                    bias=pp[:, CB + f:CB + f + 1]),
                    reads=[r_ps[ba], r_const, r_const2], writes=[r_acc[ai]] + (fence2 + r_gTk if (u == 0 and fl == 0) else []))

                S.op("dve", lambda e, f=f, ba=ba, ai=ai: e.scalar_tensor_tensor(
                    out=acc_v[ai][:, 1:T], in0=psf[ba][:, 0:T - 1], scalar=pp[:, CW1 + f:CW1 + f + 1],
                    in1=acc_v[ai][:, 1:T], op0=ALU.mult, op1=ALU.add),
                    reads=[r_ps[ba], r_acc[ai], r_const, r_const2], writes=[r_acc[ai]])
                S.op("dve", lambda e, f=f, ba=ba, ai=ai: e.scalar_tensor_tensor(
                    out=acc_v[ai][:, 2:T], in0=psf[ba][:, 0:T - 2], scalar=pp[:, CW0 + f:CW0 + f + 1],
                    in1=acc_v[ai][:, 2:T], op0=ALU.mult, op1=ALU.add),
                    reads=[r_ps[ba], r_acc[ai], r_const, r_const2], writes=[r_acc[ai]])
                if not first:
                    S.op("dve", lambda e, f=f, ai=ai: e.tensor_tensor(
                        out=acc_v[ai][:, 0:2], in0=acc_v[ai][:, 0:2], in1=bnd[:, f, :], op=ALU.add),
                        reads=[r_acc[ai], r_bnd], writes=[r_acc[ai]])
                S.op("act", lambda e, f=f, ba=ba: e.activation(out=carry[:, f, :], in_=psf[ba][:, T - 2:T],
                                                               func=AF.Copy),
                     reads=[r_ps[ba]], writes=[r_carry])
                S.op("act", lambda e, ai=ai: e.activation(out=g1_v[ai], in_=acc_v[ai], func=AF.Gelu_apprx_tanh),
                     reads=[r_acc[ai]], writes=[r_g1[ai]])
                S.op("dve", lambda e, f=f, bb=bb, ai=ai: e.tensor_tensor(out=gT_v[:, f, :], in0=g1_v[ai],
                                                                         in1=psf[bb][:], op=ALU.mult),
                     reads=[r_g1[ai], r_ps[bb]], writes=[r_gTk[f // 8]])
            W.release(j)
        dump("gT", gT_v.rearrange("p a b -> p (a b)"), r_gTk)

        nxt = (g + 1 < NG)
        if nxt:
            norm_stats(lambda i: xn_v[:, i, :], r_xn, NT)

        for c2 in range(2):
            bks = [bank() for _ in range(NT)]
            for k in range(3):
                j, sl, rs = wtake("dn%d_%d" % (c2, k))
                nf = min(8, NF - 8 * k)

                def down(e, k=k, nf=nf, sl=sl, bks=bks):
                    last = None
                    for i in range(NT):
                        for fl in range(nf):
                            f = 8 * k + fl
                            last = e.matmul(psf[bks[i]][:], lhsT=gT_v[:, f, i * 128:(i + 1) * 128],
                                            rhs=k8(sl)[:, fl, :], start=(f == 0), stop=(f == NF - 1))
                    return last
                S.op("pe", down, reads=[rs, r_gTk[k]], writes=[r_ps[bk] for bk in bks])
                W.release(j)
                if nxt and c2 == 1 and k == 0:
                    for i in range(NT):
                        norm_tile(i, lambda i: xn_v[:, i, :], r_xn, gt_mix, hT, r_hT)
            for i in range(NT):
                S.op("dve", lambda e, i=i, c2=c2, bks=bks: e.tensor_tensor(
                    out=xres[:, i, c2 * 512:(c2 + 1) * 512], in0=psf[bks[i]][:],
                    in1=xres[:, i, c2 * 512:(c2 + 1) * 512], op=ALU.add),
                    reads=[r_ps[bks[i]], r_x[i]], writes=[r_x[i]])

        for i in range(NT):
            S.op("act", lambda e, i=i: e.activation(out=junk[:], in_=xres[:, i, :], func=AF.Square,
                                                    accum_out=ssf[:, i:i + 1]),
                 reads=[r_x[i]], writes=[r_junk, r_ssf])
        S.op("dve", lambda e: e.tensor_scalar(out=rstdf[:], in0=ssf[:], scalar1=1.0 / D, scalar2=EPS, op0=ALU.mult,
                                              op1=ALU.add), reads=[r_ssf], writes=[r_rstdf])
        S.op("act", lambda e: e.activation(out=rstdf[:], in_=rstdf[:], func=AF.Sqrt), reads=[r_rstdf],
             writes=[r_rstdf])
        S.op("dve", lambda e: e.reciprocal(out=rstdf[:], in_=rstdf[:]), reads=[r_rstdf], writes=[r_rstdf])

        def tail_part(i, tok0=tok0, nxt=nxt):
            yi = i % 4
            S.op("dve", lambda e, i=i, yi=yi: e.scalar_tensor_tensor(
                out=yst[yi][:], in0=xres[:, i, :], scalar=rstdf[:, i:i + 1], in1=gt_fin[:], op0=ALU.mult,
                op1=ALU.mult),
                reads=[r_x[i], r_rstdf, r_gt], writes=[r_yst[yi]])
            o = S.op("act", lambda e, i=i, yi=yi, tok0=tok0: e.dma_start(
                out=y_d[tok0 + i * 128: tok0 + (i + 1) * 128, :], in_=yst[yi][:]),
                reads=[r_yst[yi]], dma=d_yst[yi], name="ystore")
            out_ops.append(o)
            if nxt:
                S.op("act", lambda e, i=i: e.activation(out=xres[:, i, :], in_=xn_v[:, i, :], func=AF.Copy),
                     reads=[r_xn[i]], writes=[r_x[i]])

        def tail(tail_part=tail_part):
            for i in range(NT):
                tail_part(i)
        tail.part = tail_part
        pending.append(tail)

    pending = []
    load_x(0)
    norm_to_T(lambda i: xn_v[:, i, :], r_xn, gt_mix, hT, r_hT, NT, 128)
    copy_x()
    for g in range(NG):
        do_group(g)
    pending.pop()()

    fin = S.op("sp", lambda e: None, name="final_wait")
    fin.deps = list(out_ops[-4:]) + list(d_dbg.ops)
    for o in fin.deps:
        o.signal = True
    with nc.Block() as block:
        S.emit(block)
    return nc


def make_consts():
    f32 = np.float32
    half = 64
    inv = (np.float32(10000.0) ** (-np.arange(half, dtype=f32) / np.float32(half))).astype(f32)
    pos = np.arange(SEQ, dtype=f32)
    ang = (pos[:, None] * inv[None, :]).astype(f32).astype(np.float64)
    cos = np.cos(ang)
    sin = np.sin(ang)
    lg = np.log1p(-np.exp2(-5.0 - np.arange(4, dtype=np.float64)))
    p1 = ((np.arange(16)[:, None] % NT) * 128 + np.arange(1, 129)[None, :]).astype(np.float64)
    qd = np.exp(p1[:, :, None] * lg[None, None, :])
    kd = np.exp(-p1[:, :, None] * lg[None, None, :]) * (128.0 ** -0.5)
    ropet = np.zeros((16, 128, 4, 4, 64), np.float64)
    cosr = cos.reshape(16, 128, 1, 64)
    sinr = sin.reshape(16, 128, 1, 64)
    ropet[:, :, 0] = cosr * qd[:, :, :, None]
    ropet[:, :, 1] = sinr * qd[:, :, :, None]
    ropet[:, :, 2] = cosr * kd[:, :, :, None]
    ropet[:, :, 3] = sinr * kd[:, :, :, None]
    ropet = ropet.reshape(16, 128, 1024).astype(f32)
    kk = np.arange(128)[:, None]
    qq = np.arange(128)[None, :]
    m = (kk <= qq).astype(f32)
    mask4 = np.repeat(m[:, None, :], 4, axis=1).reshape(128, 512).astype(f32)
    cst = np.zeros((128, 14, 128), np.float64)
    tp = np.arange(128)[:, None]
    t = np.arange(128)[None, :]
    for gq, w in enumerate((2, 4, 8, 16)):
        cst[:, gq, :] = ((tp <= t) & (tp > t - w)) / w - (tp == t)
        cst[:, 4 + gq, :] = ((tp - 128) > (t - w)) / w
        cst[:, 8 + gq, :] = ((tp <= t) & (tp > t - w)) / np.minimum(t + 1, w) - (tp == t)
    cst[:, 12, :] = np.eye(128)
    cst[:, 13, :] = 1.0
    return ropet, mask4, cst.astype(f32)


_PROGRAM = {}


def kernel(x, mem, g_mix, w_in, w_pool, pool_scale, w_a, g_ret, b_ret, w_r, g_mem, w_mem_kv, w_c, w_out,
           g_ffn, w_up, conv_w, conv_b, w_down, g_final, _dbg=None):
    f32 = np.float32
    x = np.asarray(x, f32)
    mem = np.asarray(mem, f32)
    ropet, mask4, cst = make_consts()
    gvec = np.ascontiguousarray(np.stack([np.asarray(g_mix, f32)[0], np.asarray(g_ffn, f32)[0],
                                          np.asarray(g_final, f32), np.asarray(g_mem, f32)[0]]))
    cw = np.asarray(conv_w, f32)[0]
    pp = np.concatenate([
        np.asarray(pool_scale, f32)[0].reshape(4, 128).T,
        np.asarray(g_ret, f32)[0].reshape(8, 128).T,
        np.asarray(b_ret, f32)[0].reshape(8, 128).T,
        cw[0].reshape(NF, 128).T, cw[1].reshape(NF, 128).T, cw[2].reshape(NF, 128).T,
        np.asarray(conv_b, f32)[0].reshape(NF, 128).T], axis=1)
    pp = np.ascontiguousarray(pp, dtype=f32)
    shared = {
        "w_in": np.ascontiguousarray(np.asarray(w_in, f32)[0]),
        "w_pool": np.ascontiguousarray(np.asarray(w_pool, f32)[0]),
        "w_a": np.ascontiguousarray(np.asarray(w_a, f32)[0]),
        "w_r": np.ascontiguousarray(np.asarray(w_r, f32)[0]),
        "w_mem_kv": np.ascontiguousarray(np.asarray(w_mem_kv, f32)[0]),
        "w_c": np.ascontiguousarray(np.asarray(w_c, f32)[0]),
        "w_out": np.ascontiguousarray(np.asarray(w_out, f32)[0]),
        "w_up": np.ascontiguousarray(np.asarray(w_up, f32)[0]),
        "w_down": np.ascontiguousarray(np.asarray(w_down, f32)[0]),
        "gvec": gvec, "pp": pp, "ropet": ropet, "mask4": mask4, "cst": cst,
    }
    in_maps = []
    for c in range(NCORES):
        m = dict(shared)
        m["x"] = np.ascontiguousarray(x[2 * c:2 * c + 2].reshape(2 * SEQ, D))
        m["mem"] = np.ascontiguousarray(mem[2 * c:2 * c + 2].reshape(2 * MEM, D))
        in_maps.append(m)
    key = tuple(sorted(_dbg.items())) if _dbg else None
    if key not in _PROGRAM:
        _PROGRAM[key] = build_program(_dbg)
    nc = _PROGRAM[key]
    res = run_bass_kernel_spmd(nc, in_maps, core_ids=list(range(NCORES)))
    out = np.concatenate([np.asarray(r["y"], f32).reshape(2, SEQ, D) for r in res.results], axis=0)
    if _dbg:
        return out, res.results
    return out
```

```python
import numpy as np
import concourse.bass as bass
import concourse.mybir as mybir
from concourse.bass_utils import run_bass_kernel_spmd

F32 = mybir.dt.float32
BF16 = mybir.dt.bfloat16
AF = mybir.ActivationFunctionType
ALU = mybir.AluOpType

NCORES = 8
D = 1024
SEQ = 2048
T = 512
NT = T // 128
NG = 2 * SEQ // T
GPS = SEQ // T
MEM = 256
FH = 2816
NF = FH // 128
EPS = 1e-6
NS = 4


class Res:
    __slots__ = ("name", "writer", "readers")

    def __init__(self, name):
        self.name = name
        self.writer = None
        self.readers = []


class Op:
    __slots__ = ("eng", "fn", "deps", "signal", "sem", "val", "inc", "name")


class DmaSem:
    def __init__(self, nc, name):
        self.sem = nc.alloc_semaphore(name)
        self.count = 0
        self.ops = []


class Sched:
    ENGS = ("pe", "act", "dve", "pool", "sp")

    def __init__(self, nc):
        self.nc = nc
        self.ops = {e: [] for e in self.ENGS}
        self.esem = {e: nc.alloc_semaphore("es_" + e) for e in ("pe", "act", "dve", "pool")}

    def op(self, eng, fn, reads=(), writes=(), dma=None, name=None, nodep=()):
        o = Op()
        o.eng = eng
        o.fn = fn
        o.name = name
        o.signal = False
        o.sem = None
        o.val = None
        o.inc = 1
        deps = []
        for r in reads:
            if r.writer is not None:
                deps.append(r.writer)
        for w in writes:
            if w.writer is not None:
                deps.append(w.writer)
            deps.extend(w.readers)
        seen = set()
        fd = []
        for d in deps:
            if id(d) in seen or d is o or d in nodep:
                continue
            seen.add(id(d))
            if d.eng == "pe" and eng == "pe":
                continue
            fd.append(d)
        o.deps = fd
        for d in fd:
            d.signal = True
        if dma is not None:
            dma.count += 16
            o.sem = dma.sem
            o.val = dma.count
            o.inc = 16
            o.signal = True
            dma.ops.append(o)
        for r in reads:
            r.readers.append(o)
        for w in writes:
            w.writer = o
            w.readers = []
        self.ops[eng].append(o)
        return o

    def finalize(self):
        for e in ("pe", "act", "dve", "pool"):
            c = 0
            for o in self.ops[e]:
                if o.sem is None and o.signal:
                    c += 1
                    o.sem = self.esem[e]
                    o.val = c
                    o.inc = 1
        for e in self.ENGS:
            for o in self.ops[e]:
                if o.signal and o.sem is None:
                    raise RuntimeError("signal op without sem: %s" % o.name)

    def emit(self, block):
        self.finalize()
        sched = self

        def run(eng_name, eng):
            waited = {}
            for o in sched.ops[eng_name]:
                need = {}
                for d in o.deps:
                    k = id(d.sem)
                    if k not in need or need[k][1] < d.val:
                        need[k] = (d.sem, d.val)
                for k, (sem, val) in need.items():
                    if waited.get(k, 0) >= val:
                        continue
                    eng.wait_ge(sem, val)
                    waited[k] = val
                last = o.fn(eng)
                if o.signal:
                    if last is None:
                        raise RuntimeError("op %s returned no instruction" % o.name)
                    last.then_inc(o.sem, o.inc)

        @block.tensor
        def _(eng):
            run("pe", eng)

        @block.scalar
        def _(eng):
            run("act", eng)

        @block.vector
        def _(eng):
            run("dve", eng)

        @block.gpsimd
        def _(eng):
            run("pool", eng)

        @block.sync
        def _(eng):
            run("sp", eng)


def build_program(dbg=None):
    nc = bass.Bass("TRN2", target_bir_lowering=False)

    def din(name, shape):
        return nc.dram_tensor(name, list(shape), F32, kind="ExternalInput").ap()

    x_d = din("x", [2 * SEQ, D])
    mem_d = din("mem", [2 * MEM, D])
    w_in_d = din("w_in", [D, 7168])
    w_pool_d = din("w_pool", [4, 128, 128])
    w_a_d = din("w_a", [512, D])
    w_r_d = din("w_r", [D, D])
    w_kv_d = din("w_mem_kv", [D, D])
    w_c_d = din("w_c", [512, D])
    w_out_d = din("w_out", [D, D])
    w_up_d = din("w_up", [D, 2 * FH])
    w_down_d = din("w_down", [FH, D])
    gvec_d = din("gvec", [4, D])
    pp_d = din("pp", [128, 108])
    ropet_d = din("ropet", [16, 128, 1024])
    mask_d = din("mask4", [128, 512])
    cst_d = din("cst", [128, 14, 128])
    y_d = nc.dram_tensor("y", [2 * SEQ, D], F32, kind="ExternalOutput").ap()
    dbg_d = {}
    if dbg:
        for nm, shp in dbg.items():
            dbg_d[nm] = nc.dram_tensor("dbg_" + nm, list(shp), F32, kind="ExternalOutput").ap()

    S = Sched(nc)

    def sb(name, shape, dt):
        return nc.alloc_sbuf_tensor("s_" + name, list(shape), dt)

    xres = sb("xres", [128, NT, D], F32)
    hT = sb("hT", [128, 8, T], BF16)
    hb = [sb("hb%d" % i, [128, D], BF16) for i in range(2)]
    junk = sb("junk", [128, D], BF16)
    ss = sb("ss", [128, NT], F32)
    rstd = sb("rstd", [128, NT], F32)
    ssf = sb("ssf", [128, NT], F32)
    rstdf = sb("rstdf", [128, NT], F32)
    r1f = sb("r1f", [128, 7680], F32)
    r1b = r1f.bitcast(BF16)
    v_v = r1b[:, 0:4096].rearrange("p (a b) -> p a b", a=NT)
    retT_v = r1b[:, 0:4096].rearrange("p (a b) -> p a b", a=8)
    ktok_v = r1b[:, 4096:6144].rearrange("p (a b) -> p a b", a=NT)
    kT_v = r1b[:, 6144:8192].rearrange("p (a b) -> p a b", a=4)
    qT_v = r1b[:, 8192:10240].rearrange("p (a b) -> p a b", a=4)
    on_v = r1b[:, 10240:14336].rearrange("p (a b) -> p a b", a=NT)
    gT_v = r1b[:, 0:NF * T].rearrange("p (a b) -> p a b", a=NF)
    acc_v = [r1f[:, 5632 + i * 512: 5632 + (i + 1) * 512] for i in range(2)]
    g1_v = [r1f[:, 6656 + i * 512: 6656 + (i + 1) * 512] for i in range(2)]
    mergedF = sb("mergedT", [128, 8 * T], F32)
    mergedT = mergedF[:, :].rearrange("p (a b) -> p a b", a=8)
    xn_v = mergedF[:, :].rearrange("p (a b) -> p a b", a=NT)
    r2 = sb("r2", [128, 4096], BF16)
    pooledT_v = r2[:, 0:2048].rearrange("p (a b) -> p a b", a=4)
    ypT_v = r2[:, 2048:4096].rearrange("p (a b) -> p a b", a=4)
    mbf_v = r2[:, 0:4096].rearrange("p (a b) -> p a b", a=8)
    hp_tok = sb("hp_tok", [128, NT + 1, 512], BF16)
    qxT = sb("qxT", [128, 4, T], BF16)
    oT = sb("oT", [128, 4, T], BF16)
    pT = [sb("pT%d" % i, [128, T], BF16) for i in range(4)]
    rden = sb("rden", [128, T], F32)
    th = [sb("th%d" % i, [128, T], F32) for i in range(2)]
    tt = [sb("tt%d" % i, [128, T], F32) for i in range(2)]
    rot = [sb("rot%d" % i, [128, 4, 4, 64], F32) for i in range(1)]
    krot = [sb("krot%d" % i, [128, 4, 128], BF16) for i in range(2)]
    ssb = [sb("ssb%d" % i, [128, 4, 128], BF16) for i in range(2)]
    ropes = [sb("ropes%d" % i, [128, 2, 4, 64], F32) for i in range(2)]
    carry = sb("carry", [128, NF, 2], F32)
    bnd = sb("bnd", [128, NF, 2], F32)
    btmp = sb("btmp", [128, 2, NF], F32)
    memT = sb("memT", [128, 8, MEM], BF16)
    kmT = sb("kmT", [128, 4, MEM], BF16)
    vm = sb("vm", [128, 2, 512], BF16)
    W32 = sb("W32", [128, 4, 256], F32)
    Rbf = sb("Rbf", [128, 4, 256], BF16)
    bnst = sb("bnst", [128, 4, 6], F32)
    mv = sb("mv", [128, 4, 2], F32)
    grs = sb("grs", [128, 4], F32)
    gnb = sb("gnb", [128, 4], F32)
    gt_mix = sb("gt_mix", [128, D], F32)
    gt_ffn = sb("gt_ffn", [128, D], F32)
    gt_fin = sb("gt_fin", [128, D], F32)
    yst = [sb("yst%d" % i, [128, D], F32) for i in range(4)]
    pp = sb("pp", [128, 108], F32)
    mask4 = sb("mask4", [128, 4, 128], F32)
    cst = sb("cst", [128, 14, 128], BF16)
    wpool = sb("wpool", [128, 4, 128], BF16)
    slots = [sb("wslot%d" % i, [128, 4096], BF16) for i in range(NS)]
    psf = [nc.alloc_psum_tensor("ps%d" % i, [128, 512], F32) for i in range(8)]
    psb = [p.bitcast(BF16) for p in psf]

    ident = cst[:, 12, :]
    ones = cst[:, 13, :]
    PS_OFF, GR_OFF, BR_OFF, CW0, CW1, CW2, CB = 0, 4, 12, 20, 42, 64, 86

    def R(n):
        return Res(n)

    r_x = [R("x%d" % i) for i in range(NT)]
    r_xn = [R("xn%d" % i) for i in range(NT)]
    r_hT = R("hT")
    r_hb = [R("hb0"), R("hb1")]
    r_junk = R("junk")
    r_ss = R("ss")
    r_rstd = R("rstd")
    r_ssf = R("ssf")
    r_rstdf = R("rstdf")
    r_v = R("v")
    r_ktok = R("ktok")
    r_kT = R("kT")
    r_qT = R("qT")
    r_on = R("on")
    r_gTk = [R("gT%d" % k) for k in range(3)]
    r_acc = [R("acc0"), R("acc1")]
    r_g1 = [R("g10"), R("g11")]
    r_merged = [R("mg%d" % j) for j in range(8)]
    r_pooledT = R("pooledT")
    r_ypT = R("ypT")
    r_mbf = R("mbf")
    r_hp = [R("hp%d" % i) for i in range(NT + 1)]
    r_qxT = R("qxT")
    r_oT = R("oT")
    r_pT = [R("pT%d" % i) for i in range(4)]
    r_rden = R("rden")
    r_th = [R("th0"), R("th1")]
    r_tt = [R("tt0"), R("tt1")]
    r_rot = [R("rot0")]
    r_krot = [R("krot0"), R("krot1")]
    r_ssb = [R("ssb0"), R("ssb1")]
    r_ropes = [R("ropes0"), R("ropes1")]
    r_carry = R("carry")
    r_bnd = R("bnd")
    r_btmp = R("btmp")
    r_memT = R("memT")
    r_kmT = R("kmT")
    r_vm = R("vm")
    r_W32 = R("W32")
    r_Rbf = R("Rbf")
    r_bn = R("bn")
    r_mv = R("mv")
    r_grs = R("grs")
    r_gnb = R("gnb")
    r_gt = R("gtiles")
    r_yst = [R("yst%d" % i) for i in range(4)]
    r_const = R("const")
    r_const2 = R("const2")
    r_ps = [R("ps%d" % i) for i in range(8)]
    r_slot = [R("slot%d" % i) for i in range(NS)]

    d_x = [DmaSem(nc, "d_x%d" % i) for i in range(NT)]
    d_yst = [DmaSem(nc, "d_y%d" % i) for i in range(4)]
    d_ropes = [DmaSem(nc, "d_r%d" % i) for i in range(2)]
    d_const = DmaSem(nc, "d_c")
    d_const2 = DmaSem(nc, "d_c2")
    d_slot = [DmaSem(nc, "d_s%d" % i) for i in range(NS)]
    d_dbg = DmaSem(nc, "d_dbg")

    bank_ctr = [0]

    def bank():
        b = bank_ctr[0]
        bank_ctr[0] = (b + 1) % 8
        return b

    NUW = 37
    wsc_d = nc.dram_tensor("wsc", [NUW, 128, 4096], BF16, kind="Internal").ap()
    r_wsc = [Res("wsc%d" % u) for u in range(NUW)]
    d_wst = [DmaSem(nc, "d_w%d" % i) for i in range(NS)]

    class WStream:
        def __init__(self):
            self.units = []
            self.issued = 0
            self.slot_of = {}
            self.free = list(range(NS))

        def add(self, name, pieces):
            uid, grp = self.cur
            self.units.append((name, pieces, uid, grp))

        def pump(self):
            while self.issued < len(self.units) and self.free:
                j = self.issued
                s = self.free.pop(0)
                self.slot_of[j] = s
                name, pieces, uid, grp = self.units[j]
                if uid is None or grp == 0:
                    prev = []
                    for (dstf, src) in pieces:
                        o = S.op("pool", lambda e, dstf=dstf, src=src, s=s: e.dma_start(out=dstf(slots[s]), in_=src),
                                 writes=[r_slot[s]], dma=d_slot[s], nodep=tuple(prev), name="wload")
                        prev.append(o)
                else:
                    S.op("pool", lambda e, s=s, uid=uid: e.dma_start(out=slots[s][:, :], in_=wsc_d[uid, :, :]),
                         reads=[r_wsc[uid]], writes=[r_slot[s]], dma=d_slot[s], name="wload2")
                self.issued += 1

        def take(self, j, name):
            assert self.units[j][0] == name, (self.units[j][0], name)
            self.pump()
            assert self.issued > j, ("weight unit not issued", j, name)
            s = self.slot_of[j]
            return slots[s], r_slot[s]

        def release(self, j):
            s = self.slot_of[j]
            name, pieces, uid, grp = self.units[j]
            if uid is not None and grp == 0:
                S.op("sp", lambda e, s=s, uid=uid: e.dma_start(out=wsc_d[uid, :, :], in_=slots[s][:, :]),
                     reads=[r_slot[s]], writes=[r_wsc[uid]], dma=d_wst[s], name="wstore")
            self.free.append(s)
            self.pump()

    W = WStream()
    w_in_v = w_in_d.rearrange("(k p) n -> p k n", p=128)
    w_r_v = w_r_d.rearrange("(k p) n -> p k n", p=128)
    w_kv_v = w_kv_d.rearrange("(k p) n -> p k n", p=128)
    w_out_v = w_out_d.rearrange("(k p) n -> p k n", p=128)
    w_up_v = w_up_d.rearrange("(k p) n -> p k n", p=128)
    w_a_v = w_a_d.rearrange("(k p) n -> p k n", p=128)
    w_c_v = w_c_d.rearrange("(k p) n -> p k n", p=128)
    w_down_v = w_down_d.rearrange("(f p) n -> p f n", p=128)

    def k8(slot):
        return slot[:, 0:4096].rearrange("p (k n) -> p k n", k=8)

    def k4(slot):
        return slot[:, 0:4096].rearrange("p (k n) -> p k n", k=4)

    def unit_k8(src_v, c0):
        return [(lambda sl: k8(sl), src_v[:, :, c0:c0 + 512])]

    IN_COL = {"hp": 0, "q": 512, "k": 1024, "v0": 1536, "v1": 2048, "gr0": 2560, "gr1": 3072, "qx": 3584,
              "gp0": 4096, "gp1": 4608, "gret0": 5120, "gret1": 5632, "gm0": 6144, "gm1": 6656}
    order = []
    for g in range(NG):
        gl = []
        gl += ["v0", "k", "v1", "q", "gr0", "gr1", "hp", "qx", "a", "gp0", "gp1", "r0", "gret0", "r1", "gret1",
                  "c", "gm0", "gm1", "out0", "out1"]
        gl += ["up%d" % u for u in range(11)]
        gl += ["dn%d_%d" % (c2, k) for c2 in range(2) for k in range(3)]
        assert len(gl) == NUW
        for u, nm in enumerate(gl):
            if nm == "hp" and g % GPS == 0:
                order += [("kvk", None, g), ("kvv", None, g)]
            order.append((nm, u, g))
    for (nm, uid, grp) in order:
        W.cur = (uid, grp)
        if nm in IN_COL:
            W.add(nm, unit_k8(w_in_v, IN_COL[nm]))
        elif nm == "kvk":
            W.add(nm, unit_k8(w_kv_v, 0))
        elif nm == "kvv":
            W.add(nm, unit_k8(w_kv_v, 512))
        elif nm in ("r0", "r1"):
            W.add(nm, unit_k8(w_r_v, 512 * int(nm[1])))
        elif nm in ("out0", "out1"):
            W.add(nm, unit_k8(w_out_v, 512 * int(nm[3])))
        elif nm == "a":
            W.add(nm, [(lambda sl: k4(sl), w_a_v)])
        elif nm == "c":
            W.add(nm, [(lambda sl: k4(sl), w_c_v)])
        elif nm.startswith("up"):
            u = int(nm[2:])
            W.add(nm, [(lambda sl: k8(sl)[:, :, 0:256], w_up_v[:, :, u * 256:(u + 1) * 256]),
                       (lambda sl: k8(sl)[:, :, 256:512], w_up_v[:, :, FH + u * 256:FH + (u + 1) * 256])])
        elif nm.startswith("dn"):
            c2 = int(nm[2])
            k = int(nm[4])
            nf = min(8, NF - 8 * k)
            W.add(nm, [(lambda sl, nf=nf: k8(sl)[:, 0:nf, :],
                        w_down_v[:, 8 * k:8 * k + nf, c2 * 512:(c2 + 1) * 512])])
        else:
            raise ValueError(nm)
    wpos = [0]

    def wtake(name):
        j = wpos[0]
        wpos[0] += 1
        sl, rs = W.take(j, name)
        return j, sl, rs

    def mm_group(e, out_ap, pairs):
        n = len(pairs)
        last = None
        for i, (l, r) in enumerate(pairs):
            last = e.matmul(out_ap, lhsT=l, rhs=r, start=(i == 0), stop=(i == n - 1))
        return last

    def dump(name, ap, res):
        if dbg and name in dbg_d and name not in dumped:
            dumped.add(name)
            S.op("pool", lambda e: e.dma_start(out=dbg_d[name], in_=ap), reads=res, dma=d_dbg, name="dbg")
    dumped = set()

    S.op("sp", lambda e: e.dma_start(out=pp[:], in_=pp_d), writes=[r_const], dma=d_const)
    S.op("sp", lambda e: e.dma_start(out=mask4[:].rearrange("p a b -> p (a b)"), in_=mask_d), writes=[r_const],
         dma=d_const, nodep=tuple(d_const.ops))
    S.op("sp", lambda e: e.dma_start(out=gt_mix[:], in_=gvec_d[0, :].partition_broadcast(128)), writes=[r_gt],
         dma=d_const)
    S.op("sp", lambda e: e.dma_start(out=gt_ffn[:], in_=gvec_d[1, :].partition_broadcast(128)), writes=[r_gt],
         dma=d_const, nodep=tuple(d_const.ops))
    S.op("sp", lambda e: e.dma_start(out=gt_fin[:], in_=gvec_d[2, :].partition_broadcast(128)), writes=[r_gt],
         dma=d_const, nodep=tuple(d_const.ops))
    S.op("pool", lambda e: e.dma_start(out=cst[:], in_=cst_d), writes=[r_const2], dma=d_const2)
    S.op("pool", lambda e: e.dma_start(out=wpool[:], in_=w_pool_d.rearrange("g c d -> c g d")), writes=[r_const2],
         dma=d_const2, nodep=tuple(d_const2.ops))
    for o in d_const.ops:
        o.val = d_const.count
    for o in d_const2.ops:
        o.val = d_const2.count

    def norm_stats(src_fn, r_src, ntiles):
        for i in range(ntiles):
            S.op("act", lambda e, i=i: e.activation(out=junk[:], in_=src_fn(i), func=AF.Square,
                                                    accum_out=ss[:, i:i + 1]),
                 reads=[r_src[i]], writes=[r_junk, r_ss], name="sq")
        S.op("dve", lambda e: e.tensor_scalar(out=rstd[:, 0:ntiles], in0=ss[:, 0:ntiles], scalar1=1.0 / D,
                                              scalar2=EPS, op0=ALU.mult, op1=ALU.add),
             reads=[r_ss], writes=[r_rstd])
        S.op("act", lambda e: e.activation(out=rstd[:, 0:ntiles], in_=rstd[:, 0:ntiles], func=AF.Sqrt),
             reads=[r_rstd], writes=[r_rstd])
        S.op("dve", lambda e: e.reciprocal(out=rstd[:, 0:ntiles], in_=rstd[:, 0:ntiles]),
             reads=[r_rstd], writes=[r_rstd])

    def norm_tile(i, src_fn, r_src, gtile, dstT, dst_res):
        hbi = i % 2
        S.op("dve", lambda e, i=i, hbi=hbi: e.scalar_tensor_tensor(
            out=hb[hbi][:], in0=src_fn(i), scalar=rstd[:, i:i + 1], in1=gtile[:], op0=ALU.mult, op1=ALU.mult),
            reads=[r_src[i], r_rstd, r_gt], writes=[r_hb[hbi]])
        b = bank()

        def tr(e, hbi=hbi, b=b):
            last = None
            for k in range(8):
                last = e.transpose(out=psb[b][:, k * 128:(k + 1) * 128], in_=hb[hbi][:, k * 128:(k + 1) * 128],
                                   identity=ident)
            return last
        S.op("pe", tr, reads=[r_hb[hbi], r_const, r_const2], writes=[r_ps[b]])
        S.op("act", lambda e, i=i, b=b: e.activation(
            out=dstT[:, :, i * 128:(i + 1) * 128],
            in_=psb[b][:, 0:1024].rearrange("p (k t) -> p k t", k=8), func=AF.Copy),
            reads=[r_ps[b]], writes=[dst_res])

    def norm_to_T(src_fn, r_src, gtile, dstT, dst_res, ntiles, tok_stride):
        norm_stats(src_fn, r_src, ntiles)
        for i in range(ntiles):
            norm_tile(i, src_fn, r_src, gtile, dstT, dst_res)

    def load_x(g):
        for i in range(NT):
            S.op("sp", lambda e, i=i, g=g: e.dma_start(out=xn_v[:, i, :],
                                                      in_=x_d[g * T + i * 128: g * T + (i + 1) * 128, :]),
                 writes=[r_xn[i], r_merged[2 * i], r_merged[2 * i + 1]], dma=d_x[i], name="xload")

    def copy_x():
        for i in range(NT):
            S.op("act", lambda e, i=i: e.activation(out=xres[:, i, :], in_=xn_v[:, i, :], func=AF.Copy),
                 reads=[r_xn[i]], writes=[r_x[i]])

    out_ops = []

    def do_group(g):
        seq = g // GPS
        gi = g % GPS
        tok0 = g * T
        first = (gi == 0)

        if first:
            S.op("dve", lambda e: e.memset(W32[:], 0.0), writes=[r_W32])
            S.op("dve", lambda e: e.memset(Rbf[:], 0.0), writes=[r_Rbf])
            S.op("dve", lambda e: e.memset(carry[:], 0.0), writes=[r_carry])
            S.op("sp", lambda e: e.dma_start(out=yst[1][:], in_=gvec_d[3, :].partition_broadcast(128)),
                 writes=[r_yst[1]], dma=d_yst[1])
            for mi, yb in ((0, 0), (1, 2)):
                S.op("sp", lambda e, mi=mi, yb=yb: e.dma_start(
                    out=yst[yb][:], in_=mem_d[seq * MEM + mi * 128: seq * MEM + (mi + 1) * 128, :]),
                    writes=[r_yst[yb]], dma=d_yst[yb])
                S.op("act", lambda e, mi=mi, yb=yb: e.activation(out=junk[:], in_=yst[yb][:], func=AF.Square,
                                                                 accum_out=ss[:, mi:mi + 1]),
                     reads=[r_yst[yb]], writes=[r_junk, r_ss])
            S.op("dve", lambda e: e.tensor_scalar(out=rstd[:, 0:2], in0=ss[:, 0:2], scalar1=1.0 / D, scalar2=EPS,
                                                  op0=ALU.mult, op1=ALU.add), reads=[r_ss], writes=[r_rstd])
            S.op("act", lambda e: e.activation(out=rstd[:, 0:2], in_=rstd[:, 0:2], func=AF.Sqrt),
                 reads=[r_rstd], writes=[r_rstd])
            S.op("dve", lambda e: e.reciprocal(out=rstd[:, 0:2], in_=rstd[:, 0:2]), reads=[r_rstd], writes=[r_rstd])
            for mi, yb in ((0, 0), (1, 2)):
                S.op("dve", lambda e, mi=mi, yb=yb: e.scalar_tensor_tensor(
                    out=hb[mi][:], in0=yst[yb][:], scalar=rstd[:, mi:mi + 1], in1=yst[1][:], op0=ALU.mult,
                    op1=ALU.mult),
                    reads=[r_yst[yb], r_yst[1], r_rstd], writes=[r_hb[mi]])

        dump("hT", hT[:].rearrange("p a b -> p (a b)"), [r_hT])

        fence1 = r_gTk + [r_acc[0], r_acc[1], r_g1[0], r_g1[1]]
        for vu, which in ((0, "k"), (1, "q")):
            jv, slv, rsv = wtake("v%d" % vu)
            j, sl, rs = wtake(which)
            pend = None
            for i in range(NT):
                c = gi * NT + i
                rb = i % 2
                tcos = 0 if which == "q" else 2
                S.op("sp", lambda e, c=c, rb=rb, tcos=tcos: e.dma_start(
                    out=ropes[rb][:, 0:2, :, :].rearrange("p a b c -> p (a b c)"),
                    in_=ropet_d[c, :, tcos * 256:(tcos + 2) * 256]),
                    writes=[r_ropes[rb]], dma=d_ropes[rb])
                b = bank()
                S.op("pe", lambda e, i=i, b=b, sl=sl: mm_group(
                    e, psf[b][:], [(hT[:, kc, i * 128:(i + 1) * 128], k8(sl)[:, kc, :]) for kc in range(8)]),
                    reads=[rs, r_hT], writes=[r_ps[b]])
                bv = bank()
                S.op("pe", lambda e, i=i, bv=bv, slv=slv: mm_group(
                    e, psf[bv][:], [(hT[:, kc, i * 128:(i + 1) * 128], k8(slv)[:, kc, :]) for kc in range(8)]),
                    reads=[rsv, r_hT], writes=[r_ps[bv]])
                S.op("act", lambda e, i=i, bv=bv, vu=vu: e.activation(out=v_v[:, i, vu * 512:(vu + 1) * 512],
                                                                       in_=psf[bv][:], func=AF.Copy),
                     reads=[r_ps[bv]], writes=[r_v] + (fence1 if (vu == 0 and i == 0) else []))
                pv = psf[b][:, :].rearrange("p (h d) -> p h d", h=4)
                x1 = pv[:, :, 0:64]
                x2 = pv[:, :, 64:128]
                kb = i % 2
                rt = rot[0]

                def rotary(e, x1=x1, x2=x2, rb=rb, rt=rt):
                    e.tensor_tensor(out=rt[:, 0, :, :], in0=x1, in1=ropes[rb][:, 0, :, :], op=ALU.mult)
                    e.tensor_tensor(out=rt[:, 1, :, :], in0=x2, in1=ropes[rb][:, 1, :, :], op=ALU.mult)
                    e.tensor_tensor(out=rt[:, 2, :, :], in0=x1, in1=ropes[rb][:, 1, :, :], op=ALU.mult)
                    return e.tensor_tensor(out=rt[:, 3, :, :], in0=x2, in1=ropes[rb][:, 0, :, :], op=ALU.mult)
                S.op("dve", rotary, reads=[r_ps[b], r_ropes[rb]], writes=[r_rot[0]])
                if which == "k":
                    dst = ktok_v[:, i, :].rearrange("p (h d) -> p h d", h=4)
                    dres = r_ktok
                else:
                    dst = krot[kb][:]
                    dres = r_krot[kb]

                def rotary2(e, dst=dst, rt=rt):
                    e.tensor_tensor(out=dst[:, :, 0:64], in0=rt[:, 0, :, :], in1=rt[:, 1, :, :], op=ALU.subtract)
                    return e.tensor_tensor(out=dst[:, :, 64:128], in0=rt[:, 2, :, :], in1=rt[:, 3, :, :], op=ALU.add)
                S.op("dve", rotary2, reads=[r_rot[0]], writes=[dres])
                if which == "q" and pending and i < NT - 1:
                    pending[-1].part(i)
                src = ktok_v[:, i, :] if which == "k" else krot[kb][:].rearrange("p h d -> p (h d)")
                dT = kT_v if which == "k" else qT_v
                dTres = r_kT if which == "k" else r_qT

                def emit_tr(i=i, src=src, dres=dres, dT=dT, dTres=dTres):
                    b2 = bank()

                    def trq(e, b2=b2, src=src):
                        last = None
                        for h in range(4):
                            last = e.transpose(out=psb[b2][:, h * 128:(h + 1) * 128],
                                               in_=src[:, h * 128:(h + 1) * 128], identity=ident)
                        return last
                    S.op("pe", trq, reads=[dres, r_const, r_const2], writes=[r_ps[b2]])
                    S.op("act", lambda e, i=i, b2=b2, dT=dT: e.activation(
                        out=dT[:, :, i * 128:(i + 1) * 128],
                        in_=psb[b2][:, 0:512].rearrange("p (h t) -> p h t", h=4), func=AF.Copy),
                        reads=[r_ps[b2]], writes=[dTres])
                if pend is not None:
                    pend()
                pend = emit_tr
            pend()
            if which == "q" and pending:
                pending[-1].part(NT - 1)
            W.release(jv)
            W.release(j)
        if pending:
            pending.pop()
        dump("qT", qT_v.rearrange("p a b -> p (a b)"), [r_qT])
        dump("kT", kT_v.rearrange("p a b -> p (a b)"), [r_kT])
        dump("v", v_v.rearrange("p a b -> p (a b)"), [r_v])

        GC = [float(np.exp(float(T) * np.log1p(-2.0 ** (-5.0 - h)))) for h in range(4)]
        sbufs = [(ssb[0], r_ssb[0]), (ssb[1], r_ssb[1])] + [(pT[q_][:, :].rearrange("p (h t) -> p h t", h=4), r_pT[q_])
                                                           for q_ in range(4)]
        sb_free = list(range(6))
        S_blk = {}
        o_banks = {}

        def rec_scores(i):
            for jt in range(i + 1):
                bs = bank()

                def scores(e, i=i, jt=jt, bs=bs):
                    last = None
                    for h in range(4):
                        last = e.matmul(psf[bs][:, h * 128:(h + 1) * 128], lhsT=kT_v[:, h, jt * 128:(jt + 1) * 128],
                                        rhs=qT_v[:, h, i * 128:(i + 1) * 128], start=True, stop=True)
                    return last
                S.op("pe", scores, reads=[r_kT, r_qT], writes=[r_ps[bs]])
                bi = sb_free.pop(0)
                buf, rbuf = sbufs[bi]
                bufap = buf[:] if bi < 2 else buf
                if jt == i:
                    S.op("dve", lambda e, bs=bs, bufap=bufap: e.tensor_tensor(
                        out=bufap, in0=psf[bs][:, :].rearrange("p (h t) -> p h t", h=4), in1=mask4[:], op=ALU.mult),
                        reads=[r_ps[bs], r_const, r_const2], writes=[rbuf])
                else:
                    S.op("act", lambda e, bs=bs, bufap=bufap: e.activation(
                        out=bufap, in_=psf[bs][:, :].rearrange("p (h t) -> p h t", h=4), func=AF.Copy),
                        reads=[r_ps[bs]], writes=[rbuf])
                S_blk[(jt, i)] = (bi, bufap, rbuf)

        def rec_o(i):
            bo = [bank(), bank()]
            blks = [S_blk[(jt, i)] for jt in range(i + 1)]

            def omm(e, i=i, bo=bo, blks=blks):
                last = None
                for h in range(4):
                    o_ap = psf[bo[h // 2]][:, (h % 2) * 256:(h % 2 + 1) * 256]
                    for jt, (bi, bufap, rbuf) in enumerate(blks):
                        e.matmul(o_ap, lhsT=bufap[:, h, :], rhs=v_v[:, jt, h * 256:(h + 1) * 256], start=(jt == 0),
                                 stop=False)
                    last = e.matmul(o_ap, lhsT=qT_v[:, h, i * 128:(i + 1) * 128], rhs=Rbf[:, h, :], start=False,
                                    stop=True)
                return last
            S.op("pe", omm, reads=[rb for (_, _, rb) in blks] + [r_v, r_qT, r_Rbf],
                 writes=[r_ps[bo[0]], r_ps[bo[1]]])
            for (bi, _, _) in blks:
                sb_free.append(bi)
            o_banks[i] = bo

        def rec_gn(i):
            bo = o_banks[i]

            def bst(e, bo=bo):
                last = None
                for h in range(4):
                    last = e.bn_stats(out=bnst[:, h, :], in_=psf[bo[h // 2]][:, (h % 2) * 256:(h % 2 + 1) * 256])
                return last
            S.op("dve", bst, reads=[r_ps[bo[0]], r_ps[bo[1]]], writes=[r_bn])

            def bag(e):
                last = None
                for h in range(4):
                    last = e.bn_aggr(out=mv[:, h, :], in_=bnst[:, h, :])
                return last
            S.op("dve", bag, reads=[r_bn], writes=[r_mv])
            S.op("dve", lambda e: e.tensor_scalar(out=grs[:], in0=mv[:, :, 1], scalar1=EPS, scalar2=None,
                                                  op0=ALU.add), reads=[r_mv], writes=[r_grs])
            S.op("act", lambda e: e.activation(out=grs[:], in_=grs[:], func=AF.Sqrt), reads=[r_grs], writes=[r_grs])
            S.op("dve", lambda e: e.reciprocal(out=grs[:], in_=grs[:]), reads=[r_grs], writes=[r_grs])
            S.op("dve", lambda e: e.scalar_tensor_tensor(out=gnb[:], in0=mv[:, :, 0], scalar=-1.0, in1=grs[:],
                                                         op0=ALU.mult, op1=ALU.mult),
                 reads=[r_mv, r_grs], writes=[r_gnb])

            def onorm(e, i=i, bo=bo):
                last = None
                for h in range(4):
                    last = e.activation(out=on_v[:, i, h * 256:(h + 1) * 256],
                                        in_=psf[bo[h // 2]][:, (h % 2) * 256:(h % 2 + 1) * 256],
                                        func=AF.Identity, scale=grs[:, h:h + 1], bias=gnb[:, h:h + 1])
                return last
            S.op("act", onorm, reads=[r_ps[bo[0]], r_ps[bo[1]], r_grs, r_gnb], writes=[r_on])

        rec_scores(0)
        rec_scores(1)
        rec_scores(2)
        rec_o(0)
        rec_o(1)
        rec_gn(0)
        rec_o(2)
        rec_scores(3)
        rec_gn(1)
        rec_gn(2)
        rec_o(3)
        rec_gn(3)

        bd = [bank(), bank()]

        def dmm(e, bd=bd):
            last = None
            for h in range(4):
                d_ap = psf[bd[h // 2]][:, (h % 2) * 256:(h % 2 + 1) * 256]
                for jt in range(NT):
                    last = e.matmul(d_ap, lhsT=ktok_v[:, jt, h * 128:(h + 1) * 128],
                                    rhs=v_v[:, jt, h * 256:(h + 1) * 256], start=(jt == 0), stop=(jt == NT - 1))
            return last
        S.op("pe", dmm, reads=[r_ktok, r_v], writes=[r_ps[bd[0]], r_ps[bd[1]]])

        def wupd(e, bd=bd):
            last = None
            for h in range(4):
                d_ap = psf[bd[h // 2]][:, (h % 2) * 256:(h % 2 + 1) * 256]
                last = e.scalar_tensor_tensor(out=W32[:, h, :], in0=W32[:, h, :], scalar=GC[h], in1=d_ap,
                                              op0=ALU.mult, op1=ALU.add)
            return last
        S.op("dve", wupd, reads=[r_ps[bd[0]], r_ps[bd[1]]], writes=[r_W32])

        def rupd(e):
            last = None
            for h in range(4):
                last = e.activation(out=Rbf[:, h, :], in_=W32[:, h, :], func=AF.Copy, scale=GC[h])
            return last
        S.op("act", rupd, reads=[r_W32], writes=[r_Rbf])
        dump("on", on_v.rearrange("p a b -> p (a b)"), [r_on])

        for i in range(NT):
            b = bank()

            def tro(e, i=i, b=b):
                last = None
                for fc in range(8):
                    last = e.transpose(out=psb[b][:, fc * 128:(fc + 1) * 128], in_=on_v[:, i, fc * 128:(fc + 1) * 128],
                                       identity=ident)
                return last
            S.op("pe", tro, reads=[r_on, r_const, r_const2], writes=[r_ps[b]])

            def aff(e, i=i, b=b):
                last = None
                for fc in range(8):
                    last = e.activation(out=retT_v[:, fc, i * 128:(i + 1) * 128], in_=psb[b][:, fc * 128:(fc + 1) * 128],
                                        func=AF.Identity, scale=pp[:, GR_OFF + fc:GR_OFF + fc + 1],
                                        bias=pp[:, BR_OFF + fc:BR_OFF + fc + 1])
                return last
            S.op("act", aff, reads=[r_ps[b], r_const, r_const2], writes=[r_v])

        cnt = 0
        for u in range(2):
            j, sl, rs = wtake("gr%d" % u)
            for fl in range(4):
                fc = u * 4 + fl
                b = bank()
                ti = cnt % 2
                cnt += 1
                S.op("pe", lambda e, fl=fl, b=b, sl=sl: mm_group(
                    e, psf[b][:], [(k8(sl)[:, kc, fl * 128:(fl + 1) * 128], hT[:, kc, :]) for kc in range(8)]),
                    reads=[rs, r_hT], writes=[r_ps[b]])
                S.op("act", lambda e, b=b, ti=ti: e.activation(out=th[ti][:], in_=psf[b][:], func=AF.Tanh, scale=0.5),
                     reads=[r_ps[b]], writes=[r_th[ti]])
                S.op("dve", lambda e, b=b, ti=ti: e.scalar_tensor_tensor(
                    out=tt[ti][:], in0=th[ti][:], scalar=1.0, in1=psf[b][:], op0=ALU.add, op1=ALU.mult),
                    reads=[r_th[ti], r_ps[b]], writes=[r_tt[ti]])
                S.op("dve", lambda e, fc=fc, ti=ti: e.scalar_tensor_tensor(
                    out=retT_v[:, fc, :], in0=tt[ti][:], scalar=0.5, in1=retT_v[:, fc, :], op0=ALU.mult, op1=ALU.mult),
                    reads=[r_tt[ti], r_v], writes=[r_v])
            W.release(j)
        dump("retT", retT_v.rearrange("p a b -> p (a b)"), [r_v])

        def branch_merge(jl_range, j0, y_pairs_fn, y_reads, gate_sl, gate_rs, mode):
            nonlocal cnt
            for jl in jl_range:
                jd = j0 + jl
                by = bank()
                S.op("pe", lambda e, jl=jl, by=by: mm_group(e, psf[by][:], y_pairs_fn(jl)),
                     reads=y_reads, writes=[r_ps[by]])
                bg = bank()
                S.op("pe", lambda e, jl=jl, bg=bg: mm_group(
                    e, psf[bg][:], [(k8(gate_sl)[:, kc, jl * 128:(jl + 1) * 128], hT[:, kc, :]) for kc in range(8)]),
                    reads=[gate_rs, r_hT], writes=[r_ps[bg]])
                ti = cnt % 2
                cnt += 1
                S.op("act", lambda e, bg=bg, ti=ti: e.activation(out=th[ti][:], in_=psf[bg][:], func=AF.Tanh,
                                                                 scale=0.5),
                     reads=[r_ps[bg]], writes=[r_th[ti]])
                if mode == "set":
                    S.op("dve", lambda e, by=by, ti=ti, jd=jd: e.scalar_tensor_tensor(
                        out=mergedT[:, jd, :], in0=th[ti][:], scalar=1.0, in1=psf[by][:], op0=ALU.add, op1=ALU.mult),
                        reads=[r_th[ti], r_ps[by]], writes=[r_merged[jd], r_xn[jd // 2]])
                else:
                    S.op("dve", lambda e, by=by, ti=ti: e.scalar_tensor_tensor(
                        out=tt[ti][:], in0=th[ti][:], scalar=1.0, in1=psf[by][:], op0=ALU.add, op1=ALU.mult),
                        reads=[r_th[ti], r_ps[by]], writes=[r_tt[ti]])
                    if mode == "add":
                        S.op("dve", lambda e, ti=ti, jd=jd: e.tensor_tensor(
                            out=mergedT[:, jd, :], in0=mergedT[:, jd, :], in1=tt[ti][:], op=ALU.add),
                            reads=[r_tt[ti], r_merged[jd]], writes=[r_merged[jd]])
                    else:
                        S.op("dve", lambda e, ti=ti, jd=jd: e.tensor_tensor(
                            out=mbf_v[:, jd, :], in0=mergedT[:, jd, :], in1=tt[ti][:], op=ALU.add),
                            reads=[r_tt[ti], r_merged[jd]], writes=[r_mbf, r_pooledT, r_ypT])

        if first:
            for mi in range(2):
                b = bank()

                def trm(e, b=b, mi=mi):
                    last = None
                    for k in range(8):
                        last = e.transpose(out=psb[b][:, k * 128:(k + 1) * 128], in_=hb[mi][:, k * 128:(k + 1) * 128],
                                           identity=ident)
                    return last
                S.op("pe", trm, reads=[r_hb[mi], r_const, r_const2], writes=[r_ps[b]])
                S.op("act", lambda e, mi=mi, b=b: e.activation(
                    out=memT[:, :, mi * 128:(mi + 1) * 128],
                    in_=psb[b][:, 0:1024].rearrange("p (k t) -> p k t", k=8), func=AF.Copy),
                    reads=[r_ps[b]], writes=[r_memT])
            j, sl, rs = wtake("kvk")
            for h in range(4):
                b = bank()
                S.op("pe", lambda e, h=h, b=b, sl=sl: mm_group(
                    e, psf[b][:, 0:MEM], [(k8(sl)[:, kc, h * 128:(h + 1) * 128], memT[:, kc, :]) for kc in range(8)]),
                    reads=[rs, r_memT], writes=[r_ps[b]])
                S.op("act", lambda e, h=h, b=b: e.activation(out=kmT[:, h, :], in_=psf[b][:, 0:MEM], func=AF.Copy),
                     reads=[r_ps[b]], writes=[r_kmT])
            W.release(j)
            j, sl, rs = wtake("kvv")
            for mc in range(2):
                b = bank()
                S.op("pe", lambda e, mc=mc, b=b, sl=sl: mm_group(
                    e, psf[b][:], [(memT[:, kc, mc * 128:(mc + 1) * 128], k8(sl)[:, kc, :]) for kc in range(8)]),
                    reads=[rs, r_memT], writes=[r_ps[b]])
                S.op("act", lambda e, mc=mc, b=b: e.activation(out=vm[:, mc, :], in_=psf[b][:], func=AF.Copy),
                     reads=[r_ps[b]], writes=[r_vm])
            W.release(j)

        jh, slh, rsh = wtake("hp")
        jq, slq, rsq = wtake("qx")
        for i in range(NT):
            b = bank()
            S.op("pe", lambda e, i=i, b=b, slh=slh: mm_group(
                e, psf[b][:], [(hT[:, kc, i * 128:(i + 1) * 128], k8(slh)[:, kc, :]) for kc in range(8)]),
                reads=[rsh, r_hT], writes=[r_ps[b]])
            S.op("act", lambda e, i=i, b=b: e.activation(out=hp_tok[:, i + 1, :], in_=psf[b][:], func=AF.Copy),
                 reads=[r_ps[b]], writes=[r_hp[i + 1]])
            h = i
            b = bank()
            S.op("pe", lambda e, h=h, b=b, slq=slq: mm_group(
                e, psf[b][:], [(k8(slq)[:, kc, h * 128:(h + 1) * 128], hT[:, kc, :]) for kc in range(8)]),
                reads=[rsq, r_hT], writes=[r_ps[b]])
            S.op("act", lambda e, h=h, b=b: e.activation(out=qxT[:, h, :], in_=psf[b][:], func=AF.Copy),
                 reads=[r_ps[b]], writes=[r_qxT])
        W.release(jh)
        W.release(jq)
        pcnt = 0
        for step in range(4):
            gq = step
            b = bank()

            def poolmm(e, gq=gq, b=b):
                last = None
                for i in range(NT):
                    o_ap = psf[b][:, i * 128:(i + 1) * 128]
                    cur = hp_tok[:, i + 1, gq * 128:(gq + 1) * 128]
                    if first and i == 0:
                        last = e.matmul(o_ap, lhsT=cur, rhs=cst[:, 8 + gq, :], start=True, stop=True)
                    else:
                        e.matmul(o_ap, lhsT=cur, rhs=cst[:, gq, :], start=True, stop=False)
                        last = e.matmul(o_ap, lhsT=hp_tok[:, i, gq * 128:(gq + 1) * 128], rhs=cst[:, 4 + gq, :],
                                        start=False, stop=True)
                return last
            S.op("pe", poolmm, reads=r_hp + [r_const, r_const2], writes=[r_ps[b]])
            S.op("act", lambda e, gq=gq, b=b: e.activation(out=pooledT_v[:, gq, :], in_=psf[b][:], func=AF.Copy),
                 reads=[r_ps[b]], writes=[r_pooledT] + ([r_mbf] if gq == 0 else []))
            h = step
            pis = []
            for mc in range(2):
                bsx = bank()
                pi = pcnt % 4
                pcnt += 1
                pis.append(pi)
                S.op("pe", lambda e, h=h, mc=mc, bsx=bsx: e.matmul(
                    psf[bsx][:], lhsT=kmT[:, h, mc * 128:(mc + 1) * 128], rhs=qxT[:, h, :], start=True, stop=True),
                    reads=[r_kmT, r_qxT], writes=[r_ps[bsx]])
                S.op("act", lambda e, bsx=bsx, pi=pi: e.activation(out=pT[pi][:], in_=psf[bsx][:], func=AF.Exp,
                                                                   scale=float(128.0 ** -0.5)),
                     reads=[r_ps[bsx]], writes=[r_pT[pi]])
            b2 = bank()
            S.op("pe", lambda e, gq=gq, b2=b2: e.matmul(psf[b2][:], lhsT=wpool[:, gq, :], rhs=pooledT_v[:, gq, :],
                                                        start=True, stop=True),
                 reads=[r_pooledT, r_const, r_const2], writes=[r_ps[b2]])
            S.op("act", lambda e, gq=gq, b2=b2: e.activation(out=ypT_v[:, gq, :], in_=psf[b2][:], func=AF.Copy,
                                                             scale=pp[:, PS_OFF + gq:PS_OFF + gq + 1]),
                 reads=[r_ps[b2], r_const, r_const2], writes=[r_ypT])
            bo_ = bank()
            S.op("pe", lambda e, h=h, bo_=bo_, pis=tuple(pis): mm_group(
                e, psf[bo_][:], [(vm[:, mc, h * 128:(h + 1) * 128], pT[pis[mc]][:]) for mc in range(2)]),
                reads=[r_vm, r_pT[pis[0]], r_pT[pis[1]]], writes=[r_ps[bo_]])
            bden = bank()
            S.op("pe", lambda e, bden=bden, pis=tuple(pis): mm_group(
                e, psf[bden][:], [(ones, pT[pis[mc]][:]) for mc in range(2)]),
                reads=[r_const, r_const2, r_pT[pis[0]], r_pT[pis[1]]], writes=[r_ps[bden]])
            S.op("dve", lambda e, bden=bden: e.reciprocal(out=rden[:], in_=psf[bden][:]),
                 reads=[r_ps[bden]], writes=[r_rden])
            S.op("dve", lambda e, h=h, bo_=bo_: e.tensor_tensor(out=oT[:, h, :], in0=psf[bo_][:], in1=rden[:],
                                                               op=ALU.mult),
                 reads=[r_ps[bo_], r_rden], writes=[r_oT])
        S.op("act", lambda e: e.activation(out=hp_tok[:, 0, :], in_=hp_tok[:, NT, :], func=AF.Copy),
             reads=[r_hp[NT]], writes=[r_hp[0]])
        dump("pooledT", pooledT_v.rearrange("p a b -> p (a b)"), [r_pooledT])
        dump("oT", oT[:].rearrange("p a b -> p (a b)"), [r_oT])

        ja, sla, rsa = wtake("a")
        for u in range(2):
            jg, slg, rsg = wtake("gp%d" % u)
            branch_merge(range(4), u * 4,
                         lambda jl, u=u, sla=sla: [(k4(sla)[:, q4, (u * 4 + jl) * 128:(u * 4 + jl + 1) * 128],
                                                   ypT_v[:, q4, :]) for q4 in range(4)],
                         [rsa, r_ypT], slg, rsg, "set")
            W.release(jg)
        W.release(ja)
        for u in range(2):
            jr, slr, rsr = wtake("r%d" % u)
            jg, slg, rsg = wtake("gret%d" % u)
            branch_merge(range(4), u * 4,
                         lambda jl, slr=slr: [(k8(slr)[:, kc, jl * 128:(jl + 1) * 128], retT_v[:, kc, :])
                                              for kc in range(8)],
                         [rsr, r_v], slg, rsg, "add")
            W.release(jr)
            W.release(jg)
        jc, slc, rsc = wtake("c")
        for u in range(2):
            jg, slg, rsg = wtake("gm%d" % u)
            branch_merge(range(4), u * 4,
                         lambda jl, u=u, slc=slc: [(k4(slc)[:, q4, (u * 4 + jl) * 128:(u * 4 + jl + 1) * 128],
                                                   oT[:, q4, :]) for q4 in range(4)],
                         [rsc, r_oT], slg, rsg, "final")
            W.release(jg)
        W.release(jc)
        dump("mbf", mbf_v.rearrange("p a b -> p (a b)"), [r_mbf])

        jo0, slo0, rso0 = wtake("out0")
        jo1, slo1, rso1 = wtake("out1")
        for i in range(NT):
            for c2, slo, rso in ((0, slo0, rso0), (1, slo1, rso1)):
                b = bank()
                S.op("pe", lambda e, i=i, b=b, slo=slo: mm_group(
                    e, psf[b][:], [(mbf_v[:, kc, i * 128:(i + 1) * 128], k8(slo)[:, kc, :]) for kc in range(8)]),
                    reads=[rso, r_mbf], writes=[r_ps[b]])
                S.op("dve", lambda e, i=i, b=b, c2=c2: e.scalar_tensor_tensor(
                    out=xres[:, i, c2 * 512:(c2 + 1) * 512], in0=psf[b][:], scalar=0.5,
                    in1=xres[:, i, c2 * 512:(c2 + 1) * 512], op0=ALU.mult, op1=ALU.add),
                    reads=[r_ps[b], r_x[i]], writes=[r_x[i]])
        W.release(jo0)
        W.release(jo1)
        dump("x2", xres[:].rearrange("p a b -> p (a b)"), r_x)
        if g + 1 < NG:
            load_x(g + 1)

        norm_to_T(lambda i: xres[:, i, :], r_x, gt_ffn, hT, r_hT, NT, 128)
        fence2 = [r_v, r_ktok, r_kT, r_qT, r_on]
        acnt = 0
        if not first:
            S.op("dve", lambda e: e.tensor_tensor(out=btmp[:, 0, :], in0=carry[:, :, 1], in1=pp[:, CW1:CW1 + NF],
                                                  op=ALU.mult), reads=[r_carry, r_const, r_const2], writes=[r_btmp])
            S.op("dve", lambda e: e.tensor_tensor(out=btmp[:, 1, :], in0=carry[:, :, 0], in1=pp[:, CW0:CW0 + NF],
                                                  op=ALU.mult), reads=[r_carry, r_const, r_const2], writes=[r_btmp])
            S.op("dve", lambda e: e.tensor_tensor(out=bnd[:, :, 1], in0=carry[:, :, 1], in1=pp[:, CW0:CW0 + NF],
                                                  op=ALU.mult), reads=[r_carry, r_const, r_const2], writes=[r_bnd])
            S.op("dve", lambda e: e.tensor_tensor(out=bnd[:, :, 0], in0=btmp[:, 0, :], in1=btmp[:, 1, :],
                                                  op=ALU.add), reads=[r_btmp, r_bnd], writes=[r_bnd])
        for u in range(11):
            j, sl, rs = wtake("up%d" % u)
            for fl in range(2):
                f = 2 * u + fl
                ai = acnt % 2
                acnt += 1
                ba = bank()
                S.op("pe", lambda e, fl=fl, ba=ba, sl=sl: mm_group(
                    e, psf[ba][:], [(k8(sl)[:, kc, fl * 128:(fl + 1) * 128], hT[:, kc, :]) for kc in range(8)]),
                    reads=[rs, r_hT], writes=[r_ps[ba]])
                bb = bank()
                S.op("pe", lambda e, fl=fl, bb=bb, sl=sl: mm_group(
                    e, psf[bb][:], [(k8(sl)[:, kc, 256 + fl * 128:256 + (fl + 1) * 128], hT[:, kc, :])
                                    for kc in range(8)]),
                    reads=[rs, r_hT], writes=[r_ps[bb]])
                S.op("act", lambda e, f=f, ba=ba, ai=ai: e.activation(
                    out=acc_v[ai], in_=psf[ba][:], func=AF.Identity, scale=pp[:, CW2 + f:CW2 + f + 1],
                    bias=pp[:, CB + f:CB + f + 1]),
                    reads=[r_ps[ba], r_const, r_const2], writes=[r_acc[ai]] + (fence2 + r_gTk if (u == 0 and fl == 0) else []))

                S.op("dve", lambda e, f=f, ba=ba, ai=ai: e.scalar_tensor_tensor(
                    out=acc_v[ai][:, 1:T], in0=psf[ba][:, 0:T - 1], scalar=pp[:, CW1 + f:CW1 + f + 1],
                    in1=acc_v[ai][:, 1:T], op0=ALU.mult, op1=ALU.add),
                    reads=[r_ps[ba], r_acc[ai], r_const, r_const2], writes=[r_acc[ai]])
                S.op("dve", lambda e, f=f, ba=ba, ai=ai: e.scalar_tensor_tensor(
                    out=acc_v[ai][:, 2:T], in0=psf[ba][:, 0:T - 2], scalar=pp[:, CW0 + f:CW0 + f + 1],
                    in1=acc_v[ai][:, 2:T], op0=ALU.mult, op1=ALU.add),
                    reads=[r_ps[ba], r_acc[ai], r_const, r_const2], writes=[r_acc[ai]])
                if not first:
                    S.op("dve", lambda e, f=f, ai=ai: e.tensor_tensor(
                        out=acc_v[ai][:, 0:2], in0=acc_v[ai][:, 0:2], in1=bnd[:, f, :], op=ALU.add),
                        reads=[r_acc[ai], r_bnd], writes=[r_acc[ai]])
                S.op("act", lambda e, f=f, ba=ba: e.activation(out=carry[:, f, :], in_=psf[ba][:, T - 2:T],
                                                               func=AF.Copy),
                     reads=[r_ps[ba]], writes=[r_carry])
                S.op("act", lambda e, ai=ai: e.activation(out=g1_v[ai], in_=acc_v[ai], func=AF.Gelu_apprx_tanh),
                     reads=[r_acc[ai]], writes=[r_g1[ai]])
                S.op("dve", lambda e, f=f, bb=bb, ai=ai: e.tensor_tensor(out=gT_v[:, f, :], in0=g1_v[ai],
                                                                         in1=psf[bb][:], op=ALU.mult),
                     reads=[r_g1[ai], r_ps[bb]], writes=[r_gTk[f // 8]])
            W.release(j)
        dump("gT", gT_v.rearrange("p a b -> p (a b)"), r_gTk)

        nxt = (g + 1 < NG)
        if nxt:
            norm_stats(lambda i: xn_v[:, i, :], r_xn, NT)

        for c2 in range(2):
            bks = [bank() for _ in range(NT)]
            for k in range(3):
                j, sl, rs = wtake("dn%d_%d" % (c2, k))
                nf = min(8, NF - 8 * k)

                def down(e, k=k, nf=nf, sl=sl, bks=bks):
                    last = None
                    for i in range(NT):
                        for fl in range(nf):
                            f = 8 * k + fl
                            last = e.matmul(psf[bks[i]][:], lhsT=gT_v[:, f, i * 128:(i + 1) * 128],
                                            rhs=k8(sl)[:, fl, :], start=(f == 0), stop=(f == NF - 1))
                    return last
                S.op("pe", down, reads=[rs, r_gTk[k]], writes=[r_ps[bk] for bk in bks])
                W.release(j)
                if nxt and c2 == 1 and k == 0:
                    for i in range(NT):
                        norm_tile(i, lambda i: xn_v[:, i, :], r_xn, gt_mix, hT, r_hT)
            for i in range(NT):
                S.op("dve", lambda e, i=i, c2=c2, bks=bks: e.tensor_tensor(
                    out=xres[:, i, c2 * 512:(c2 + 1) * 512], in0=psf[bks[i]][:],
                    in1=xres[:, i, c2 * 512:(c2 + 1) * 512], op=ALU.add),
                    reads=[r_ps[bks[i]], r_x[i]], writes=[r_x[i]])

        for i in range(NT):
            S.op("act", lambda e, i=i: e.activation(out=junk[:], in_=xres[:, i, :], func=AF.Square,
                                                    accum_out=ssf[:, i:i + 1]),
                 reads=[r_x[i]], writes=[r_junk, r_ssf])
        S.op("dve", lambda e: e.tensor_scalar(out=rstdf[:], in0=ssf[:], scalar1=1.0 / D, scalar2=EPS, op0=ALU.mult,
                                              op1=ALU.add), reads=[r_ssf], writes=[r_rstdf])
        S.op("act", lambda e: e.activation(out=rstdf[:], in_=rstdf[:], func=AF.Sqrt), reads=[r_rstdf],
             writes=[r_rstdf])
        S.op("dve", lambda e: e.reciprocal(out=rstdf[:], in_=rstdf[:]), reads=[r_rstdf], writes=[r_rstdf])

        def tail_part(i, tok0=tok0, nxt=nxt):
            yi = i % 4
            S.op("dve", lambda e, i=i, yi=yi: e.scalar_tensor_tensor(
                out=yst[yi][:], in0=xres[:, i, :], scalar=rstdf[:, i:i + 1], in1=gt_fin[:], op0=ALU.mult,
                op1=ALU.mult),
                reads=[r_x[i], r_rstdf, r_gt], writes=[r_yst[yi]])
            o = S.op("act", lambda e, i=i, yi=yi, tok0=tok0: e.dma_start(
                out=y_d[tok0 + i * 128: tok0 + (i + 1) * 128, :], in_=yst[yi][:]),
                reads=[r_yst[yi]], dma=d_yst[yi], name="ystore")
            out_ops.append(o)
            if nxt:
                S.op("act", lambda e, i=i: e.activation(out=xres[:, i, :], in_=xn_v[:, i, :], func=AF.Copy),
                     reads=[r_xn[i]], writes=[r_x[i]])

        def tail(tail_part=tail_part):
            for i in range(NT):
                tail_part(i)
        tail.part = tail_part
        pending.append(tail)

    pending = []
    load_x(0)
    norm_to_T(lambda i: xn_v[:, i, :], r_xn, gt_mix, hT, r_hT, NT, 128)
    copy_x()
    for g in range(NG):
        do_group(g)
    pending.pop()()

    fin = S.op("sp", lambda e: None, name="final_wait")
    fin.deps = list(out_ops[-4:]) + list(d_dbg.ops)
    for o in fin.deps:
        o.signal = True
    with nc.Block() as block:
        S.emit(block)
    return nc


def make_consts():
    f32 = np.float32
    half = 64
    inv = (np.float32(10000.0) ** (-np.arange(half, dtype=f32) / np.float32(half))).astype(f32)
    pos = np.arange(SEQ, dtype=f32)
    ang = (pos[:, None] * inv[None, :]).astype(f32).astype(np.float64)
    cos = np.cos(ang)
    sin = np.sin(ang)
    lg = np.log1p(-np.exp2(-5.0 - np.arange(4, dtype=np.float64)))
    p1 = ((np.arange(16)[:, None] % NT) * 128 + np.arange(1, 129)[None, :]).astype(np.float64)
    qd = np.exp(p1[:, :, None] * lg[None, None, :])
    kd = np.exp(-p1[:, :, None] * lg[None, None, :]) * (128.0 ** -0.5)
    ropet = np.zeros((16, 128, 4, 4, 64), np.float64)
    cosr = cos.reshape(16, 128, 1, 64)
    sinr = sin.reshape(16, 128, 1, 64)
    ropet[:, :, 0] = cosr * qd[:, :, :, None]
    ropet[:, :, 1] = sinr * qd[:, :, :, None]
    ropet[:, :, 2] = cosr * kd[:, :, :, None]
    ropet[:, :, 3] = sinr * kd[:, :, :, None]
    ropet = ropet.reshape(16, 128, 1024).astype(f32)
    kk = np.arange(128)[:, None]
    qq = np.arange(128)[None, :]
    m = (kk <= qq).astype(f32)
    mask4 = np.repeat(m[:, None, :], 4, axis=1).reshape(128, 512).astype(f32)
    cst = np.zeros((128, 14, 128), np.float64)
    tp = np.arange(128)[:, None]
    t = np.arange(128)[None, :]
    for gq, w in enumerate((2, 4, 8, 16)):
        cst[:, gq, :] = ((tp <= t) & (tp > t - w)) / w - (tp == t)
        cst[:, 4 + gq, :] = ((tp - 128) > (t - w)) / w
        cst[:, 8 + gq, :] = ((tp <= t) & (tp > t - w)) / np.minimum(t + 1, w) - (tp == t)
    cst[:, 12, :] = np.eye(128)
    cst[:, 13, :] = 1.0
    return ropet, mask4, cst.astype(f32)


_PROGRAM = {}


def kernel(x, mem, g_mix, w_in, w_pool, pool_scale, w_a, g_ret, b_ret, w_r, g_mem, w_mem_kv, w_c, w_out,
           g_ffn, w_up, conv_w, conv_b, w_down, g_final, _dbg=None):
    f32 = np.float32
    x = np.asarray(x, f32)
    mem = np.asarray(mem, f32)
    ropet, mask4, cst = make_consts()
    gvec = np.ascontiguousarray(np.stack([np.asarray(g_mix, f32)[0], np.asarray(g_ffn, f32)[0],
                                          np.asarray(g_final, f32), np.asarray(g_mem, f32)[0]]))
    cw = np.asarray(conv_w, f32)[0]
    pp = np.concatenate([
        np.asarray(pool_scale, f32)[0].reshape(4, 128).T,
        np.asarray(g_ret, f32)[0].reshape(8, 128).T,
        np.asarray(b_ret, f32)[0].reshape(8, 128).T,
        cw[0].reshape(NF, 128).T, cw[1].reshape(NF, 128).T, cw[2].reshape(NF, 128).T,
        np.asarray(conv_b, f32)[0].reshape(NF, 128).T], axis=1)
    pp = np.ascontiguousarray(pp, dtype=f32)
    shared = {
        "w_in": np.ascontiguousarray(np.asarray(w_in, f32)[0]),
        "w_pool": np.ascontiguousarray(np.asarray(w_pool, f32)[0]),
        "w_a": np.ascontiguousarray(np.asarray(w_a, f32)[0]),
        "w_r": np.ascontiguousarray(np.asarray(w_r, f32)[0]),
        "w_mem_kv": np.ascontiguousarray(np.asarray(w_mem_kv, f32)[0]),
        "w_c": np.ascontiguousarray(np.asarray(w_c, f32)[0]),
        "w_out": np.ascontiguousarray(np.asarray(w_out, f32)[0]),
        "w_up": np.ascontiguousarray(np.asarray(w_up, f32)[0]),
        "w_down": np.ascontiguousarray(np.asarray(w_down, f32)[0]),
        "gvec": gvec, "pp": pp, "ropet": ropet, "mask4": mask4, "cst": cst,
    }
    in_maps = []
    for c in range(NCORES):
        m = dict(shared)
        m["x"] = np.ascontiguousarray(x[2 * c:2 * c + 2].reshape(2 * SEQ, D))
        m["mem"] = np.ascontiguousarray(mem[2 * c:2 * c + 2].reshape(2 * MEM, D))
        in_maps.append(m)
    key = tuple(sorted(_dbg.items())) if _dbg else None
    if key not in _PROGRAM:
        _PROGRAM[key] = build_program(_dbg)
    nc = _PROGRAM[key]
    res = run_bass_kernel_spmd(nc, in_maps, core_ids=list(range(NCORES)))
    out = np.concatenate([np.asarray(r["y"], f32).reshape(2, SEQ, D) for r in res.results], axis=0)
    if _dbg:
        return out, res.results
    return out
```

```python
import numpy as np
import concourse.bass as bass
import concourse.mybir as mybir
from concourse.bass_utils import run_bass_kernel_spmd

F32 = mybir.dt.float32
BF16 = mybir.dt.bfloat16
AF = mybir.ActivationFunctionType
ALU = mybir.AluOpType

NCORES = 8
D = 1024
SEQ = 2048
T = 512
NT = T // 128
NG = 2 * SEQ // T
GPS = SEQ // T
MEM = 256
FH = 2816
NF = FH // 128
EPS = 1e-6
NS = 4


class Res:
    __slots__ = ("name", "writer", "readers")

    def __init__(self, name):
        self.name = name
        self.writer = None
        self.readers = []


class Op:
    __slots__ = ("eng", "fn", "deps", "signal", "sem", "val", "inc", "name")


class DmaSem:
    def __init__(self, nc, name):
        self.sem = nc.alloc_semaphore(name)
        self.count = 0
        self.ops = []


class Sched:
    ENGS = ("pe", "act", "dve", "pool", "sp")

    def __init__(self, nc):
        self.nc = nc
        self.ops = {e: [] for e in self.ENGS}
        self.esem = {e: nc.alloc_semaphore("es_" + e) for e in ("pe", "act", "dve", "pool")}

    def op(self, eng, fn, reads=(), writes=(), dma=None, name=None, nodep=()):
        o = Op()
        o.eng = eng
        o.fn = fn
        o.name = name
        o.signal = False
        o.sem = None
        o.val = None
        o.inc = 1
        deps = []
        for r in reads:
            if r.writer is not None:
                deps.append(r.writer)
        for w in writes:
            if w.writer is not None:
                deps.append(w.writer)
            deps.extend(w.readers)
        seen = set()
        fd = []
        for d in deps:
            if id(d) in seen or d is o or d in nodep:
                continue
            seen.add(id(d))
            if d.eng == "pe" and eng == "pe":
                continue
            fd.append(d)
        o.deps = fd
        for d in fd:
            d.signal = True
        if dma is not None:
            dma.count += 16
            o.sem = dma.sem
            o.val = dma.count
            o.inc = 16
            o.signal = True
            dma.ops.append(o)
        for r in reads:
            r.readers.append(o)
        for w in writes:
            w.writer = o
            w.readers = []
        self.ops[eng].append(o)
        return o

    def finalize(self):
        for e in ("pe", "act", "dve", "pool"):
            c = 0
            for o in self.ops[e]:
                if o.sem is None and o.signal:
                    c += 1
                    o.sem = self.esem[e]
                    o.val = c
                    o.inc = 1
        for e in self.ENGS:
            for o in self.ops[e]:
                if o.signal and o.sem is None:
                    raise RuntimeError("signal op without sem: %s" % o.name)

    def emit(self, block):
        self.finalize()
        sched = self

        def run(eng_name, eng):
            waited = {}
            for o in sched.ops[eng_name]:
                need = {}
                for d in o.deps:
                    k = id(d.sem)
                    if k not in need or need[k][1] < d.val:
                        need[k] = (d.sem, d.val)
                for k, (sem, val) in need.items():
                    if waited.get(k, 0) >= val:
                        continue
                    eng.wait_ge(sem, val)
                    waited[k] = val
                last = o.fn(eng)
                if o.signal:
                    if last is None:
                        raise RuntimeError("op %s returned no instruction" % o.name)
                    last.then_inc(o.sem, o.inc)

        @block.tensor
        def _(eng):
            run("pe", eng)

        @block.scalar
        def _(eng):
            run("act", eng)

        @block.vector
        def _(eng):
            run("dve", eng)

        @block.gpsimd
        def _(eng):
            run("pool", eng)

        @block.sync
        def _(eng):
            run("sp", eng)


def build_program(dbg=None):
    nc = bass.Bass("TRN2", target_bir_lowering=False)

    def din(name, shape):
        return nc.dram_tensor(name, list(shape), F32, kind="ExternalInput").ap()

    x_d = din("x", [2 * SEQ, D])
    mem_d = din("mem", [2 * MEM, D])
    w_in_d = din("w_in", [D, 7168])
    w_pool_d = din("w_pool", [4, 128, 128])
    w_a_d = din("w_a", [512, D])
    w_r_d = din("w_r", [D, D])
    w_kv_d = din("w_mem_kv", [D, D])
    w_c_d = din("w_c", [512, D])
    w_out_d = din("w_out", [D, D])
    w_up_d = din("w_up", [D, 2 * FH])
    w_down_d = din("w_down", [FH, D])
    gvec_d = din("gvec", [4, D])
    pp_d = din("pp", [128, 108])
    ropet_d = din("ropet", [16, 128, 1024])
    mask_d = din("mask4", [128, 512])
    cst_d = din("cst", [128, 14, 128])
    y_d = nc.dram_tensor("y", [2 * SEQ, D], F32, kind="ExternalOutput").ap()
    dbg_d = {}
    if dbg:
        for nm, shp in dbg.items():
            dbg_d[nm] = nc.dram_tensor("dbg_" + nm, list(shp), F32, kind="ExternalOutput").ap()

    S = Sched(nc)

    def sb(name, shape, dt):
        return nc.alloc_sbuf_tensor("s_" + name, list(shape), dt)

    xres = sb("xres", [128, NT, D], F32)
    hT = sb("hT", [128, 8, T], BF16)
    hb = [sb("hb%d" % i, [128, D], BF16) for i in range(2)]
    junk = sb("junk", [128, D], BF16)
    ss = sb("ss", [128, NT], F32)
    rstd = sb("rstd", [128, NT], F32)
    ssf = sb("ssf", [128, NT], F32)
    rstdf = sb("rstdf", [128, NT], F32)
    r1f = sb("r1f", [128, 7680], F32)
    r1b = r1f.bitcast(BF16)
    v_v = r1b[:, 0:4096].rearrange("p (a b) -> p a b", a=NT)
    retT_v = r1b[:, 0:4096].rearrange("p (a b) -> p a b", a=8)
    ktok_v = r1b[:, 4096:6144].rearrange("p (a b) -> p a b", a=NT)
    kT_v = r1b[:, 6144:8192].rearrange("p (a b) -> p a b", a=4)
    qT_v = r1b[:, 8192:10240].rearrange("p (a b) -> p a b", a=4)
    on_v = r1b[:, 10240:14336].rearrange("p (a b) -> p a b", a=NT)
    gT_v = r1b[:, 0:NF * T].rearrange("p (a b) -> p a b", a=NF)
    acc_v = [r1f[:, 5632 + i * 512: 5632 + (i + 1) * 512] for i in range(2)]
    g1_v = [r1f[:, 6656 + i * 512: 6656 + (i + 1) * 512] for i in range(2)]
    mergedF = sb("mergedT", [128, 8 * T], F32)
    mergedT = mergedF[:, :].rearrange("p (a b) -> p a b", a=8)
    xn_v = mergedF[:, :].rearrange("p (a b) -> p a b", a=NT)
    r2 = sb("r2", [128, 4096], BF16)
    pooledT_v = r2[:, 0:2048].rearrange("p (a b) -> p a b", a=4)
    ypT_v = r2[:, 2048:4096].rearrange("p (a b) -> p a b", a=4)
    mbf_v = r2[:, 0:4096].rearrange("p (a b) -> p a b", a=8)
    hp_tok = sb("hp_tok", [128, NT + 1, 512], BF16)
    qxT = sb("qxT", [128, 4, T], BF16)
    oT = sb("oT", [128, 4, T], BF16)
    pT = [sb("pT%d" % i, [128, T], BF16) for i in range(4)]
    rden = sb("rden", [128, T], F32)
    th = [sb("th%d" % i, [128, T], F32) for i in range(2)]
    tt = [sb("tt%d" % i, [128, T], F32) for i in range(2)]
    rot = [sb("rot%d" % i, [128, 4, 4, 64], F32) for i in range(1)]
    krot = [sb("krot%d" % i, [128, 4, 128], BF16) for i in range(2)]
    ssb = [sb("ssb%d" % i, [128, 4, 128], BF16) for i in range(2)]
    ropes = [sb("ropes%d" % i, [128, 2, 4, 64], F32) for i in range(2)]
    carry = sb("carry", [128, NF, 2], F32)
    bnd = sb("bnd", [128, NF, 2], F32)
    btmp = sb("btmp", [128, 2, NF], F32)
    memT = sb("memT", [128, 8, MEM], BF16)
    kmT = sb("kmT", [128, 4, MEM], BF16)
    vm = sb("vm", [128, 2, 512], BF16)
    W32 = sb("W32", [128, 4, 256], F32)
    Rbf = sb("Rbf", [128, 4, 256], BF16)
    bnst = sb("bnst", [128, 4, 6], F32)
    mv = sb("mv", [128, 4, 2], F32)
    grs = sb("grs", [128, 4], F32)
    gnb = sb("gnb", [128, 4], F32)
    gt_mix = sb("gt_mix", [128, D], F32)
    gt_ffn = sb("gt_ffn", [128, D], F32)
    gt_fin = sb("gt_fin", [128, D], F32)
    yst = [sb("yst%d" % i, [128, D], F32) for i in range(4)]
    pp = sb("pp", [128, 108], F32)
    dmy = sb("dmy", [128, 1], F32)
    mask4 = sb("mask4", [128, 4, 128], F32)
    cst = sb("cst", [128, 14, 128], BF16)
    wpool = sb("wpool", [128, 4, 128], BF16)
    slots = [sb("wslot%d" % i, [128, 4096], BF16) for i in range(NS)]
    psf = [nc.alloc_psum_tensor("ps%d" % i, [128, 512], F32) for i in range(8)]
    psb = [p.bitcast(BF16) for p in psf]

    ident = cst[:, 12, :]
    ones = cst[:, 13, :]
    PS_OFF, GR_OFF, BR_OFF, CW0, CW1, CW2, CB = 0, 4, 12, 20, 42, 64, 86

    def R(n):
        return Res(n)

    r_x = [R("x%d" % i) for i in range(NT)]
    r_xn = [R("xn%d" % i) for i in range(NT)]
    r_hT = R("hT")
    r_hb = [R("hb0"), R("hb1")]
    r_junk = R("junk")
    r_ss = R("ss")
    r_rstd = R("rstd")
    r_ssf = R("ssf")
    r_rstdf = R("rstdf")
    r_v = R("v")
    r_ktok = R("ktok")
    r_kT = R("kT")
    r_qT = R("qT")
    r_on = R("on")
    r_gTk = [R("gT%d" % k) for k in range(3)]
    r_acc = [R("acc0"), R("acc1")]
    r_g1 = [R("g10"), R("g11")]
    r_merged = [R("mg%d" % j) for j in range(8)]
    r_pooledT = R("pooledT")
    r_ypT = R("ypT")
    r_mbf = R("mbf")
    r_hp = [R("hp%d" % i) for i in range(NT + 1)]
    r_qxT = R("qxT")
    r_oT = R("oT")
    r_pT = [R("pT%d" % i) for i in range(4)]
    r_rden = R("rden")
    r_th = [R("th0"), R("th1")]
    r_tt = [R("tt0"), R("tt1")]
    r_rot = [R("rot0")]
    r_krot = [R("krot0"), R("krot1")]
    r_ssb = [R("ssb0"), R("ssb1")]
    r_ropes = [R("ropes0"), R("ropes1")]
    r_carry = R("carry")
    r_bnd = R("bnd")
    r_btmp = R("btmp")
    r_memT = R("memT")
    r_kmT = R("kmT")
    r_vm = R("vm")
    r_W32 = R("W32")
    r_Rbf = R("Rbf")
    r_bn = R("bn")
    r_mv = R("mv")
    r_grs = R("grs")
    r_gnb = R("gnb")
    r_gt = R("gtiles")
    r_yst = [R("yst%d" % i) for i in range(4)]
    r_const = R("const")
    r_dmy = R("dmy")
    r_const2 = R("const2")
    r_ps = [R("ps%d" % i) for i in range(8)]
    r_slot = [R("slot%d" % i) for i in range(NS)]

    d_x = [DmaSem(nc, "d_x%d" % i) for i in range(NT)]
    d_yst = [DmaSem(nc, "d_y%d" % i) for i in range(4)]
    d_ropes = [DmaSem(nc, "d_r%d" % i) for i in range(2)]
    d_const = DmaSem(nc, "d_c")
    d_const2 = DmaSem(nc, "d_c2")
    d_slot = [DmaSem(nc, "d_s%d" % i) for i in range(NS)]
    d_dbg = DmaSem(nc, "d_dbg")

    bank_ctr = [0]

    def bank():
        b = bank_ctr[0]
        bank_ctr[0] = (b + 1) % 8
        return b

    NUW = 37
    wsc_d = nc.dram_tensor("wsc", [NUW, 128, 4096], BF16, kind="Internal").ap()
    r_wsc = [Res("wsc%d" % u) for u in range(NUW)]
    d_wst = [DmaSem(nc, "d_w%d" % i) for i in range(NS)]

    class WStream:
        def __init__(self):
            self.units = []
            self.issued = 0
            self.slot_of = {}
            self.free = list(range(NS))

        def add(self, name, pieces):
            uid, grp = self.cur
            self.units.append((name, pieces, uid, grp))

        def pump(self):
            while self.issued < len(self.units) and self.free:
                j = self.issued
                s = self.free.pop(0)
                self.slot_of[j] = s
                name, pieces, uid, grp = self.units[j]
                if uid is None or grp == 0:
                    prev = []
                    for (dstf, src) in pieces:
                        o = S.op("pool", lambda e, dstf=dstf, src=src, s=s: e.dma_start(out=dstf(slots[s]), in_=src),
                                 writes=[r_slot[s]], dma=d_slot[s], nodep=tuple(prev), name="wload")
                        prev.append(o)
                else:
                    S.op("pool", lambda e, s=s, uid=uid: e.dma_start(out=slots[s][:, :], in_=wsc_d[uid, :, :]),
                         reads=[r_wsc[uid]], writes=[r_slot[s]], dma=d_slot[s], name="wload2")
                self.issued += 1

        def take(self, j, name):
            assert self.units[j][0] == name, (self.units[j][0], name)
            self.pump()
            assert self.issued > j, ("weight unit not issued", j, name)
            s = self.slot_of[j]
            return slots[s], r_slot[s]

        def release(self, j):
            s = self.slot_of[j]
            name, pieces, uid, grp = self.units[j]
            if uid is not None and grp == 0:
                S.op("sp", lambda e, s=s, uid=uid: e.dma_start(out=wsc_d[uid, :, :], in_=slots[s][:, :]),
                     reads=[r_slot[s]], writes=[r_wsc[uid]], dma=d_wst[s], name="wstore")
            self.free.append(s)
            self.pump()

    W = WStream()
    w_in_v = w_in_d.rearrange("(k p) n -> p k n", p=128)
    w_r_v = w_r_d.rearrange("(k p) n -> p k n", p=128)
    w_kv_v = w_kv_d.rearrange("(k p) n -> p k n", p=128)
    w_out_v = w_out_d.rearrange("(k p) n -> p k n", p=128)
    w_up_v = w_up_d.rearrange("(k p) n -> p k n", p=128)
    w_a_v = w_a_d.rearrange("(k p) n -> p k n", p=128)
    w_c_v = w_c_d.rearrange("(k p) n -> p k n", p=128)
    w_down_v = w_down_d.rearrange("(f p) n -> p f n", p=128)

    def k8(slot):
        return slot[:, 0:4096].rearrange("p (k n) -> p k n", k=8)

    def k4(slot):
        return slot[:, 0:4096].rearrange("p (k n) -> p k n", k=4)

    def unit_k8(src_v, c0):
        return [(lambda sl: k8(sl), src_v[:, :, c0:c0 + 512])]

    IN_COL = {"hp": 0, "q": 512, "k": 1024, "v0": 1536, "v1": 2048, "gr0": 2560, "gr1": 3072, "qx": 3584,
              "gp0": 4096, "gp1": 4608, "gret0": 5120, "gret1": 5632, "gm0": 6144, "gm1": 6656}
    order = []
    for g in range(NG):
        gl = []
        gl += ["v0", "k", "v1", "q", "gr0", "gr1", "hp", "qx", "a", "gp0", "gp1", "r0", "gret0", "r1", "gret1",
                  "c", "gm0", "gm1", "out0", "out1"]
        gl += ["up%d" % u for u in range(11)]
        gl += ["dn%d_%d" % (c2, k) for c2 in range(2) for k in range(3)]
        assert len(gl) == NUW
        for u, nm in enumerate(gl):
            if nm == "hp" and g % GPS == 0:
                order += [("kvk", None, g), ("kvv", None, g)]
            order.append((nm, u, g))
    for (nm, uid, grp) in order:
        W.cur = (uid, grp)
        if nm in IN_COL:
            W.add(nm, unit_k8(w_in_v, IN_COL[nm]))
        elif nm == "kvk":
            W.add(nm, unit_k8(w_kv_v, 0))
        elif nm == "kvv":
            W.add(nm, unit_k8(w_kv_v, 512))
        elif nm in ("r0", "r1"):
            W.add(nm, unit_k8(w_r_v, 512 * int(nm[1])))
        elif nm in ("out0", "out1"):
            W.add(nm, unit_k8(w_out_v, 512 * int(nm[3])))
        elif nm == "a":
            W.add(nm, [(lambda sl: k4(sl), w_a_v)])
        elif nm == "c":
            W.add(nm, [(lambda sl: k4(sl), w_c_v)])
        elif nm.startswith("up"):
            u = int(nm[2:])
            W.add(nm, [(lambda sl: k8(sl)[:, :, 0:256], w_up_v[:, :, u * 256:(u + 1) * 256]),
                       (lambda sl: k8(sl)[:, :, 256:512], w_up_v[:, :, FH + u * 256:FH + (u + 1) * 256])])
        elif nm.startswith("dn"):
            c2 = int(nm[2])
            k = int(nm[4])
            nf = min(8, NF - 8 * k)
            W.add(nm, [(lambda sl, nf=nf: k8(sl)[:, 0:nf, :],
                        w_down_v[:, 8 * k:8 * k + nf, c2 * 512:(c2 + 1) * 512])])
        else:
            raise ValueError(nm)
    wpos = [0]

    def wtake(name):
        j = wpos[0]
        wpos[0] += 1
        sl, rs = W.take(j, name)
        return j, sl, rs

    def mm_group(e, out_ap, pairs):
        n = len(pairs)
        last = None
        for i, (l, r) in enumerate(pairs):
            last = e.matmul(out_ap, lhsT=l, rhs=r, start=(i == 0), stop=(i == n - 1))
        return last

    def dump(name, ap, res):
        if dbg and name in dbg_d and name not in dumped:
            dumped.add(name)
            S.op("pool", lambda e: e.dma_start(out=dbg_d[name], in_=ap), reads=res, dma=d_dbg, name="dbg")
    dumped = set()

    for i in range(NT):
        S.op("sp", lambda e, i=i: e.dma_start(out=xn_v[:, i, :], in_=x_d[i * 128:(i + 1) * 128, :]),
             writes=[r_xn[i], r_merged[2 * i], r_merged[2 * i + 1]], dma=d_x[i], name="xload")
    S.op("dve", lambda e: e.memset(dmy[:], 1.0), writes=[r_dmy])
    S.op("sp", lambda e: e.dma_start(out=pp[:], in_=pp_d), writes=[r_const], dma=d_const)
    S.op("sp", lambda e: e.dma_start(out=mask4[:].rearrange("p a b -> p (a b)"), in_=mask_d), writes=[r_const],
         dma=d_const, nodep=tuple(d_const.ops))
    S.op("sp", lambda e: e.dma_start(out=gt_mix[:], in_=gvec_d[0, :].partition_broadcast(128)), writes=[r_gt],
         dma=d_const)
    S.op("sp", lambda e: e.dma_start(out=gt_ffn[:], in_=gvec_d[1, :].partition_broadcast(128)), writes=[r_gt],
         dma=d_const, nodep=tuple(d_const.ops))
    S.op("sp", lambda e: e.dma_start(out=gt_fin[:], in_=gvec_d[2, :].partition_broadcast(128)), writes=[r_gt],
         dma=d_const, nodep=tuple(d_const.ops))
    S.op("pool", lambda e: e.dma_start(out=cst[:], in_=cst_d), writes=[r_const2], dma=d_const2)
    S.op("pool", lambda e: e.dma_start(out=wpool[:], in_=w_pool_d.rearrange("g c d -> c g d")), writes=[r_const2],
         dma=d_const2, nodep=tuple(d_const2.ops))
    for o in d_const.ops:
        o.val = d_const.count
    for o in d_const2.ops:
        o.val = d_const2.count

    def norm_stats(src_fn, r_src, ntiles):
        for i in range(ntiles):
            S.op("act", lambda e, i=i: e.activation(out=junk[:], in_=src_fn(i), func=AF.Square,
                                                    accum_out=ss[:, i:i + 1]),
                 reads=[r_src[i]], writes=[r_junk, r_ss], name="sq")
        S.op("dve", lambda e: e.tensor_scalar(out=rstd[:, 0:ntiles], in0=ss[:, 0:ntiles], scalar1=1.0 / D,
                                              scalar2=EPS, op0=ALU.mult, op1=ALU.add),
             reads=[r_ss], writes=[r_rstd])
        S.op("act", lambda e: e.activation(out=rstd[:, 0:ntiles], in_=rstd[:, 0:ntiles], func=AF.Sqrt),
             reads=[r_rstd], writes=[r_rstd])
        S.op("dve", lambda e: e.reciprocal(out=rstd[:, 0:ntiles], in_=rstd[:, 0:ntiles]),
             reads=[r_rstd], writes=[r_rstd])

    def norm_tile(i, src_fn, r_src, gtile, dstT, dst_res):
        hbi = i % 2
        S.op("dve", lambda e, i=i, hbi=hbi: e.scalar_tensor_tensor(
            out=hb[hbi][:], in0=src_fn(i), scalar=rstd[:, i:i + 1], in1=gtile[:], op0=ALU.mult, op1=ALU.mult),
            reads=[r_src[i], r_rstd, r_gt], writes=[r_hb[hbi]])
        b = bank()

        def tr(e, hbi=hbi, b=b):
            last = None
            for k in range(8):
                last = e.transpose(out=psb[b][:, k * 128:(k + 1) * 128], in_=hb[hbi][:, k * 128:(k + 1) * 128],
                                   identity=ident)
            return last
        S.op("pe", tr, reads=[r_hb[hbi], r_const, r_const2], writes=[r_ps[b]])
        S.op("act", lambda e, i=i, b=b: e.activation(
            out=dstT[:, :, i * 128:(i + 1) * 128],
            in_=psb[b][:, 0:1024].rearrange("p (k t) -> p k t", k=8), func=AF.Copy),
            reads=[r_ps[b]], writes=[dst_res])

    def norm_to_T(src_fn, r_src, gtile, dstT, dst_res, ntiles, tok_stride):
        norm_stats(src_fn, r_src, ntiles)
        for i in range(ntiles):
            norm_tile(i, src_fn, r_src, gtile, dstT, dst_res)

    def load_x(g):
        for i in range(NT):
            S.op("sp", lambda e, i=i, g=g: e.dma_start(out=xn_v[:, i, :],
                                                      in_=x_d[g * T + i * 128: g * T + (i + 1) * 128, :]),
                 writes=[r_xn[i], r_merged[2 * i], r_merged[2 * i + 1]], dma=d_x[i], name="xload")

    def copy_x():
        for i in range(NT):
            S.op("act", lambda e, i=i: e.activation(out=xres[:, i, :], in_=xn_v[:, i, :], func=AF.Copy),
                 reads=[r_xn[i]], writes=[r_x[i]])

    out_ops = []

    def do_group(g):
        seq = g // GPS
        gi = g % GPS
        tok0 = g * T
        first = (gi == 0)

        if first:
            S.op("dve", lambda e: e.memset(W32[:], 0.0), writes=[r_W32])
            S.op("dve", lambda e: e.memset(Rbf[:], 0.0), writes=[r_Rbf])
            S.op("dve", lambda e: e.memset(carry[:], 0.0), writes=[r_carry])
            S.op("sp", lambda e: e.dma_start(out=yst[1][:], in_=gvec_d[3, :].partition_broadcast(128)),
                 writes=[r_yst[1]], dma=d_yst[1])
            for mi, yb in ((0, 0), (1, 2)):
                S.op("sp", lambda e, mi=mi, yb=yb: e.dma_start(
                    out=yst[yb][:], in_=mem_d[seq * MEM + mi * 128: seq * MEM + (mi + 1) * 128, :]),
                    writes=[r_yst[yb]], dma=d_yst[yb])
                S.op("act", lambda e, mi=mi, yb=yb: e.activation(out=junk[:], in_=yst[yb][:], func=AF.Square,
                                                                 accum_out=ss[:, mi:mi + 1]),
                     reads=[r_yst[yb]], writes=[r_junk, r_ss])
            S.op("dve", lambda e: e.tensor_scalar(out=rstd[:, 0:2], in0=ss[:, 0:2], scalar1=1.0 / D, scalar2=EPS,
                                                  op0=ALU.mult, op1=ALU.add), reads=[r_ss], writes=[r_rstd])
            S.op("act", lambda e: e.activation(out=rstd[:, 0:2], in_=rstd[:, 0:2], func=AF.Sqrt),
                 reads=[r_rstd], writes=[r_rstd])
            S.op("dve", lambda e: e.reciprocal(out=rstd[:, 0:2], in_=rstd[:, 0:2]), reads=[r_rstd], writes=[r_rstd])
            for mi, yb in ((0, 0), (1, 2)):
                S.op("dve", lambda e, mi=mi, yb=yb: e.scalar_tensor_tensor(
                    out=hb[mi][:], in0=yst[yb][:], scalar=rstd[:, mi:mi + 1], in1=yst[1][:], op0=ALU.mult,
                    op1=ALU.mult),
                    reads=[r_yst[yb], r_yst[1], r_rstd], writes=[r_hb[mi]])

        dump("hT", hT[:].rearrange("p a b -> p (a b)"), [r_hT])

        fence1 = r_gTk + [r_acc[0], r_acc[1], r_g1[0], r_g1[1]]
        for vu, which in ((0, "k"), (1, "q")):
            jv, slv, rsv = wtake("v%d" % vu)
            j, sl, rs = wtake(which)
            pend = None
            for i in range(NT):
                c = gi * NT + i
                rb = i % 2
                tcos = 0 if which == "q" else 2
                S.op("sp", lambda e, c=c, rb=rb, tcos=tcos: e.dma_start(
                    out=ropes[rb][:, 0:2, :, :].rearrange("p a b c -> p (a b c)"),
                    in_=ropet_d[c, :, tcos * 256:(tcos + 2) * 256]),
                    writes=[r_ropes[rb]], dma=d_ropes[rb])
                b = bank()
                S.op("pe", lambda e, i=i, b=b, sl=sl: mm_group(
                    e, psf[b][:], [(hT[:, kc, i * 128:(i + 1) * 128], k8(sl)[:, kc, :]) for kc in range(8)]),
                    reads=[rs, r_hT], writes=[r_ps[b]])
                bv = bank()
                S.op("pe", lambda e, i=i, bv=bv, slv=slv: mm_group(
                    e, psf[bv][:], [(hT[:, kc, i * 128:(i + 1) * 128], k8(slv)[:, kc, :]) for kc in range(8)]),
                    reads=[rsv, r_hT], writes=[r_ps[bv]])
                S.op("act", lambda e, i=i, bv=bv, vu=vu: e.activation(out=v_v[:, i, vu * 512:(vu + 1) * 512],
                                                                       in_=psf[bv][:], func=AF.Copy),
                     reads=[r_ps[bv]], writes=[r_v] + (fence1 if (vu == 0 and i == 0) else []))
                pv = psf[b][:, :].rearrange("p (h d) -> p h d", h=4)
                x1 = pv[:, :, 0:64]
                x2 = pv[:, :, 64:128]
                kb = i % 2
                rt = rot[0]

                def rotary(e, x1=x1, x2=x2, rb=rb, rt=rt):
                    e.tensor_tensor(out=rt[:, 0, :, :], in0=x1, in1=ropes[rb][:, 0, :, :], op=ALU.mult)
                    e.tensor_tensor(out=rt[:, 1, :, :], in0=x2, in1=ropes[rb][:, 1, :, :], op=ALU.mult)
                    e.tensor_tensor(out=rt[:, 2, :, :], in0=x1, in1=ropes[rb][:, 1, :, :], op=ALU.mult)
                    return e.tensor_tensor(out=rt[:, 3, :, :], in0=x2, in1=ropes[rb][:, 0, :, :], op=ALU.mult)
                S.op("dve", rotary, reads=[r_ps[b], r_ropes[rb]], writes=[r_rot[0]])
                if which == "k":
                    dst = ktok_v[:, i, :].rearrange("p (h d) -> p h d", h=4)
                    dres = r_ktok
                else:
                    dst = krot[kb][:]
                    dres = r_krot[kb]

                def rotary2(e, dst=dst, rt=rt):
                    e.tensor_tensor(out=dst[:, :, 0:64], in0=rt[:, 0, :, :], in1=rt[:, 1, :, :], op=ALU.subtract)
                    return e.tensor_tensor(out=dst[:, :, 64:128], in0=rt[:, 2, :, :], in1=rt[:, 3, :, :], op=ALU.add)
                S.op("dve", rotary2, reads=[r_rot[0]], writes=[dres])
                if which == "q" and pending and i < NT - 1:
                    pending[-1].part(i)
                src = ktok_v[:, i, :] if which == "k" else krot[kb][:].rearrange("p h d -> p (h d)")
                dT = kT_v if which == "k" else qT_v
                dTres = r_kT if which == "k" else r_qT

                def emit_tr(i=i, src=src, dres=dres, dT=dT, dTres=dTres):
                    b2 = bank()

                    def trq(e, b2=b2, src=src):
                        last = None
                        for h in range(4):
                            last = e.transpose(out=psb[b2][:, h * 128:(h + 1) * 128],
                                               in_=src[:, h * 128:(h + 1) * 128], identity=ident)
                        return last
                    S.op("pe", trq, reads=[dres, r_const, r_const2], writes=[r_ps[b2]])
                    S.op("act", lambda e, i=i, b2=b2, dT=dT: e.activation(
                        out=dT[:, :, i * 128:(i + 1) * 128],
                        in_=psb[b2][:, 0:512].rearrange("p (h t) -> p h t", h=4), func=AF.Copy),
                        reads=[r_ps[b2]], writes=[dTres])
                if pend is not None:
                    pend()
                pend = emit_tr
            pend()
            if which == "q" and pending:
                pending[-1].part(NT - 1)
            W.release(jv)
            W.release(j)
        if pending:
            pending.pop()
        dump("qT", qT_v.rearrange("p a b -> p (a b)"), [r_qT])
        dump("kT", kT_v.rearrange("p a b -> p (a b)"), [r_kT])
        dump("v", v_v.rearrange("p a b -> p (a b)"), [r_v])

        GC = [float(np.exp(float(T) * np.log1p(-2.0 ** (-5.0 - h)))) for h in range(4)]
        sbufs = [(ssb[0], r_ssb[0]), (ssb[1], r_ssb[1])] + [(pT[q_][:, :].rearrange("p (h t) -> p h t", h=4), r_pT[q_])
                                                           for q_ in range(4)]
        sb_free = list(range(6))
        S_blk = {}
        o_banks = {}

        def rec_scores(i):
            for jt in range(i + 1):
                bs = bank()

                def scores(e, i=i, jt=jt, bs=bs):
                    last = None
                    for h in range(4):
                        last = e.matmul(psf[bs][:, h * 128:(h + 1) * 128], lhsT=kT_v[:, h, jt * 128:(jt + 1) * 128],
                                        rhs=qT_v[:, h, i * 128:(i + 1) * 128], start=True, stop=True)
                    return last
                S.op("pe", scores, reads=[r_kT, r_qT], writes=[r_ps[bs]])
                bi = sb_free.pop(0)
                buf, rbuf = sbufs[bi]
                bufap = buf[:] if bi < 2 else buf
                if jt == i:
                    S.op("dve", lambda e, bs=bs, bufap=bufap: e.tensor_tensor(
                        out=bufap, in0=psf[bs][:, :].rearrange("p (h t) -> p h t", h=4), in1=mask4[:], op=ALU.mult),
                        reads=[r_ps[bs], r_const, r_const2], writes=[rbuf])
                else:
                    S.op("act", lambda e, bs=bs, bufap=bufap: e.activation(
                        out=bufap, in_=psf[bs][:, :].rearrange("p (h t) -> p h t", h=4), func=AF.Copy),
                        reads=[r_ps[bs]], writes=[rbuf])
                S_blk[(jt, i)] = (bi, bufap, rbuf)

        def rec_o(i):
            bo = [bank(), bank()]
            blks = [S_blk[(jt, i)] for jt in range(i + 1)]

            def omm(e, i=i, bo=bo, blks=blks):
                last = None
                for h in range(4):
                    o_ap = psf[bo[h // 2]][:, (h % 2) * 256:(h % 2 + 1) * 256]
                    for jt, (bi, bufap, rbuf) in enumerate(blks):
                        e.matmul(o_ap, lhsT=bufap[:, h, :], rhs=v_v[:, jt, h * 256:(h + 1) * 256], start=(jt == 0),
                                 stop=False)
                    last = e.matmul(o_ap, lhsT=qT_v[:, h, i * 128:(i + 1) * 128], rhs=Rbf[:, h, :], start=False,
                                    stop=True)
                return last
            S.op("pe", omm, reads=[rb for (_, _, rb) in blks] + [r_v, r_qT, r_Rbf],
                 writes=[r_ps[bo[0]], r_ps[bo[1]]])
            for (bi, _, _) in blks:
                sb_free.append(bi)
            o_banks[i] = bo

        def rec_gn(i):
            bo = o_banks[i]

            def bst(e, bo=bo):
                last = None
                for h in range(4):
                    last = e.bn_stats(out=bnst[:, h, :], in_=psf[bo[h // 2]][:, (h % 2) * 256:(h % 2 + 1) * 256])
                return last
            S.op("dve", bst, reads=[r_ps[bo[0]], r_ps[bo[1]]], writes=[r_bn])

            def bag(e):
                last = None
                for h in range(4):
                    last = e.bn_aggr(out=mv[:, h, :], in_=bnst[:, h, :])
                return last
            S.op("dve", bag, reads=[r_bn], writes=[r_mv])
            S.op("dve", lambda e: e.tensor_scalar(out=grs[:], in0=mv[:, :, 1], scalar1=EPS, scalar2=None,
                                                  op0=ALU.add), reads=[r_mv], writes=[r_grs])
            S.op("act", lambda e: e.activation(out=grs[:], in_=grs[:], func=AF.Sqrt), reads=[r_grs], writes=[r_grs])
            S.op("dve", lambda e: e.reciprocal(out=grs[:], in_=grs[:]), reads=[r_grs], writes=[r_grs])
            S.op("dve", lambda e: e.scalar_tensor_tensor(out=gnb[:], in0=mv[:, :, 0], scalar=-1.0, in1=grs[:],
                                                         op0=ALU.mult, op1=ALU.mult),
                 reads=[r_mv, r_grs], writes=[r_gnb])

            def onorm(e, i=i, bo=bo):
                last = None
                for h in range(4):
                    last = e.activation(out=on_v[:, i, h * 256:(h + 1) * 256],
                                        in_=psf[bo[h // 2]][:, (h % 2) * 256:(h % 2 + 1) * 256],
                                        func=AF.Identity, scale=grs[:, h:h + 1], bias=gnb[:, h:h + 1])
                return last
            S.op("act", onorm, reads=[r_ps[bo[0]], r_ps[bo[1]], r_grs, r_gnb], writes=[r_on])

        rec_scores(0)
        rec_scores(1)
        rec_scores(2)
        rec_o(0)
        rec_o(1)
        rec_gn(0)
        rec_o(2)
        rec_gn(1)
        rec_scores(3)
        rec_gn(2)
        rec_o(3)
        rec_gn(3)

        bd = [bank(), bank()]

        def dmm(e, bd=bd):
            last = None
            for h in range(4):
                d_ap = psf[bd[h // 2]][:, (h % 2) * 256:(h % 2 + 1) * 256]
                for jt in range(NT):
                    last = e.matmul(d_ap, lhsT=ktok_v[:, jt, h * 128:(h + 1) * 128],
                                    rhs=v_v[:, jt, h * 256:(h + 1) * 256], start=(jt == 0), stop=(jt == NT - 1))
            return last
        S.op("pe", dmm, reads=[r_ktok, r_v], writes=[r_ps[bd[0]], r_ps[bd[1]]])

        def wupd(e, bd=bd):
            last = None
            for h in range(4):
                d_ap = psf[bd[h // 2]][:, (h % 2) * 256:(h % 2 + 1) * 256]
                last = e.scalar_tensor_tensor(out=W32[:, h, :], in0=W32[:, h, :], scalar=GC[h], in1=d_ap,
                                              op0=ALU.mult, op1=ALU.add)
            return last
        S.op("dve", wupd, reads=[r_ps[bd[0]], r_ps[bd[1]]], writes=[r_W32])

        def rupd(e):
            last = None
            for h in range(4):
                last = e.activation(out=Rbf[:, h, :], in_=W32[:, h, :], func=AF.Copy, scale=GC[h])
            return last
        S.op("act", rupd, reads=[r_W32], writes=[r_Rbf])
        dump("on", on_v.rearrange("p a b -> p (a b)"), [r_on])

        for i in range(NT):
            b = bank()

            def tro(e, i=i, b=b):
                last = None
                for fc in range(8):
                    last = e.transpose(out=psb[b][:, fc * 128:(fc + 1) * 128], in_=on_v[:, i, fc * 128:(fc + 1) * 128],
                                       identity=ident)
                return last
            S.op("pe", tro, reads=[r_on, r_const, r_const2], writes=[r_ps[b]])

            def aff(e, i=i, b=b):
                last = None
                for fc in range(8):
                    last = e.activation(out=retT_v[:, fc, i * 128:(i + 1) * 128], in_=psb[b][:, fc * 128:(fc + 1) * 128],
                                        func=AF.Identity, scale=pp[:, GR_OFF + fc:GR_OFF + fc + 1],
                                        bias=pp[:, BR_OFF + fc:BR_OFF + fc + 1])
                return last
            S.op("act", aff, reads=[r_ps[b], r_const, r_const2], writes=[r_v])

        cnt = 0
        for u in range(2):
            j, sl, rs = wtake("gr%d" % u)
            for fl in range(4):
                fc = u * 4 + fl
                b = bank()
                ti = cnt % 2
                cnt += 1
                S.op("pe", lambda e, fl=fl, b=b, sl=sl: mm_group(
                    e, psf[b][:], [(k8(sl)[:, kc, fl * 128:(fl + 1) * 128], hT[:, kc, :]) for kc in range(8)]),
                    reads=[rs, r_hT], writes=[r_ps[b]])
                S.op("act", lambda e, b=b, ti=ti: e.activation(out=th[ti][:], in_=psf[b][:], func=AF.Tanh, scale=0.5),
                     reads=[r_ps[b]], writes=[r_th[ti]])
                S.op("dve", lambda e, b=b, ti=ti: e.scalar_tensor_tensor(
                    out=tt[ti][:], in0=th[ti][:], scalar=1.0, in1=psf[b][:], op0=ALU.add, op1=ALU.mult),
                    reads=[r_th[ti], r_ps[b]], writes=[r_tt[ti]])
                S.op("dve", lambda e, fc=fc, ti=ti: e.scalar_tensor_tensor(
                    out=retT_v[:, fc, :], in0=tt[ti][:], scalar=0.5, in1=retT_v[:, fc, :], op0=ALU.mult, op1=ALU.mult),
                    reads=[r_tt[ti], r_v], writes=[r_v])
            W.release(j)
        dump("retT", retT_v.rearrange("p a b -> p (a b)"), [r_v])

        def branch_merge(jl_range, j0, y_pairs_fn, y_reads, gate_sl, gate_rs, mode):
            nonlocal cnt
            for jl in jl_range:
                jd = j0 + jl
                by = bank()
                S.op("pe", lambda e, jl=jl, by=by: mm_group(e, psf[by][:], y_pairs_fn(jl)),
                     reads=y_reads, writes=[r_ps[by]])
                bg = bank()
                S.op("pe", lambda e, jl=jl, bg=bg: mm_group(
                    e, psf[bg][:], [(k8(gate_sl)[:, kc, jl * 128:(jl + 1) * 128], hT[:, kc, :]) for kc in range(8)]),
                    reads=[gate_rs, r_hT], writes=[r_ps[bg]])
                ti = cnt % 2
                cnt += 1
                S.op("act", lambda e, bg=bg, ti=ti: e.activation(out=th[ti][:], in_=psf[bg][:], func=AF.Tanh,
                                                                 scale=0.5),
                     reads=[r_ps[bg]], writes=[r_th[ti]])
                if mode == "set":
                    S.op("dve", lambda e, by=by, ti=ti, jd=jd: e.scalar_tensor_tensor(
                        out=mergedT[:, jd, :], in0=th[ti][:], scalar=1.0, in1=psf[by][:], op0=ALU.add, op1=ALU.mult),
                        reads=[r_th[ti], r_ps[by]], writes=[r_merged[jd], r_xn[jd // 2]])
                else:
                    S.op("dve", lambda e, by=by, ti=ti: e.scalar_tensor_tensor(
                        out=tt[ti][:], in0=th[ti][:], scalar=1.0, in1=psf[by][:], op0=ALU.add, op1=ALU.mult),
                        reads=[r_th[ti], r_ps[by]], writes=[r_tt[ti]])
                    if mode == "add":
                        S.op("dve", lambda e, ti=ti, jd=jd: e.tensor_tensor(
                            out=mergedT[:, jd, :], in0=mergedT[:, jd, :], in1=tt[ti][:], op=ALU.add),
                            reads=[r_tt[ti], r_merged[jd]], writes=[r_merged[jd]])
                    else:
                        S.op("dve", lambda e, ti=ti, jd=jd: e.tensor_tensor(
                            out=mbf_v[:, jd, :], in0=mergedT[:, jd, :], in1=tt[ti][:], op=ALU.add),
                            reads=[r_tt[ti], r_merged[jd]], writes=[r_mbf, r_pooledT, r_ypT])

        if first:
            for mi in range(2):
                b = bank()

                def trm(e, b=b, mi=mi):
                    last = None
                    for k in range(8):
                        last = e.transpose(out=psb[b][:, k * 128:(k + 1) * 128], in_=hb[mi][:, k * 128:(k + 1) * 128],
                                           identity=ident)
                    return last
                S.op("pe", trm, reads=[r_hb[mi], r_const, r_const2], writes=[r_ps[b]])
                S.op("act", lambda e, mi=mi, b=b: e.activation(
                    out=memT[:, :, mi * 128:(mi + 1) * 128],
                    in_=psb[b][:, 0:1024].rearrange("p (k t) -> p k t", k=8), func=AF.Copy),
                    reads=[r_ps[b]], writes=[r_memT])
            j, sl, rs = wtake("kvk")
            for h in range(4):
                b = bank()
                S.op("pe", lambda e, h=h, b=b, sl=sl: mm_group(
                    e, psf[b][:, 0:MEM], [(k8(sl)[:, kc, h * 128:(h + 1) * 128], memT[:, kc, :]) for kc in range(8)]),
                    reads=[rs, r_memT], writes=[r_ps[b]])
                S.op("act", lambda e, h=h, b=b: e.activation(out=kmT[:, h, :], in_=psf[b][:, 0:MEM], func=AF.Copy),
                     reads=[r_ps[b]], writes=[r_kmT])
            W.release(j)
            j, sl, rs = wtake("kvv")
            for mc in range(2):
                b = bank()
                S.op("pe", lambda e, mc=mc, b=b, sl=sl: mm_group(
                    e, psf[b][:], [(memT[:, kc, mc * 128:(mc + 1) * 128], k8(sl)[:, kc, :]) for kc in range(8)]),
                    reads=[rs, r_memT], writes=[r_ps[b]])
                S.op("act", lambda e, mc=mc, b=b: e.activation(out=vm[:, mc, :], in_=psf[b][:], func=AF.Copy),
                     reads=[r_ps[b]], writes=[r_vm])
            W.release(j)

        jh, slh, rsh = wtake("hp")
        jq, slq, rsq = wtake("qx")
        for i in range(NT):
            b = bank()
            S.op("pe", lambda e, i=i, b=b, slh=slh: mm_group(
                e, psf[b][:], [(hT[:, kc, i * 128:(i + 1) * 128], k8(slh)[:, kc, :]) for kc in range(8)]),
                reads=[rsh, r_hT], writes=[r_ps[b]])
            S.op("act", lambda e, i=i, b=b: e.activation(out=hp_tok[:, i + 1, :], in_=psf[b][:], func=AF.Copy),
                 reads=[r_ps[b]], writes=[r_hp[i + 1]])
            h = i
            b = bank()
            S.op("pe", lambda e, h=h, b=b, slq=slq: mm_group(
                e, psf[b][:], [(k8(slq)[:, kc, h * 128:(h + 1) * 128], hT[:, kc, :]) for kc in range(8)]),
                reads=[rsq, r_hT], writes=[r_ps[b]])
            S.op("act", lambda e, h=h, b=b: e.activation(out=qxT[:, h, :], in_=psf[b][:], func=AF.Copy),
                 reads=[r_ps[b]], writes=[r_qxT])
        W.release(jh)
        W.release(jq)
        pcnt = 0
        for step in range(4):
            gq = step
            b = bank()

            def poolmm(e, gq=gq, b=b):
                last = None
                for i in range(NT):
                    o_ap = psf[b][:, i * 128:(i + 1) * 128]
                    cur = hp_tok[:, i + 1, gq * 128:(gq + 1) * 128]
                    if first and i == 0:
                        last = e.matmul(o_ap, lhsT=cur, rhs=cst[:, 8 + gq, :], start=True, stop=True)
                    else:
                        e.matmul(o_ap, lhsT=cur, rhs=cst[:, gq, :], start=True, stop=False)
                        last = e.matmul(o_ap, lhsT=hp_tok[:, i, gq * 128:(gq + 1) * 128], rhs=cst[:, 4 + gq, :],
                                        start=False, stop=True)
                return last
            S.op("pe", poolmm, reads=r_hp + [r_const, r_const2], writes=[r_ps[b]])
            S.op("act", lambda e, gq=gq, b=b: e.activation(out=pooledT_v[:, gq, :], in_=psf[b][:], func=AF.Copy),
                 reads=[r_ps[b]], writes=[r_pooledT] + ([r_mbf] if gq == 0 else []))
            h = step
            pis = []
            for mc in range(2):
                bsx = bank()
                pi = pcnt % 4
                pcnt += 1
                pis.append(pi)
                S.op("pe", lambda e, h=h, mc=mc, bsx=bsx: e.matmul(
                    psf[bsx][:], lhsT=kmT[:, h, mc * 128:(mc + 1) * 128], rhs=qxT[:, h, :], start=True, stop=True),
                    reads=[r_kmT, r_qxT], writes=[r_ps[bsx]])
                S.op("act", lambda e, bsx=bsx, pi=pi: e.activation(out=pT[pi][:], in_=psf[bsx][:], func=AF.Exp,
                                                                   scale=float(128.0 ** -0.5)),
                     reads=[r_ps[bsx]], writes=[r_pT[pi]])
            b2 = bank()
            S.op("pe", lambda e, gq=gq, b2=b2: e.matmul(psf[b2][:], lhsT=wpool[:, gq, :], rhs=pooledT_v[:, gq, :],
                                                        start=True, stop=True),
                 reads=[r_pooledT, r_const, r_const2], writes=[r_ps[b2]])
            S.op("act", lambda e, gq=gq, b2=b2: e.activation(out=ypT_v[:, gq, :], in_=psf[b2][:], func=AF.Copy,
                                                             scale=pp[:, PS_OFF + gq:PS_OFF + gq + 1]),
                 reads=[r_ps[b2], r_const, r_const2], writes=[r_ypT])
            bo_ = bank()
            S.op("pe", lambda e, h=h, bo_=bo_, pis=tuple(pis): mm_group(
                e, psf[bo_][:], [(vm[:, mc, h * 128:(h + 1) * 128], pT[pis[mc]][:]) for mc in range(2)]),
                reads=[r_vm, r_pT[pis[0]], r_pT[pis[1]]], writes=[r_ps[bo_]])
            bden = bank()
            S.op("pe", lambda e, bden=bden, pis=tuple(pis): mm_group(
                e, psf[bden][:], [(ones, pT[pis[mc]][:]) for mc in range(2)]),
                reads=[r_const, r_const2, r_pT[pis[0]], r_pT[pis[1]]], writes=[r_ps[bden]])
            S.op("dve", lambda e, bden=bden: e.reciprocal(out=rden[:], in_=psf[bden][:]),
                 reads=[r_ps[bden]], writes=[r_rden])
            S.op("dve", lambda e, h=h, bo_=bo_: e.tensor_tensor(out=oT[:, h, :], in0=psf[bo_][:], in1=rden[:],
                                                               op=ALU.mult),
                 reads=[r_ps[bo_], r_rden], writes=[r_oT])
        S.op("act", lambda e: e.activation(out=hp_tok[:, 0, :], in_=hp_tok[:, NT, :], func=AF.Copy),
             reads=[r_hp[NT]], writes=[r_hp[0]])
        dump("pooledT", pooledT_v.rearrange("p a b -> p (a b)"), [r_pooledT])
        dump("oT", oT[:].rearrange("p a b -> p (a b)"), [r_oT])

        ja, sla, rsa = wtake("a")
        for u in range(2):
            jg, slg, rsg = wtake("gp%d" % u)
            branch_merge(range(4), u * 4,
                         lambda jl, u=u, sla=sla: [(k4(sla)[:, q4, (u * 4 + jl) * 128:(u * 4 + jl + 1) * 128],
                                                   ypT_v[:, q4, :]) for q4 in range(4)],
                         [rsa, r_ypT], slg, rsg, "set")
            W.release(jg)
        W.release(ja)
        for u in range(2):
            jr, slr, rsr = wtake("r%d" % u)
            jg, slg, rsg = wtake("gret%d" % u)
            branch_merge(range(4), u * 4,
                         lambda jl, slr=slr: [(k8(slr)[:, kc, jl * 128:(jl + 1) * 128], retT_v[:, kc, :])
                                              for kc in range(8)],
                         [rsr, r_v], slg, rsg, "add")
            W.release(jr)
            W.release(jg)
        jc, slc, rsc = wtake("c")
        for u in range(2):
            jg, slg, rsg = wtake("gm%d" % u)
            branch_merge(range(4), u * 4,
                         lambda jl, u=u, slc=slc: [(k4(slc)[:, q4, (u * 4 + jl) * 128:(u * 4 + jl + 1) * 128],
                                                   oT[:, q4, :]) for q4 in range(4)],
                         [rsc, r_oT], slg, rsg, "final")
            W.release(jg)
        W.release(jc)
        S.op("act", lambda e: e.activation(out=dmy[:], in_=dmy[:], func=AF.Sqrt), reads=[r_dmy], writes=[r_dmy])
        dump("mbf", mbf_v.rearrange("p a b -> p (a b)"), [r_mbf])

        jo0, slo0, rso0 = wtake("out0")
        jo1, slo1, rso1 = wtake("out1")
        for i in range(NT):
            for c2, slo, rso in ((0, slo0, rso0), (1, slo1, rso1)):
                b = bank()
                S.op("pe", lambda e, i=i, b=b, slo=slo: mm_group(
                    e, psf[b][:], [(mbf_v[:, kc, i * 128:(i + 1) * 128], k8(slo)[:, kc, :]) for kc in range(8)]),
                    reads=[rso, r_mbf], writes=[r_ps[b]])
                S.op("dve", lambda e, i=i, b=b, c2=c2: e.scalar_tensor_tensor(
                    out=xres[:, i, c2 * 512:(c2 + 1) * 512], in0=psf[b][:], scalar=0.5,
                    in1=xres[:, i, c2 * 512:(c2 + 1) * 512], op0=ALU.mult, op1=ALU.add),
                    reads=[r_ps[b], r_x[i]], writes=[r_x[i]])
        W.release(jo0)
        W.release(jo1)
        dump("x2", xres[:].rearrange("p a b -> p (a b)"), r_x)
        if g + 1 < NG:
            load_x(g + 1)

        norm_to_T(lambda i: xres[:, i, :], r_x, gt_ffn, hT, r_hT, NT, 128)
        fence2 = [r_v, r_ktok, r_kT, r_qT, r_on]
        acnt = 0
        if not first:
            S.op("dve", lambda e: e.tensor_tensor(out=btmp[:, 0, :], in0=carry[:, :, 1], in1=pp[:, CW1:CW1 + NF],
                                                  op=ALU.mult), reads=[r_carry, r_const, r_const2], writes=[r_btmp])
            S.op("dve", lambda e: e.tensor_tensor(out=btmp[:, 1, :], in0=carry[:, :, 0], in1=pp[:, CW0:CW0 + NF],
                                                  op=ALU.mult), reads=[r_carry, r_const, r_const2], writes=[r_btmp])
            S.op("dve", lambda e: e.tensor_tensor(out=bnd[:, :, 1], in0=carry[:, :, 1], in1=pp[:, CW0:CW0 + NF],
                                                  op=ALU.mult), reads=[r_carry, r_const, r_const2], writes=[r_bnd])
            S.op("dve", lambda e: e.tensor_tensor(out=bnd[:, :, 0], in0=btmp[:, 0, :], in1=btmp[:, 1, :],
                                                  op=ALU.add), reads=[r_btmp, r_bnd], writes=[r_bnd])
        for u in range(11):
            j, sl, rs = wtake("up%d" % u)
            for fl in range(2):
                f = 2 * u + fl
                ai = acnt % 2
                acnt += 1
                ba = bank()
                S.op("pe", lambda e, fl=fl, ba=ba, sl=sl: mm_group(
                    e, psf[ba][:], [(k8(sl)[:, kc, fl * 128:(fl + 1) * 128], hT[:, kc, :]) for kc in range(8)]),
                    reads=[rs, r_hT], writes=[r_ps[ba]])
                bb = bank()
                S.op("pe", lambda e, fl=fl, bb=bb, sl=sl: mm_group(
                    e, psf[bb][:], [(k8(sl)[:, kc, 256 + fl * 128:256 + (fl + 1) * 128], hT[:, kc, :])
                                    for kc in range(8)]),
                    reads=[rs, r_hT], writes=[r_ps[bb]])
                S.op("act", lambda e, f=f, ba=ba, ai=ai: e.activation(
                    out=acc_v[ai], in_=psf[ba][:], func=AF.Identity, scale=pp[:, CW2 + f:CW2 + f + 1],
                    bias=pp[:, CB + f:CB + f + 1]),
                    reads=[r_ps[ba], r_const, r_const2], writes=[r_acc[ai]] + (fence2 + r_gTk if (u == 0 and fl == 0) else []))

                S.op("dve", lambda e, f=f, ba=ba, ai=ai: e.scalar_tensor_tensor(
                    out=acc_v[ai][:, 1:T], in0=psf[ba][:, 0:T - 1], scalar=pp[:, CW1 + f:CW1 + f + 1],
                    in1=acc_v[ai][:, 1:T], op0=ALU.mult, op1=ALU.add),
                    reads=[r_ps[ba], r_acc[ai], r_const, r_const2], writes=[r_acc[ai]])
                S.op("dve", lambda e, f=f, ba=ba, ai=ai: e.scalar_tensor_tensor(
                    out=acc_v[ai][:, 2:T], in0=psf[ba][:, 0:T - 2], scalar=pp[:, CW0 + f:CW0 + f + 1],
                    in1=acc_v[ai][:, 2:T], op0=ALU.mult, op1=ALU.add),
                    reads=[r_ps[ba], r_acc[ai], r_const, r_const2], writes=[r_acc[ai]])
                if not first:
                    S.op("dve", lambda e, f=f, ai=ai: e.tensor_tensor(
                        out=acc_v[ai][:, 0:2], in0=acc_v[ai][:, 0:2], in1=bnd[:, f, :], op=ALU.add),
                        reads=[r_acc[ai], r_bnd], writes=[r_acc[ai]])
                S.op("act", lambda e, f=f, ba=ba: e.activation(out=carry[:, f, :], in_=psf[ba][:, T - 2:T],
                                                               func=AF.Copy),
                     reads=[r_ps[ba]], writes=[r_carry])
                S.op("act", lambda e, ai=ai: e.activation(out=g1_v[ai], in_=acc_v[ai], func=AF.Gelu_apprx_tanh),
                     reads=[r_acc[ai]], writes=[r_g1[ai]])
                S.op("dve", lambda e, f=f, bb=bb, ai=ai: e.tensor_tensor(out=gT_v[:, f, :], in0=g1_v[ai],
                                                                         in1=psf[bb][:], op=ALU.mult),
                     reads=[r_g1[ai], r_ps[bb]], writes=[r_gTk[f // 8]])
            W.release(j)
        dump("gT", gT_v.rearrange("p a b -> p (a b)"), r_gTk)

        nxt = (g + 1 < NG)
        if nxt:
            norm_stats(lambda i: xn_v[:, i, :], r_xn, NT)

        for c2 in range(2):
            bks = [bank() for _ in range(NT)]
            for k in range(3):
                j, sl, rs = wtake("dn%d_%d" % (c2, k))
                nf = min(8, NF - 8 * k)

                def down(e, k=k, nf=nf, sl=sl, bks=bks):
                    last = None
                    for i in range(NT):
                        for fl in range(nf):
                            f = 8 * k + fl
                            last = e.matmul(psf[bks[i]][:], lhsT=gT_v[:, f, i * 128:(i + 1) * 128],
                                            rhs=k8(sl)[:, fl, :], start=(f == 0), stop=(f == NF - 1))
                    return last
                S.op("pe", down, reads=[rs, r_gTk[k]], writes=[r_ps[bk] for bk in bks])
                W.release(j)
                if nxt and c2 == 1 and k == 0:
                    for i in range(NT):
                        norm_tile(i, lambda i: xn_v[:, i, :], r_xn, gt_mix, hT, r_hT)
            for i in range(NT):
                S.op("dve", lambda e, i=i, c2=c2, bks=bks: e.tensor_tensor(
                    out=xres[:, i, c2 * 512:(c2 + 1) * 512], in0=psf[bks[i]][:],
                    in1=xres[:, i, c2 * 512:(c2 + 1) * 512], op=ALU.add),
                    reads=[r_ps[bks[i]], r_x[i]], writes=[r_x[i]])

        for i in range(NT):
            S.op("act", lambda e, i=i: e.activation(out=junk[:], in_=xres[:, i, :], func=AF.Square,
                                                    accum_out=ssf[:, i:i + 1]),
                 reads=[r_x[i]], writes=[r_junk, r_ssf])
        S.op("dve", lambda e: e.tensor_scalar(out=rstdf[:], in0=ssf[:], scalar1=1.0 / D, scalar2=EPS, op0=ALU.mult,
                                              op1=ALU.add), reads=[r_ssf], writes=[r_rstdf])
        S.op("act", lambda e: e.activation(out=rstdf[:], in_=rstdf[:], func=AF.Sqrt), reads=[r_rstdf],
             writes=[r_rstdf])
        S.op("dve", lambda e: e.reciprocal(out=rstdf[:], in_=rstdf[:]), reads=[r_rstdf], writes=[r_rstdf])

        def tail_part(i, tok0=tok0, nxt=nxt):
            yi = i % 4
            S.op("dve", lambda e, i=i, yi=yi: e.scalar_tensor_tensor(
                out=yst[yi][:], in0=xres[:, i, :], scalar=rstdf[:, i:i + 1], in1=gt_fin[:], op0=ALU.mult,
                op1=ALU.mult),
                reads=[r_x[i], r_rstdf, r_gt], writes=[r_yst[yi]])
            o = S.op("act", lambda e, i=i, yi=yi, tok0=tok0: e.dma_start(
                out=y_d[tok0 + i * 128: tok0 + (i + 1) * 128, :], in_=yst[yi][:]),
                reads=[r_yst[yi]], dma=d_yst[yi], name="ystore")
            out_ops.append(o)
            if nxt:
                S.op("act", lambda e, i=i: e.activation(out=xres[:, i, :], in_=xn_v[:, i, :], func=AF.Copy),
                     reads=[r_xn[i]], writes=[r_x[i]])

        def tail(tail_part=tail_part):
            for i in range(NT):
                tail_part(i)
        tail.part = tail_part
        pending.append(tail)

    pending = []
    norm_to_T(lambda i: xn_v[:, i, :], r_xn, gt_mix, hT, r_hT, NT, 128)
    copy_x()
    for g in range(NG):
        do_group(g)
    pending.pop()()

    fin = S.op("sp", lambda e: None, name="final_wait")
    fin.deps = list(out_ops[-4:]) + list(d_dbg.ops)
    for o in fin.deps:
        o.signal = True
    with nc.Block() as block:
        S.emit(block)
    return nc


def make_consts():
    f32 = np.float32
    half = 64
    inv = (np.float32(10000.0) ** (-np.arange(half, dtype=f32) / np.float32(half))).astype(f32)
    pos = np.arange(SEQ, dtype=f32)
    ang = (pos[:, None] * inv[None, :]).astype(f32).astype(np.float64)
    cos = np.cos(ang)
    sin = np.sin(ang)
    lg = np.log1p(-np.exp2(-5.0 - np.arange(4, dtype=np.float64)))
    p1 = ((np.arange(16)[:, None] % NT) * 128 + np.arange(1, 129)[None, :]).astype(np.float64)
    qd = np.exp(p1[:, :, None] * lg[None, None, :])
    kd = np.exp(-p1[:, :, None] * lg[None, None, :]) * (128.0 ** -0.5)
    ropet = np.zeros((16, 128, 4, 4, 64), np.float64)
    cosr = cos.reshape(16, 128, 1, 64)
    sinr = sin.reshape(16, 128, 1, 64)
    ropet[:, :, 0] = cosr * qd[:, :, :, None]
    ropet[:, :, 1] = sinr * qd[:, :, :, None]
    ropet[:, :, 2] = cosr * kd[:, :, :, None]
    ropet[:, :, 3] = sinr * kd[:, :, :, None]
    ropet = ropet.reshape(16, 128, 1024).astype(f32)
    kk = np.arange(128)[:, None]
    qq = np.arange(128)[None, :]
    m = (kk <= qq).astype(f32)
    mask4 = np.repeat(m[:, None, :], 4, axis=1).reshape(128, 512).astype(f32)
    cst = np.zeros((128, 14, 128), np.float64)
    tp = np.arange(128)[:, None]
    t = np.arange(128)[None, :]
    for gq, w in enumerate((2, 4, 8, 16)):
        cst[:, gq, :] = ((tp <= t) & (tp > t - w)) / w - (tp == t)
        cst[:, 4 + gq, :] = ((tp - 128) > (t - w)) / w
        cst[:, 8 + gq, :] = ((tp <= t) & (tp > t - w)) / np.minimum(t + 1, w) - (tp == t)
    cst[:, 12, :] = np.eye(128)
    cst[:, 13, :] = 1.0
    return ropet, mask4, cst.astype(f32)


_PROGRAM = {}


def kernel(x, mem, g_mix, w_in, w_pool, pool_scale, w_a, g_ret, b_ret, w_r, g_mem, w_mem_kv, w_c, w_out,
           g_ffn, w_up, conv_w, conv_b, w_down, g_final, _dbg=None):
    f32 = np.float32
    x = np.asarray(x, f32)
    mem = np.asarray(mem, f32)
    ropet, mask4, cst = make_consts()
    gvec = np.ascontiguousarray(np.stack([np.asarray(g_mix, f32)[0], np.asarray(g_ffn, f32)[0],
                                          np.asarray(g_final, f32), np.asarray(g_mem, f32)[0]]))
    cw = np.asarray(conv_w, f32)[0]
    pp = np.concatenate([
        np.asarray(pool_scale, f32)[0].reshape(4, 128).T,
        np.asarray(g_ret, f32)[0].reshape(8, 128).T,
        np.asarray(b_ret, f32)[0].reshape(8, 128).T,
        cw[0].reshape(NF, 128).T, cw[1].reshape(NF, 128).T, cw[2].reshape(NF, 128).T,
        np.asarray(conv_b, f32)[0].reshape(NF, 128).T], axis=1)
    pp = np.ascontiguousarray(pp, dtype=f32)
    shared = {
        "w_in": np.ascontiguousarray(np.asarray(w_in, f32)[0]),
        "w_pool": np.ascontiguousarray(np.asarray(w_pool, f32)[0]),
        "w_a": np.ascontiguousarray(np.asarray(w_a, f32)[0]),
        "w_r": np.ascontiguousarray(np.asarray(w_r, f32)[0]),
        "w_mem_kv": np.ascontiguousarray(np.asarray(w_mem_kv, f32)[0]),
        "w_c": np.ascontiguousarray(np.asarray(w_c, f32)[0]),
        "w_out": np.ascontiguousarray(np.asarray(w_out, f32)[0]),
        "w_up": np.ascontiguousarray(np.asarray(w_up, f32)[0]),
        "w_down": np.ascontiguousarray(np.asarray(w_down, f32)[0]),
        "gvec": gvec, "pp": pp, "ropet": ropet, "mask4": mask4, "cst": cst,
    }
    in_maps = []
    for c in range(NCORES):
        m = dict(shared)
        m["x"] = np.ascontiguousarray(x[2 * c:2 * c + 2].reshape(2 * SEQ, D))
        m["mem"] = np.ascontiguousarray(mem[2 * c:2 * c + 2].reshape(2 * MEM, D))
        in_maps.append(m)
    key = tuple(sorted(_dbg.items())) if _dbg else None
    if key not in _PROGRAM:
        _PROGRAM[key] = build_program(_dbg)
    nc = _PROGRAM[key]
    res = run_bass_kernel_spmd(nc, in_maps, core_ids=list(range(NCORES)))
    out = np.concatenate([np.asarray(r["y"], f32).reshape(2, SEQ, D) for r in res.results], axis=0)
    if _dbg:
        return out, res.results
    return out
```

```python
import numpy as np
import concourse.bass as bass
import concourse.mybir as mybir
from concourse.bass_utils import run_bass_kernel_spmd

F32 = mybir.dt.float32
BF16 = mybir.dt.bfloat16
AF = mybir.ActivationFunctionType
ALU = mybir.AluOpType

NCORES = 8
D = 1024
SEQ = 2048
T = 512
NT = T // 128
NG = 2 * SEQ // T
GPS = SEQ // T
MEM = 256
FH = 2816
NF = FH // 128
EPS = 1e-6
NS = 4


class Res:
    __slots__ = ("name", "writer", "readers")

    def __init__(self, name):
        self.name = name
        self.writer = None
        self.readers = []


class Op:
    __slots__ = ("eng", "fn", "deps", "signal", "sem", "val", "inc", "name")


class DmaSem:
    def __init__(self, nc, name):
        self.sem = nc.alloc_semaphore(name)
        self.count = 0
        self.ops = []


class Sched:
    ENGS = ("pe", "act", "dve", "pool", "sp")

    def __init__(self, nc):
        self.nc = nc
        self.ops = {e: [] for e in self.ENGS}
        self.esem = {e: nc.alloc_semaphore("es_" + e) for e in ("pe", "act", "dve", "pool")}

    def op(self, eng, fn, reads=(), writes=(), dma=None, name=None, nodep=()):
        o = Op()
        o.eng = eng
        o.fn = fn
        o.name = name
        o.signal = False
        o.sem = None
        o.val = None
        o.inc = 1
        deps = []
        for r in reads:
            if r.writer is not None:
                deps.append(r.writer)
        for w in writes:
            if w.writer is not None:
                deps.append(w.writer)
            deps.extend(w.readers)
        seen = set()
        fd = []
        for d in deps:
            if id(d) in seen or d is o or d in nodep:
                continue
            seen.add(id(d))
            if d.eng == "pe" and eng == "pe":
                continue
            fd.append(d)
        o.deps = fd
        for d in fd:
            d.signal = True
        if dma is not None:
            dma.count += 16
            o.sem = dma.sem
            o.val = dma.count
            o.inc = 16
            o.signal = True
            dma.ops.append(o)
        for r in reads:
            r.readers.append(o)
        for w in writes:
            w.writer = o
            w.readers = []
        self.ops[eng].append(o)
        return o

    def finalize(self):
        for e in ("pe", "act", "dve", "pool"):
            c = 0
            for o in self.ops[e]:
                if o.sem is None and o.signal:
                    c += 1
                    o.sem = self.esem[e]
                    o.val = c
                    o.inc = 1
        for e in self.ENGS:
            for o in self.ops[e]:
                if o.signal and o.sem is None:
                    raise RuntimeError("signal op without sem: %s" % o.name)

    def emit(self, block):
        self.finalize()
        sched = self

        def run(eng_name, eng):
            waited = {}
            for o in sched.ops[eng_name]:
                need = {}
                for d in o.deps:
                    k = id(d.sem)
                    if k not in need or need[k][1] < d.val:
                        need[k] = (d.sem, d.val)
                for k, (sem, val) in need.items():
                    if waited.get(k, 0) >= val:
                        continue
                    eng.wait_ge(sem, val)
                    waited[k] = val
                last = o.fn(eng)
                if o.signal:
                    if last is None:
                        raise RuntimeError("op %s returned no instruction" % o.name)
                    last.then_inc(o.sem, o.inc)

        @block.tensor
        def _(eng):
            run("pe", eng)

        @block.scalar
        def _(eng):
            run("act", eng)

        @block.vector
        def _(eng):
            run("dve", eng)

        @block.gpsimd
        def _(eng):
            run("pool", eng)

        @block.sync
        def _(eng):
            run("sp", eng)


def build_program(dbg=None):
    nc = bass.Bass("TRN2", target_bir_lowering=False)

    def din(name, shape):
        return nc.dram_tensor(name, list(shape), F32, kind="ExternalInput").ap()

    x_d = din("x", [2 * SEQ, D])
    mem_d = din("mem", [2 * MEM, D])
    w_in_d = din("w_in", [D, 7168])
    w_pool_d = din("w_pool", [4, 128, 128])
    w_a_d = din("w_a", [512, D])
    w_r_d = din("w_r", [D, D])
    w_kv_d = din("w_mem_kv", [D, D])
    w_c_d = din("w_c", [512, D])
    w_out_d = din("w_out", [D, D])
    w_up_d = din("w_up", [D, 2 * FH])
    w_down_d = din("w_down", [FH, D])
    gvec_d = din("gvec", [4, D])
    pp_d = din("pp", [128, 108])
    ropet_d = din("ropet", [16, 128, 1024])
    mask_d = din("mask4", [128, 512])
    cst_d = din("cst", [128, 14, 128])
    y_d = nc.dram_tensor("y", [2 * SEQ, D], F32, kind="ExternalOutput").ap()
    dbg_d = {}
    if dbg:
        for nm, shp in dbg.items():
            dbg_d[nm] = nc.dram_tensor("dbg_" + nm, list(shp), F32, kind="ExternalOutput").ap()

    S = Sched(nc)

    def sb(name, shape, dt):
        return nc.alloc_sbuf_tensor("s_" + name, list(shape), dt)

    xres = sb("xres", [128, NT, D], F32)
    hT = sb("hT", [128, 8, T], BF16)
    hb = [sb("hb%d" % i, [128, D], BF16) for i in range(2)]
    junk = sb("junk", [128, D], BF16)
    ss = sb("ss", [128, NT], F32)
    rstd = sb("rstd", [128, NT], F32)
    ssf = sb("ssf", [128, NT], F32)
    rstdf = sb("rstdf", [128, NT], F32)
    r1f = sb("r1f", [128, 7680], F32)
    r1b = r1f.bitcast(BF16)
    v_v = r1b[:, 0:4096].rearrange("p (a b) -> p a b", a=NT)
    retT_v = r1b[:, 0:4096].rearrange("p (a b) -> p a b", a=8)
    ktok_v = r1b[:, 4096:6144].rearrange("p (a b) -> p a b", a=NT)
    kT_v = r1b[:, 6144:8192].rearrange("p (a b) -> p a b", a=4)
    qT_v = r1b[:, 8192:10240].rearrange("p (a b) -> p a b", a=4)
    on_v = r1b[:, 10240:14336].rearrange("p (a b) -> p a b", a=NT)
    gT_v = r1b[:, 0:NF * T].rearrange("p (a b) -> p a b", a=NF)
    acc_v = [r1f[:, 5632 + i * 512: 5632 + (i + 1) * 512] for i in range(2)]
    g1_v = [r1f[:, 6656 + i * 512: 6656 + (i + 1) * 512] for i in range(2)]
    mergedF = sb("mergedT", [128, 8 * T], F32)
    mergedT = mergedF[:, :].rearrange("p (a b) -> p a b", a=8)
    xn_v = mergedF[:, :].rearrange("p (a b) -> p a b", a=NT)
    r2 = sb("r2", [128, 4096], BF16)
    pooledT_v = r2[:, 0:2048].rearrange("p (a b) -> p a b", a=4)
    ypT_v = r2[:, 2048:4096].rearrange("p (a b) -> p a b", a=4)
    mbf_v = r2[:, 0:4096].rearrange("p (a b) -> p a b", a=8)
    hp_tok = sb("hp_tok", [128, NT + 1, 512], BF16)
    qxT = sb("qxT", [128, 4, T], BF16)
    oT = sb("oT", [128, 4, T], BF16)
    pT = [sb("pT%d" % i, [128, T], BF16) for i in range(4)]
    rden = sb("rden", [128, T], F32)
    th = [sb("th%d" % i, [128, T], F32) for i in range(2)]
    tt = [sb("tt%d" % i, [128, T], F32) for i in range(2)]
    rot = [sb("rot%d" % i, [128, 4, 4, 64], F32) for i in range(1)]
    krot = [sb("krot%d" % i, [128, 4, 128], BF16) for i in range(2)]
    ssb = [sb("ssb%d" % i, [128, 4, 128], BF16) for i in range(2)]
    ropes = [sb("ropes%d" % i, [128, 2, 4, 64], F32) for i in range(2)]
    carry = sb("carry", [128, NF, 2], F32)
    bnd = sb("bnd", [128, NF, 2], F32)
    btmp = sb("btmp", [128, 2, NF], F32)
    memT = sb("memT", [128, 8, MEM], BF16)
    kmT = sb("kmT", [128, 4, MEM], BF16)
    vm = sb("vm", [128, 2, 512], BF16)
    W32 = sb("W32", [128, 4, 256], F32)
    Rbf = sb("Rbf", [128, 4, 256], BF16)
    bnst = sb("bnst", [128, 4, 6], F32)
    mv = sb("mv", [128, 4, 2], F32)
    grs = sb("grs", [128, 4], F32)
    gnb = sb("gnb", [128, 4], F32)
    gt_mix = sb("gt_mix", [128, D], F32)
    gt_ffn = sb("gt_ffn", [128, D], F32)
    gt_fin = sb("gt_fin", [128, D], F32)
    yst = [sb("yst%d" % i, [128, D], F32) for i in range(4)]
    pp = sb("pp", [128, 108], F32)
    dmy = sb("dmy", [128, 1], F32)
    mask4 = sb("mask4", [128, 4, 128], F32)
    cst = sb("cst", [128, 14, 128], BF16)
    wpool = sb("wpool", [128, 4, 128], BF16)
    slots = [sb("wslot%d" % i, [128, 4096], BF16) for i in range(NS)]
    psf = [nc.alloc_psum_tensor("ps%d" % i, [128, 512], F32) for i in range(8)]
    psb = [p.bitcast(BF16) for p in psf]

    ident = cst[:, 12, :]
    ones = cst[:, 13, :]
    PS_OFF, GR_OFF, BR_OFF, CW0, CW1, CW2, CB = 0, 4, 12, 20, 42, 64, 86

    def R(n):
        return Res(n)

    r_x = [R("x%d" % i) for i in range(NT)]
    r_xn = [R("xn%d" % i) for i in range(NT)]
    r_hT = R("hT")
    r_hb = [R("hb0"), R("hb1")]
    r_junk = R("junk")
    r_ss = R("ss")
    r_rstd = R("rstd")
    r_ssf = R("ssf")
    r_rstdf = R("rstdf")
    r_v = R("v")
    r_ktok = R("ktok")
    r_kT = R("kT")
    r_qT = R("qT")
    r_on = R("on")
    r_gTk = [R("gT%d" % k) for k in range(3)]
    r_acc = [R("acc0"), R("acc1")]
    r_g1 = [R("g10"), R("g11")]
    r_merged = [R("mg%d" % j) for j in range(8)]
    r_pooledT = R("pooledT")
    r_ypT = R("ypT")
    r_mbf = R("mbf")
    r_hp = [R("hp%d" % i) for i in range(NT + 1)]
    r_qxT = R("qxT")
    r_oT = R("oT")
    r_pT = [R("pT%d" % i) for i in range(4)]
    r_rden = R("rden")
    r_th = [R("th0"), R("th1")]
    r_tt = [R("tt0"), R("tt1")]
    r_rot = [R("rot0")]
    r_krot = [R("krot0"), R("krot1")]
    r_ssb = [R("ssb0"), R("ssb1")]
    r_ropes = [R("ropes0"), R("ropes1")]
    r_carry = R("carry")
    r_bnd = R("bnd")
    r_btmp = R("btmp")
    r_memT = R("memT")
    r_kmT = R("kmT")
    r_vm = R("vm")
    r_W32 = R("W32")
    r_Rbf = R("Rbf")
    r_bn = R("bn")
    r_mv = R("mv")
    r_grs = R("grs")
    r_gnb = R("gnb")
    r_gt = R("gtiles")
    r_yst = [R("yst%d" % i) for i in range(4)]
    r_const = R("const")
    r_dmy = R("dmy")
    r_const2 = R("const2")
    r_ps = [R("ps%d" % i) for i in range(8)]
    r_slot = [R("slot%d" % i) for i in range(NS)]

    d_x = [DmaSem(nc, "d_x%d" % i) for i in range(NT)]
    d_yst = [DmaSem(nc, "d_y%d" % i) for i in range(4)]
    d_ropes = [DmaSem(nc, "d_r%d" % i) for i in range(2)]
    d_const = DmaSem(nc, "d_c")
    d_const2 = DmaSem(nc, "d_c2")
    d_slot = [DmaSem(nc, "d_s%d" % i) for i in range(NS)]
    d_dbg = DmaSem(nc, "d_dbg")

    bank_ctr = [0]

    def bank():
        b = bank_ctr[0]
        bank_ctr[0] = (b + 1) % 8
        return b

    NUW = 37
    wsc_d = nc.dram_tensor("wsc", [NUW, 128, 4096], BF16, kind="Internal").ap()
    r_wsc = [Res("wsc%d" % u) for u in range(NUW)]
    d_wst = [DmaSem(nc, "d_w%d" % i) for i in range(NS)]

    class WStream:
        def __init__(self):
            self.units = []
            self.issued = 0
            self.slot_of = {}
            self.free = list(range(NS))

        def add(self, name, pieces):
            uid, grp = self.cur
            self.units.append((name, pieces, uid, grp))

        def pump(self):
            while self.issued < len(self.units) and self.free:
                j = self.issued
                s = self.free.pop(0)
                self.slot_of[j] = s
                name, pieces, uid, grp = self.units[j]
                if uid is None or grp == 0:
                    prev = []
                    for (dstf, src) in pieces:
                        o = S.op("pool", lambda e, dstf=dstf, src=src, s=s: e.dma_start(out=dstf(slots[s]), in_=src),
                                 writes=[r_slot[s]], dma=d_slot[s], nodep=tuple(prev), name="wload")
                        prev.append(o)
                else:
                    S.op("pool", lambda e, s=s, uid=uid: e.dma_start(out=slots[s][:, :], in_=wsc_d[uid, :, :]),
                         reads=[r_wsc[uid]], writes=[r_slot[s]], dma=d_slot[s], name="wload2")
                self.issued += 1

        def take(self, j, name):
            assert self.units[j][0] == name, (self.units[j][0], name)
            self.pump()
            assert self.issued > j, ("weight unit not issued", j, name)
            s = self.slot_of[j]
            return slots[s], r_slot[s]

        def release(self, j):
            s = self.slot_of[j]
            name, pieces, uid, grp = self.units[j]
            if uid is not None and grp == 0:
                S.op("sp", lambda e, s=s, uid=uid: e.dma_start(out=wsc_d[uid, :, :], in_=slots[s][:, :]),
                     reads=[r_slot[s]], writes=[r_wsc[uid]], dma=d_wst[s], name="wstore")
            self.free.append(s)
            self.pump()

    W = WStream()
    w_in_v = w_in_d.rearrange("(k p) n -> p k n", p=128)
    w_r_v = w_r_d.rearrange("(k p) n -> p k n", p=128)
    w_kv_v = w_kv_d.rearrange("(k p) n -> p k n", p=128)
    w_out_v = w_out_d.rearrange("(k p) n -> p k n", p=128)
    w_up_v = w_up_d.rearrange("(k p) n -> p k n", p=128)
    w_a_v = w_a_d.rearrange("(k p) n -> p k n", p=128)
    w_c_v = w_c_d.rearrange("(k p) n -> p k n", p=128)
    w_down_v = w_down_d.rearrange("(f p) n -> p f n", p=128)

    def k8(slot):
        return slot[:, 0:4096].rearrange("p (k n) -> p k n", k=8)

    def k4(slot):
        return slot[:, 0:4096].rearrange("p (k n) -> p k n", k=4)

    def unit_k8(src_v, c0):
        return [(lambda sl: k8(sl), src_v[:, :, c0:c0 + 512])]

    IN_COL = {"hp": 0, "q": 512, "k": 1024, "v0": 1536, "v1": 2048, "gr0": 2560, "gr1": 3072, "qx": 3584,
              "gp0": 4096, "gp1": 4608, "gret0": 5120, "gret1": 5632, "gm0": 6144, "gm1": 6656}
    order = []
    for g in range(NG):
        gl = []
        gl += ["v0", "k", "v1", "q", "gr0", "gr1", "hp", "qx", "a", "gp0", "gp1", "r0", "gret0", "r1", "gret1",
                  "c", "gm0", "gm1", "out0", "out1"]
        gl += ["up%d" % u for u in range(11)]
        gl += ["dn%d_%d" % (c2, k) for c2 in range(2) for k in range(3)]
        assert len(gl) == NUW
        for u, nm in enumerate(gl):
            if nm == "hp" and g % GPS == 0:
                order += [("kvk", None, g), ("kvv", None, g)]
            order.append((nm, u, g))
    for (nm, uid, grp) in order:
        W.cur = (uid, grp)
        if nm in IN_COL:
            W.add(nm, unit_k8(w_in_v, IN_COL[nm]))
        elif nm == "kvk":
            W.add(nm, unit_k8(w_kv_v, 0))
        elif nm == "kvv":
            W.add(nm, unit_k8(w_kv_v, 512))
        elif nm in ("r0", "r1"):
            W.add(nm, unit_k8(w_r_v, 512 * int(nm[1])))
        elif nm in ("out0", "out1"):
            W.add(nm, unit_k8(w_out_v, 512 * int(nm[3])))
        elif nm == "a":
            W.add(nm, [(lambda sl: k4(sl), w_a_v)])
        elif nm == "c":
            W.add(nm, [(lambda sl: k4(sl), w_c_v)])
        elif nm.startswith("up"):
            u = int(nm[2:])
            W.add(nm, [(lambda sl: k8(sl)[:, :, 0:256], w_up_v[:, :, u * 256:(u + 1) * 256]),
                       (lambda sl: k8(sl)[:, :, 256:512], w_up_v[:, :, FH + u * 256:FH + (u + 1) * 256])])
        elif nm.startswith("dn"):
            c2 = int(nm[2])
            k = int(nm[4])
            nf = min(8, NF - 8 * k)
            W.add(nm, [(lambda sl, nf=nf: k8(sl)[:, 0:nf, :],
                        w_down_v[:, 8 * k:8 * k + nf, c2 * 512:(c2 + 1) * 512])])
        else:
            raise ValueError(nm)
    wpos = [0]

    def wtake(name):
        j = wpos[0]
        wpos[0] += 1
        sl, rs = W.take(j, name)
        return j, sl, rs

    def mm_group(e, out_ap, pairs):
        n = len(pairs)
        last = None
        for i, (l, r) in enumerate(pairs):
            last = e.matmul(out_ap, lhsT=l, rhs=r, start=(i == 0), stop=(i == n - 1))
        return last

    def dump(name, ap, res):
        if dbg and name in dbg_d and name not in dumped:
            dumped.add(name)
            S.op("pool", lambda e: e.dma_start(out=dbg_d[name], in_=ap), reads=res, dma=d_dbg, name="dbg")
    dumped = set()

    for i in range(NT):
        S.op("sp", lambda e, i=i: e.dma_start(out=xn_v[:, i, :], in_=x_d[i * 128:(i + 1) * 128, :]),
             writes=[r_xn[i], r_merged[2 * i], r_merged[2 * i + 1]], dma=d_x[i], name="xload")
    S.op("dve", lambda e: e.memset(dmy[:], 1.0), writes=[r_dmy])
    S.op("sp", lambda e: e.dma_start(out=pp[:], in_=pp_d), writes=[r_const], dma=d_const)
    S.op("sp", lambda e: e.dma_start(out=mask4[:].rearrange("p a b -> p (a b)"), in_=mask_d), writes=[r_const],
         dma=d_const, nodep=tuple(d_const.ops))
    S.op("sp", lambda e: e.dma_start(out=gt_mix[:], in_=gvec_d[0, :].partition_broadcast(128)), writes=[r_gt],
         dma=d_const)
    S.op("sp", lambda e: e.dma_start(out=gt_ffn[:], in_=gvec_d[1, :].partition_broadcast(128)), writes=[r_gt],
         dma=d_const, nodep=tuple(d_const.ops))
    S.op("sp", lambda e: e.dma_start(out=gt_fin[:], in_=gvec_d[2, :].partition_broadcast(128)), writes=[r_gt],
         dma=d_const, nodep=tuple(d_const.ops))
    S.op("pool", lambda e: e.dma_start(out=cst[:], in_=cst_d), writes=[r_const2], dma=d_const2)
    S.op("pool", lambda e: e.dma_start(out=wpool[:], in_=w_pool_d.rearrange("g c d -> c g d")), writes=[r_const2],
         dma=d_const2, nodep=tuple(d_const2.ops))
    for o in d_const.ops:
        o.val = d_const.count
    for o in d_const2.ops:
        o.val = d_const2.count

    def norm_stats(src_fn, r_src, ntiles):
        for i in range(ntiles):
            S.op("act", lambda e, i=i: e.activation(out=junk[:], in_=src_fn(i), func=AF.Square,
                                                    accum_out=ss[:, i:i + 1]),
                 reads=[r_src[i]], writes=[r_junk, r_ss], name="sq")
        S.op("dve", lambda e: e.tensor_scalar(out=rstd[:, 0:ntiles], in0=ss[:, 0:ntiles], scalar1=1.0 / D,
                                              scalar2=EPS, op0=ALU.mult, op1=ALU.add),
             reads=[r_ss], writes=[r_rstd])
        S.op("act", lambda e: e.activation(out=rstd[:, 0:ntiles], in_=rstd[:, 0:ntiles], func=AF.Sqrt),
             reads=[r_rstd], writes=[r_rstd])
        S.op("dve", lambda e: e.reciprocal(out=rstd[:, 0:ntiles], in_=rstd[:, 0:ntiles]),
             reads=[r_rstd], writes=[r_rstd])

    def norm_tile(i, src_fn, r_src, gtile, dstT, dst_res):
        hbi = i % 2
        S.op("dve", lambda e, i=i, hbi=hbi: e.scalar_tensor_tensor(
            out=hb[hbi][:], in0=src_fn(i), scalar=rstd[:, i:i + 1], in1=gtile[:], op0=ALU.mult, op1=ALU.mult),
            reads=[r_src[i], r_rstd, r_gt], writes=[r_hb[hbi]])
        b = bank()

        def tr(e, hbi=hbi, b=b):
            last = None
            for k in range(8):
                last = e.transpose(out=psb[b][:, k * 128:(k + 1) * 128], in_=hb[hbi][:, k * 128:(k + 1) * 128],
                                   identity=ident)
            return last
        S.op("pe", tr, reads=[r_hb[hbi], r_const, r_const2], writes=[r_ps[b]])
        S.op("act", lambda e, i=i, b=b: e.activation(
            out=dstT[:, :, i * 128:(i + 1) * 128],
            in_=psb[b][:, 0:1024].rearrange("p (k t) -> p k t", k=8), func=AF.Copy),
            reads=[r_ps[b]], writes=[dst_res])

    def norm_to_T(src_fn, r_src, gtile, dstT, dst_res, ntiles, tok_stride):
        norm_stats(src_fn, r_src, ntiles)
        for i in range(ntiles):
            norm_tile(i, src_fn, r_src, gtile, dstT, dst_res)

    def load_x(g):
        for i in range(NT):
            S.op("sp", lambda e, i=i, g=g: e.dma_start(out=xn_v[:, i, :],
                                                      in_=x_d[g * T + i * 128: g * T + (i + 1) * 128, :]),
                 writes=[r_xn[i], r_merged[2 * i], r_merged[2 * i + 1]], dma=d_x[i], name="xload")

    def copy_x():
        for i in range(NT):
            S.op("act", lambda e, i=i: e.activation(out=xres[:, i, :], in_=xn_v[:, i, :], func=AF.Copy),
                 reads=[r_xn[i]], writes=[r_x[i]])

    out_ops = []

    def do_group(g):
        seq = g // GPS
        gi = g % GPS
        tok0 = g * T
        first = (gi == 0)

        if first:
            S.op("dve", lambda e: e.memset(W32[:], 0.0), writes=[r_W32])
            S.op("dve", lambda e: e.memset(Rbf[:], 0.0), writes=[r_Rbf])
            S.op("dve", lambda e: e.memset(carry[:], 0.0), writes=[r_carry])
            S.op("sp", lambda e: e.dma_start(out=yst[1][:], in_=gvec_d[3, :].partition_broadcast(128)),
                 writes=[r_yst[1]], dma=d_yst[1])
            for mi, yb in ((0, 0), (1, 2)):
                S.op("sp", lambda e, mi=mi, yb=yb: e.dma_start(
                    out=yst[yb][:], in_=mem_d[seq * MEM + mi * 128: seq * MEM + (mi + 1) * 128, :]),
                    writes=[r_yst[yb]], dma=d_yst[yb])
                S.op("act", lambda e, mi=mi, yb=yb: e.activation(out=junk[:], in_=yst[yb][:], func=AF.Square,
                                                                 accum_out=ss[:, mi:mi + 1]),
                     reads=[r_yst[yb]], writes=[r_junk, r_ss])
            S.op("dve", lambda e: e.tensor_scalar(out=rstd[:, 0:2], in0=ss[:, 0:2], scalar1=1.0 / D, scalar2=EPS,
                                                  op0=ALU.mult, op1=ALU.add), reads=[r_ss], writes=[r_rstd])
            S.op("act", lambda e: e.activation(out=rstd[:, 0:2], in_=rstd[:, 0:2], func=AF.Sqrt),
                 reads=[r_rstd], writes=[r_rstd])
            S.op("dve", lambda e: e.reciprocal(out=rstd[:, 0:2], in_=rstd[:, 0:2]), reads=[r_rstd], writes=[r_rstd])
            for mi, yb in ((0, 0), (1, 2)):
                S.op("dve", lambda e, mi=mi, yb=yb: e.scalar_tensor_tensor(
                    out=hb[mi][:], in0=yst[yb][:], scalar=rstd[:, mi:mi + 1], in1=yst[1][:], op0=ALU.mult,
                    op1=ALU.mult),
                    reads=[r_yst[yb], r_yst[1], r_rstd], writes=[r_hb[mi]])

        dump("hT", hT[:].rearrange("p a b -> p (a b)"), [r_hT])

        fence1 = r_gTk + [r_acc[0], r_acc[1], r_g1[0], r_g1[1]]
        for vu, which in ((0, "k"), (1, "q")):
            jv, slv, rsv = wtake("v%d" % vu)
            j, sl, rs = wtake(which)
            pend = None
            for i in range(NT):
                c = gi * NT + i
                rb = i % 2
                tcos = 0 if which == "q" else 2
                S.op("sp", lambda e, c=c, rb=rb, tcos=tcos: e.dma_start(
                    out=ropes[rb][:, 0:2, :, :].rearrange("p a b c -> p (a b c)"),
                    in_=ropet_d[c, :, tcos * 256:(tcos + 2) * 256]),
                    writes=[r_ropes[rb]], dma=d_ropes[rb])
                b = bank()
                S.op("pe", lambda e, i=i, b=b, sl=sl: mm_group(
                    e, psf[b][:], [(hT[:, kc, i * 128:(i + 1) * 128], k8(sl)[:, kc, :]) for kc in range(8)]),
                    reads=[rs, r_hT], writes=[r_ps[b]])
                bv = bank()
                S.op("pe", lambda e, i=i, bv=bv, slv=slv: mm_group(
                    e, psf[bv][:], [(hT[:, kc, i * 128:(i + 1) * 128], k8(slv)[:, kc, :]) for kc in range(8)]),
                    reads=[rsv, r_hT], writes=[r_ps[bv]])
                S.op("act", lambda e, i=i, bv=bv, vu=vu: e.activation(out=v_v[:, i, vu * 512:(vu + 1) * 512],
                                                                       in_=psf[bv][:], func=AF.Copy),
                     reads=[r_ps[bv]], writes=[r_v] + (fence1 if (vu == 0 and i == 0) else []))
                pv = psf[b][:, :].rearrange("p (h d) -> p h d", h=4)
                x1 = pv[:, :, 0:64]
                x2 = pv[:, :, 64:128]
                kb = i % 2
                rt = rot[0]

                def rotary(e, x1=x1, x2=x2, rb=rb, rt=rt):
                    e.tensor_tensor(out=rt[:, 0, :, :], in0=x1, in1=ropes[rb][:, 0, :, :], op=ALU.mult)
                    e.tensor_tensor(out=rt[:, 1, :, :], in0=x2, in1=ropes[rb][:, 1, :, :], op=ALU.mult)
                    e.tensor_tensor(out=rt[:, 2, :, :], in0=x1, in1=ropes[rb][:, 1, :, :], op=ALU.mult)
                    return e.tensor_tensor(out=rt[:, 3, :, :], in0=x2, in1=ropes[rb][:, 0, :, :], op=ALU.mult)
                S.op("dve", rotary, reads=[r_ps[b], r_ropes[rb]], writes=[r_rot[0]])
                if which == "k":
                    dst = ktok_v[:, i, :].rearrange("p (h d) -> p h d", h=4)
                    dres = r_ktok
                else:
                    dst = krot[kb][:]
                    dres = r_krot[kb]

                def rotary2(e, dst=dst, rt=rt):
                    e.tensor_tensor(out=dst[:, :, 0:64], in0=rt[:, 0, :, :], in1=rt[:, 1, :, :], op=ALU.subtract)
                    return e.tensor_tensor(out=dst[:, :, 64:128], in0=rt[:, 2, :, :], in1=rt[:, 3, :, :], op=ALU.add)
                S.op("dve", rotary2, reads=[r_rot[0]], writes=[dres])
                if which == "q" and pending and i < NT - 1:
                    pending[-1].part(i)
                src = ktok_v[:, i, :] if which == "k" else krot[kb][:].rearrange("p h d -> p (h d)")
                dT = kT_v if which == "k" else qT_v
                dTres = r_kT if which == "k" else r_qT

                def emit_tr(i=i, src=src, dres=dres, dT=dT, dTres=dTres):
                    b2 = bank()

                    def trq(e, b2=b2, src=src):
                        last = None
                        for h in range(4):
                            last = e.transpose(out=psb[b2][:, h * 128:(h + 1) * 128],
                                               in_=src[:, h * 128:(h + 1) * 128], identity=ident)
                        return last
                    S.op("pe", trq, reads=[dres, r_const, r_const2], writes=[r_ps[b2]])
                    S.op("act", lambda e, i=i, b2=b2, dT=dT: e.activation(
                        out=dT[:, :, i * 128:(i + 1) * 128],
                        in_=psb[b2][:, 0:512].rearrange("p (h t) -> p h t", h=4), func=AF.Copy),
                        reads=[r_ps[b2]], writes=[dTres])
                if pend is not None:
                    pend()
                pend = emit_tr
            pend()
            if which == "q" and pending:
                pending[-1].part(NT - 1)
            W.release(jv)
            W.release(j)
        if pending:
            pending.pop()
        dump("qT", qT_v.rearrange("p a b -> p (a b)"), [r_qT])
        dump("kT", kT_v.rearrange("p a b -> p (a b)"), [r_kT])
        dump("v", v_v.rearrange("p a b -> p (a b)"), [r_v])

        GC = [float(np.exp(float(T) * np.log1p(-2.0 ** (-5.0 - h)))) for h in range(4)]
        sbufs = [(ssb[0], r_ssb[0]), (ssb[1], r_ssb[1])] + [(pT[q_][:, :].rearrange("p (h t) -> p h t", h=4), r_pT[q_])
                                                           for q_ in range(4)]
        sb_free = list(range(6))
        S_blk = {}
        o_banks = {}

        def rec_scores(i):
            for jt in range(i + 1):
                bs = bank()

                def scores(e, i=i, jt=jt, bs=bs):
                    last = None
                    for h in range(4):
                        last = e.matmul(psf[bs][:, h * 128:(h + 1) * 128], lhsT=kT_v[:, h, jt * 128:(jt + 1) * 128],
                                        rhs=qT_v[:, h, i * 128:(i + 1) * 128], start=True, stop=True)
                    return last
                S.op("pe", scores, reads=[r_kT, r_qT], writes=[r_ps[bs]])
                bi = sb_free.pop(0)
                buf, rbuf = sbufs[bi]
                bufap = buf[:] if bi < 2 else buf
                if jt == i:
                    S.op("dve", lambda e, bs=bs, bufap=bufap: e.tensor_tensor(
                        out=bufap, in0=psf[bs][:, :].rearrange("p (h t) -> p h t", h=4), in1=mask4[:], op=ALU.mult),
                        reads=[r_ps[bs], r_const, r_const2], writes=[rbuf])
                elif i == NT - 1 and jt >= 1:
                    S.op("dve", lambda e, bs=bs, bufap=bufap: e.tensor_copy(
                        out=bufap, in_=psf[bs][:, :].rearrange("p (h t) -> p h t", h=4)),
                        reads=[r_ps[bs]], writes=[rbuf])
                else:
                    S.op("act", lambda e, bs=bs, bufap=bufap: e.activation(
                        out=bufap, in_=psf[bs][:, :].rearrange("p (h t) -> p h t", h=4), func=AF.Copy),
                        reads=[r_ps[bs]], writes=[rbuf])
                S_blk[(jt, i)] = (bi, bufap, rbuf)

        def rec_o(i):
            bo = [bank(), bank()]
            blks = [S_blk[(jt, i)] for jt in range(i + 1)]

            def omm(e, i=i, bo=bo, blks=blks):
                last = None
                for h in range(4):
                    o_ap = psf[bo[h // 2]][:, (h % 2) * 256:(h % 2 + 1) * 256]
                    for jt, (bi, bufap, rbuf) in enumerate(blks):
                        e.matmul(o_ap, lhsT=bufap[:, h, :], rhs=v_v[:, jt, h * 256:(h + 1) * 256], start=(jt == 0),
                                 stop=False)
                    last = e.matmul(o_ap, lhsT=qT_v[:, h, i * 128:(i + 1) * 128], rhs=Rbf[:, h, :], start=False,
                                    stop=True)
                return last
            S.op("pe", omm, reads=[rb for (_, _, rb) in blks] + [r_v, r_qT, r_Rbf],
                 writes=[r_ps[bo[0]], r_ps[bo[1]]])
            for (bi, _, _) in blks:
                sb_free.append(bi)
            o_banks[i] = bo

        def rec_gn(i):
            bo = o_banks[i]

            def bst(e, bo=bo):
                last = None
                for h in range(4):
                    last = e.bn_stats(out=bnst[:, h, :], in_=psf[bo[h // 2]][:, (h % 2) * 256:(h % 2 + 1) * 256])
                return last
            S.op("dve", bst, reads=[r_ps[bo[0]], r_ps[bo[1]]], writes=[r_bn])

            def bag(e):
                last = None
                for h in range(4):
                    last = e.bn_aggr(out=mv[:, h, :], in_=bnst[:, h, :])
                return last
            S.op("dve", bag, reads=[r_bn], writes=[r_mv])
            S.op("dve", lambda e: e.tensor_scalar(out=grs[:], in0=mv[:, :, 1], scalar1=EPS, scalar2=None,
                                                  op0=ALU.add), reads=[r_mv], writes=[r_grs])
            S.op("act", lambda e: e.activation(out=grs[:], in_=grs[:], func=AF.Sqrt), reads=[r_grs], writes=[r_grs])
            S.op("dve", lambda e: e.reciprocal(out=grs[:], in_=grs[:]), reads=[r_grs], writes=[r_grs])
            S.op("dve", lambda e: e.scalar_tensor_tensor(out=gnb[:], in0=mv[:, :, 0], scalar=-1.0, in1=grs[:],
                                                         op0=ALU.mult, op1=ALU.mult),
                 reads=[r_mv, r_grs], writes=[r_gnb])

            def onorm(e, i=i, bo=bo):
                last = None
                for h in range(4):
                    last = e.activation(out=on_v[:, i, h * 256:(h + 1) * 256],
                                        in_=psf[bo[h // 2]][:, (h % 2) * 256:(h % 2 + 1) * 256],
                                        func=AF.Identity, scale=grs[:, h:h + 1], bias=gnb[:, h:h + 1])
                return last
            S.op("act", onorm, reads=[r_ps[bo[0]], r_ps[bo[1]], r_grs, r_gnb], writes=[r_on])

        rec_scores(0)
        rec_scores(1)
        rec_scores(2)
        rec_o(0)
        rec_o(1)
        rec_gn(0)
        rec_o(2)
        rec_gn(1)
        rec_scores(3)
        rec_gn(2)
        rec_o(3)
        rec_gn(3)

        bd = [bank(), bank()]

        def dmm(e, bd=bd):
            last = None
            for h in range(4):
                d_ap = psf[bd[h // 2]][:, (h % 2) * 256:(h % 2 + 1) * 256]
                for jt in range(NT):
                    last = e.matmul(d_ap, lhsT=ktok_v[:, jt, h * 128:(h + 1) * 128],
                                    rhs=v_v[:, jt, h * 256:(h + 1) * 256], start=(jt == 0), stop=(jt == NT - 1))
            return last
        S.op("pe", dmm, reads=[r_ktok, r_v], writes=[r_ps[bd[0]], r_ps[bd[1]]])

        def wupd(e, bd=bd):
            last = None
            for h in range(4):
                d_ap = psf[bd[h // 2]][:, (h % 2) * 256:(h % 2 + 1) * 256]
                last = e.scalar_tensor_tensor(out=W32[:, h, :], in0=W32[:, h, :], scalar=GC[h], in1=d_ap,
                                              op0=ALU.mult, op1=ALU.add)
            return last
        S.op("dve", wupd, reads=[r_ps[bd[0]], r_ps[bd[1]]], writes=[r_W32])

        def rupd(e):
            last = None
            for h in range(4):
                last = e.activation(out=Rbf[:, h, :], in_=W32[:, h, :], func=AF.Copy, scale=GC[h])
            return last
        S.op("act", rupd, reads=[r_W32], writes=[r_Rbf])
        dump("on", on_v.rearrange("p a b -> p (a b)"), [r_on])

        for i in range(NT):
            b = bank()

            def tro(e, i=i, b=b):
                last = None
                for fc in range(8):
                    last = e.transpose(out=psb[b][:, fc * 128:(fc + 1) * 128], in_=on_v[:, i, fc * 128:(fc + 1) * 128],
                                       identity=ident)
                return last
            S.op("pe", tro, reads=[r_on, r_const, r_const2], writes=[r_ps[b]])

            def aff(e, i=i, b=b):
                last = None
                for fc in range(8):
                    last = e.activation(out=retT_v[:, fc, i * 128:(i + 1) * 128], in_=psb[b][:, fc * 128:(fc + 1) * 128],
                                        func=AF.Identity, scale=pp[:, GR_OFF + fc:GR_OFF + fc + 1],
                                        bias=pp[:, BR_OFF + fc:BR_OFF + fc + 1])
                return last
            S.op("act", aff, reads=[r_ps[b], r_const, r_const2], writes=[r_v])

        cnt = 0
        for u in range(2):
            j, sl, rs = wtake("gr%d" % u)
            for fl in range(4):
                fc = u * 4 + fl
                b = bank()
                ti = cnt % 2
                cnt += 1
                S.op("pe", lambda e, fl=fl, b=b, sl=sl: mm_group(
                    e, psf[b][:], [(k8(sl)[:, kc, fl * 128:(fl + 1) * 128], hT[:, kc, :]) for kc in range(8)]),
                    reads=[rs, r_hT], writes=[r_ps[b]])
                S.op("act", lambda e, b=b, ti=ti: e.activation(out=th[ti][:], in_=psf[b][:], func=AF.Tanh, scale=0.5),
                     reads=[r_ps[b]], writes=[r_th[ti]])
                S.op("dve", lambda e, b=b, ti=ti: e.scalar_tensor_tensor(
                    out=tt[ti][:], in0=th[ti][:], scalar=1.0, in1=psf[b][:], op0=ALU.add, op1=ALU.mult),
                    reads=[r_th[ti], r_ps[b]], writes=[r_tt[ti]])
                S.op("dve", lambda e, fc=fc, ti=ti: e.scalar_tensor_tensor(
                    out=retT_v[:, fc, :], in0=tt[ti][:], scalar=0.5, in1=retT_v[:, fc, :], op0=ALU.mult, op1=ALU.mult),
                    reads=[r_tt[ti], r_v], writes=[r_v])
            W.release(j)
        dump("retT", retT_v.rearrange("p a b -> p (a b)"), [r_v])

        def branch_merge(jl_range, j0, y_pairs_fn, y_reads, gate_sl, gate_rs, mode):
            nonlocal cnt
            for jl in jl_range:
                jd = j0 + jl
                by = bank()
                S.op("pe", lambda e, jl=jl, by=by: mm_group(e, psf[by][:], y_pairs_fn(jl)),
                     reads=y_reads, writes=[r_ps[by]])
                bg = bank()
                S.op("pe", lambda e, jl=jl, bg=bg: mm_group(
                    e, psf[bg][:], [(k8(gate_sl)[:, kc, jl * 128:(jl + 1) * 128], hT[:, kc, :]) for kc in range(8)]),
                    reads=[gate_rs, r_hT], writes=[r_ps[bg]])
                ti = cnt % 2
                cnt += 1
                S.op("act", lambda e, bg=bg, ti=ti: e.activation(out=th[ti][:], in_=psf[bg][:], func=AF.Tanh,
                                                                 scale=0.5),
                     reads=[r_ps[bg]], writes=[r_th[ti]])
                if mode == "set":
                    S.op("dve", lambda e, by=by, ti=ti, jd=jd: e.scalar_tensor_tensor(
                        out=mergedT[:, jd, :], in0=th[ti][:], scalar=1.0, in1=psf[by][:], op0=ALU.add, op1=ALU.mult),
                        reads=[r_th[ti], r_ps[by]], writes=[r_merged[jd], r_xn[jd // 2]])
                else:
                    S.op("dve", lambda e, by=by, ti=ti: e.scalar_tensor_tensor(
                        out=tt[ti][:], in0=th[ti][:], scalar=1.0, in1=psf[by][:], op0=ALU.add, op1=ALU.mult),
                        reads=[r_th[ti], r_ps[by]], writes=[r_tt[ti]])
                    if mode == "add":
                        S.op("dve", lambda e, ti=ti, jd=jd: e.tensor_tensor(
                            out=mergedT[:, jd, :], in0=mergedT[:, jd, :], in1=tt[ti][:], op=ALU.add),
                            reads=[r_tt[ti], r_merged[jd]], writes=[r_merged[jd]])
                    else:
                        S.op("dve", lambda e, ti=ti, jd=jd: e.tensor_tensor(
                            out=mbf_v[:, jd, :], in0=mergedT[:, jd, :], in1=tt[ti][:], op=ALU.add),
                            reads=[r_tt[ti], r_merged[jd]], writes=[r_mbf, r_pooledT, r_ypT])

        if first:
            for mi in range(2):
                b = bank()

                def trm(e, b=b, mi=mi):
                    last = None
                    for k in range(8):
                        last = e.transpose(out=psb[b][:, k * 128:(k + 1) * 128], in_=hb[mi][:, k * 128:(k + 1) * 128],
                                           identity=ident)
                    return last
                S.op("pe", trm, reads=[r_hb[mi], r_const, r_const2], writes=[r_ps[b]])
                S.op("act", lambda e, mi=mi, b=b: e.activation(
                    out=memT[:, :, mi * 128:(mi + 1) * 128],
                    in_=psb[b][:, 0:1024].rearrange("p (k t) -> p k t", k=8), func=AF.Copy),
                    reads=[r_ps[b]], writes=[r_memT])
            j, sl, rs = wtake("kvk")
            for h in range(4):
                b = bank()
                S.op("pe", lambda e, h=h, b=b, sl=sl: mm_group(
                    e, psf[b][:, 0:MEM], [(k8(sl)[:, kc, h * 128:(h + 1) * 128], memT[:, kc, :]) for kc in range(8)]),
                    reads=[rs, r_memT], writes=[r_ps[b]])
                S.op("act", lambda e, h=h, b=b: e.activation(out=kmT[:, h, :], in_=psf[b][:, 0:MEM], func=AF.Copy),
                     reads=[r_ps[b]], writes=[r_kmT])
            W.release(j)
            j, sl, rs = wtake("kvv")
            for mc in range(2):
                b = bank()
                S.op("pe", lambda e, mc=mc, b=b, sl=sl: mm_group(
                    e, psf[b][:], [(memT[:, kc, mc * 128:(mc + 1) * 128], k8(sl)[:, kc, :]) for kc in range(8)]),
                    reads=[rs, r_memT], writes=[r_ps[b]])
                S.op("act", lambda e, mc=mc, b=b: e.activation(out=vm[:, mc, :], in_=psf[b][:], func=AF.Copy),
                     reads=[r_ps[b]], writes=[r_vm])
            W.release(j)

        jh, slh, rsh = wtake("hp")
        jq, slq, rsq = wtake("qx")
        for i in range(NT):
            b = bank()
            S.op("pe", lambda e, i=i, b=b, slh=slh: mm_group(
                e, psf[b][:], [(hT[:, kc, i * 128:(i + 1) * 128], k8(slh)[:, kc, :]) for kc in range(8)]),
                reads=[rsh, r_hT], writes=[r_ps[b]])
            S.op("act", lambda e, i=i, b=b: e.activation(out=hp_tok[:, i + 1, :], in_=psf[b][:], func=AF.Copy),
                 reads=[r_ps[b]], writes=[r_hp[i + 1]])
            h = i
            b = bank()
            S.op("pe", lambda e, h=h, b=b, slq=slq: mm_group(
                e, psf[b][:], [(k8(slq)[:, kc, h * 128:(h + 1) * 128], hT[:, kc, :]) for kc in range(8)]),
                reads=[rsq, r_hT], writes=[r_ps[b]])
            S.op("act", lambda e, h=h, b=b: e.activation(out=qxT[:, h, :], in_=psf[b][:], func=AF.Copy),
                 reads=[r_ps[b]], writes=[r_qxT])
        W.release(jh)
        W.release(jq)
        pcnt = 0
        for step in range(4):
            gq = step
            b = bank()

            def poolmm(e, gq=gq, b=b):
                last = None
                for i in range(NT):
                    o_ap = psf[b][:, i * 128:(i + 1) * 128]
                    cur = hp_tok[:, i + 1, gq * 128:(gq + 1) * 128]
                    if first and i == 0:
                        last = e.matmul(o_ap, lhsT=cur, rhs=cst[:, 8 + gq, :], start=True, stop=True)
                    else:
                        e.matmul(o_ap, lhsT=cur, rhs=cst[:, gq, :], start=True, stop=False)
                        last = e.matmul(o_ap, lhsT=hp_tok[:, i, gq * 128:(gq + 1) * 128], rhs=cst[:, 4 + gq, :],
                                        start=False, stop=True)
                return last
            S.op("pe", poolmm, reads=r_hp + [r_const, r_const2], writes=[r_ps[b]])
            S.op("act", lambda e, gq=gq, b=b: e.activation(out=pooledT_v[:, gq, :], in_=psf[b][:], func=AF.Copy),
                 reads=[r_ps[b]], writes=[r_pooledT] + ([r_mbf] if gq == 0 else []))
            h = step
            pis = []
            for mc in range(2):
                bsx = bank()
                pi = pcnt % 4
                pcnt += 1
                pis.append(pi)
                S.op("pe", lambda e, h=h, mc=mc, bsx=bsx: e.matmul(
                    psf[bsx][:], lhsT=kmT[:, h, mc * 128:(mc + 1) * 128], rhs=qxT[:, h, :], start=True, stop=True),
                    reads=[r_kmT, r_qxT], writes=[r_ps[bsx]])
                S.op("act", lambda e, bsx=bsx, pi=pi: e.activation(out=pT[pi][:], in_=psf[bsx][:], func=AF.Exp,
                                                                   scale=float(128.0 ** -0.5)),
                     reads=[r_ps[bsx]], writes=[r_pT[pi]])
            b2 = bank()
            S.op("pe", lambda e, gq=gq, b2=b2: e.matmul(psf[b2][:], lhsT=wpool[:, gq, :], rhs=pooledT_v[:, gq, :],
                                                        start=True, stop=True),
                 reads=[r_pooledT, r_const, r_const2], writes=[r_ps[b2]])
            S.op("act", lambda e, gq=gq, b2=b2: e.activation(out=ypT_v[:, gq, :], in_=psf[b2][:], func=AF.Copy,
                                                             scale=pp[:, PS_OFF + gq:PS_OFF + gq + 1]),
                 reads=[r_ps[b2], r_const, r_const2], writes=[r_ypT])
            bo_ = bank()
            S.op("pe", lambda e, h=h, bo_=bo_, pis=tuple(pis): mm_group(
                e, psf[bo_][:], [(vm[:, mc, h * 128:(h + 1) * 128], pT[pis[mc]][:]) for mc in range(2)]),
                reads=[r_vm, r_pT[pis[0]], r_pT[pis[1]]], writes=[r_ps[bo_]])
            bden = bank()
            S.op("pe", lambda e, bden=bden, pis=tuple(pis): mm_group(
                e, psf[bden][:], [(ones, pT[pis[mc]][:]) for mc in range(2)]),
                reads=[r_const, r_const2, r_pT[pis[0]], r_pT[pis[1]]], writes=[r_ps[bden]])
            S.op("dve", lambda e, bden=bden: e.reciprocal(out=rden[:], in_=psf[bden][:]),
                 reads=[r_ps[bden]], writes=[r_rden])
            S.op("dve", lambda e, h=h, bo_=bo_: e.tensor_tensor(out=oT[:, h, :], in0=psf[bo_][:], in1=rden[:],
                                                               op=ALU.mult),
                 reads=[r_ps[bo_], r_rden], writes=[r_oT])
        S.op("act", lambda e: e.activation(out=hp_tok[:, 0, :], in_=hp_tok[:, NT, :], func=AF.Copy),
             reads=[r_hp[NT]], writes=[r_hp[0]])
        dump("pooledT", pooledT_v.rearrange("p a b -> p (a b)"), [r_pooledT])
        dump("oT", oT[:].rearrange("p a b -> p (a b)"), [r_oT])

        ja, sla, rsa = wtake("a")
        for u in range(2):
            jg, slg, rsg = wtake("gp%d" % u)
            branch_merge(range(4), u * 4,
                         lambda jl, u=u, sla=sla: [(k4(sla)[:, q4, (u * 4 + jl) * 128:(u * 4 + jl + 1) * 128],
                                                   ypT_v[:, q4, :]) for q4 in range(4)],
                         [rsa, r_ypT], slg, rsg, "set")
            W.release(jg)
        W.release(ja)
        for u in range(2):
            jr, slr, rsr = wtake("r%d" % u)
            jg, slg, rsg = wtake("gret%d" % u)
            branch_merge(range(4), u * 4,
                         lambda jl, slr=slr: [(k8(slr)[:, kc, jl * 128:(jl + 1) * 128], retT_v[:, kc, :])
                                              for kc in range(8)],
                         [rsr, r_v], slg, rsg, "add")
            W.release(jr)
            W.release(jg)
        jc, slc, rsc = wtake("c")
        for u in range(2):
            jg, slg, rsg = wtake("gm%d" % u)
            branch_merge(range(4), u * 4,
                         lambda jl, u=u, slc=slc: [(k4(slc)[:, q4, (u * 4 + jl) * 128:(u * 4 + jl + 1) * 128],
                                                   oT[:, q4, :]) for q4 in range(4)],
                         [rsc, r_oT], slg, rsg, "final")
            W.release(jg)
        W.release(jc)
        S.op("act", lambda e: e.activation(out=dmy[:], in_=dmy[:], func=AF.Sqrt), reads=[r_dmy], writes=[r_dmy])
        dump("mbf", mbf_v.rearrange("p a b -> p (a b)"), [r_mbf])

        jo0, slo0, rso0 = wtake("out0")
        jo1, slo1, rso1 = wtake("out1")
        for i in range(NT):
            for c2, slo, rso in ((0, slo0, rso0), (1, slo1, rso1)):
                b = bank()
                S.op("pe", lambda e, i=i, b=b, slo=slo: mm_group(
                    e, psf[b][:], [(mbf_v[:, kc, i * 128:(i + 1) * 128], k8(slo)[:, kc, :]) for kc in range(8)]),
                    reads=[rso, r_mbf], writes=[r_ps[b]])
                S.op("dve", lambda e, i=i, b=b, c2=c2: e.scalar_tensor_tensor(
                    out=xres[:, i, c2 * 512:(c2 + 1) * 512], in0=psf[b][:], scalar=0.5,
                    in1=xres[:, i, c2 * 512:(c2 + 1) * 512], op0=ALU.mult, op1=ALU.add),
                    reads=[r_ps[b], r_x[i]], writes=[r_x[i]])
        W.release(jo0)
        W.release(jo1)
        dump("x2", xres[:].rearrange("p a b -> p (a b)"), r_x)
        if g + 1 < NG:
            load_x(g + 1)

        norm_to_T(lambda i: xres[:, i, :], r_x, gt_ffn, hT, r_hT, NT, 128)
        fence2 = [r_v, r_ktok, r_kT, r_qT, r_on]
        acnt = 0
        if not first:
            S.op("dve", lambda e: e.tensor_tensor(out=btmp[:, 0, :], in0=carry[:, :, 1], in1=pp[:, CW1:CW1 + NF],
                                                  op=ALU.mult), reads=[r_carry, r_const, r_const2], writes=[r_btmp])
            S.op("dve", lambda e: e.tensor_tensor(out=btmp[:, 1, :], in0=carry[:, :, 0], in1=pp[:, CW0:CW0 + NF],
                                                  op=ALU.mult), reads=[r_carry, r_const, r_const2], writes=[r_btmp])
            S.op("dve", lambda e: e.tensor_tensor(out=bnd[:, :, 1], in0=carry[:, :, 1], in1=pp[:, CW0:CW0 + NF],
                                                  op=ALU.mult), reads=[r_carry, r_const, r_const2], writes=[r_bnd])
            S.op("dve", lambda e: e.tensor_tensor(out=bnd[:, :, 0], in0=btmp[:, 0, :], in1=btmp[:, 1, :],
                                                  op=ALU.add), reads=[r_btmp, r_bnd], writes=[r_bnd])
        for u in range(11):
            j, sl, rs = wtake("up%d" % u)
            for fl in range(2):
                f = 2 * u + fl
                ai = acnt % 2
                acnt += 1
                ba = bank()
                S.op("pe", lambda e, fl=fl, ba=ba, sl=sl: mm_group(
                    e, psf[ba][:], [(k8(sl)[:, kc, fl * 128:(fl + 1) * 128], hT[:, kc, :]) for kc in range(8)]),
                    reads=[rs, r_hT], writes=[r_ps[ba]])
                bb = bank()
                S.op("pe", lambda e, fl=fl, bb=bb, sl=sl: mm_group(
                    e, psf[bb][:], [(k8(sl)[:, kc, 256 + fl * 128:256 + (fl + 1) * 128], hT[:, kc, :])
                                    for kc in range(8)]),
                    reads=[rs, r_hT], writes=[r_ps[bb]])
                S.op("act", lambda e, f=f, ba=ba, ai=ai: e.activation(
                    out=acc_v[ai], in_=psf[ba][:], func=AF.Identity, scale=pp[:, CW2 + f:CW2 + f + 1],
                    bias=pp[:, CB + f:CB + f + 1]),
                    reads=[r_ps[ba], r_const, r_const2], writes=[r_acc[ai]] + (fence2 + r_gTk if (u == 0 and fl == 0) else []))

                S.op("dve", lambda e, f=f, ba=ba, ai=ai: e.scalar_tensor_tensor(
                    out=acc_v[ai][:, 1:T], in0=psf[ba][:, 0:T - 1], scalar=pp[:, CW1 + f:CW1 + f + 1],
                    in1=acc_v[ai][:, 1:T], op0=ALU.mult, op1=ALU.add),
                    reads=[r_ps[ba], r_acc[ai], r_const, r_const2], writes=[r_acc[ai]])
                S.op("dve", lambda e, f=f, ba=ba, ai=ai: e.scalar_tensor_tensor(
                    out=acc_v[ai][:, 2:T], in0=psf[ba][:, 0:T - 2], scalar=pp[:, CW0 + f:CW0 + f + 1],
                    in1=acc_v[ai][:, 2:T], op0=ALU.mult, op1=ALU.add),
                    reads=[r_ps[ba], r_acc[ai], r_const, r_const2], writes=[r_acc[ai]])
                if not first:
                    S.op("dve", lambda e, f=f, ai=ai: e.tensor_tensor(
                        out=acc_v[ai][:, 0:2], in0=acc_v[ai][:, 0:2], in1=bnd[:, f, :], op=ALU.add),
                        reads=[r_acc[ai], r_bnd], writes=[r_acc[ai]])
                S.op("act", lambda e, f=f, ba=ba: e.activation(out=carry[:, f, :], in_=psf[ba][:, T - 2:T],
                                                               func=AF.Copy),
                     reads=[r_ps[ba]], writes=[r_carry])
                S.op("act", lambda e, ai=ai: e.activation(out=g1_v[ai], in_=acc_v[ai], func=AF.Gelu_apprx_tanh),
                     reads=[r_acc[ai]], writes=[r_g1[ai]])
                S.op("dve", lambda e, f=f, bb=bb, ai=ai: e.tensor_tensor(out=gT_v[:, f, :], in0=g1_v[ai],
                                                                         in1=psf[bb][:], op=ALU.mult),
                     reads=[r_g1[ai], r_ps[bb]], writes=[r_gTk[f // 8]])
            W.release(j)
        dump("gT", gT_v.rearrange("p a b -> p (a b)"), r_gTk)

        nxt = (g + 1 < NG)
        if nxt:
            norm_stats(lambda i: xn_v[:, i, :], r_xn, NT)

        for c2 in range(2):
            bks = [bank() for _ in range(NT)]
            for k in range(3):
                j, sl, rs = wtake("dn%d_%d" % (c2, k))
                nf = min(8, NF - 8 * k)

                def down(e, k=k, nf=nf, sl=sl, bks=bks):
                    last = None
                    for i in range(NT):
                        for fl in range(nf):
                            f = 8 * k + fl
                            last = e.matmul(psf[bks[i]][:], lhsT=gT_v[:, f, i * 128:(i + 1) * 128],
                                            rhs=k8(sl)[:, fl, :], start=(f == 0), stop=(f == NF - 1))
                    return last
                S.op("pe", down, reads=[rs, r_gTk[k]], writes=[r_ps[bk] for bk in bks])
                W.release(j)
                if nxt and c2 == 1 and k == 0:
                    for i in range(NT):
                        norm_tile(i, lambda i: xn_v[:, i, :], r_xn, gt_mix, hT, r_hT)
            for i in range(NT):
                S.op("dve", lambda e, i=i, c2=c2, bks=bks: e.tensor_tensor(
                    out=xres[:, i, c2 * 512:(c2 + 1) * 512], in0=psf[bks[i]][:],
                    in1=xres[:, i, c2 * 512:(c2 + 1) * 512], op=ALU.add),
                    reads=[r_ps[bks[i]], r_x[i]], writes=[r_x[i]])

        for i in range(NT):
            S.op("act", lambda e, i=i: e.activation(out=junk[:], in_=xres[:, i, :], func=AF.Square,
                                                    accum_out=ssf[:, i:i + 1]),
                 reads=[r_x[i]], writes=[r_junk, r_ssf])
        S.op("dve", lambda e: e.tensor_scalar(out=rstdf[:], in0=ssf[:], scalar1=1.0 / D, scalar2=EPS, op0=ALU.mult,
                                              op1=ALU.add), reads=[r_ssf], writes=[r_rstdf])
        S.op("act", lambda e: e.activation(out=rstdf[:], in_=rstdf[:], func=AF.Sqrt), reads=[r_rstdf],
             writes=[r_rstdf])
        S.op("dve", lambda e: e.reciprocal(out=rstdf[:], in_=rstdf[:]), reads=[r_rstdf], writes=[r_rstdf])

        def tail_part(i, tok0=tok0, nxt=nxt):
            yi = i % 4
            S.op("dve", lambda e, i=i, yi=yi: e.scalar_tensor_tensor(
                out=yst[yi][:], in0=xres[:, i, :], scalar=rstdf[:, i:i + 1], in1=gt_fin[:], op0=ALU.mult,
                op1=ALU.mult),
                reads=[r_x[i], r_rstdf, r_gt], writes=[r_yst[yi]])
            o = S.op("act", lambda e, i=i, yi=yi, tok0=tok0: e.dma_start(
                out=y_d[tok0 + i * 128: tok0 + (i + 1) * 128, :], in_=yst[yi][:]),
                reads=[r_yst[yi]], dma=d_yst[yi], name="ystore")
            out_ops.append(o)
            if nxt:
                S.op("act", lambda e, i=i: e.activation(out=xres[:, i, :], in_=xn_v[:, i, :], func=AF.Copy),
                     reads=[r_xn[i]], writes=[r_x[i]])

        def tail(tail_part=tail_part):
            for i in range(NT):
                tail_part(i)
        tail.part = tail_part
        pending.append(tail)

    pending = []
    norm_to_T(lambda i: xn_v[:, i, :], r_xn, gt_mix, hT, r_hT, NT, 128)
    copy_x()
    for g in range(NG):
        do_group(g)
    pending.pop()()

    fin = S.op("sp", lambda e: None, name="final_wait")
    fin.deps = list(out_ops[-4:]) + list(d_dbg.ops)
    for o in fin.deps:
        o.signal = True
    with nc.Block() as block:
        S.emit(block)
    return nc


def make_consts():
    f32 = np.float32
    half = 64
    inv = (np.float32(10000.0) ** (-np.arange(half, dtype=f32) / np.float32(half))).astype(f32)
    pos = np.arange(SEQ, dtype=f32)
    ang = (pos[:, None] * inv[None, :]).astype(f32).astype(np.float64)
    cos = np.cos(ang)
    sin = np.sin(ang)
    lg = np.log1p(-np.exp2(-5.0 - np.arange(4, dtype=np.float64)))
    p1 = ((np.arange(16)[:, None] % NT) * 128 + np.arange(1, 129)[None, :]).astype(np.float64)
    qd = np.exp(p1[:, :, None] * lg[None, None, :])
    kd = np.exp(-p1[:, :, None] * lg[None, None, :]) * (128.0 ** -0.5)
    ropet = np.zeros((16, 128, 4, 4, 64), np.float64)
    cosr = cos.reshape(16, 128, 1, 64)
    sinr = sin.reshape(16, 128, 1, 64)
    ropet[:, :, 0] = cosr * qd[:, :, :, None]
    ropet[:, :, 1] = sinr * qd[:, :, :, None]
    ropet[:, :, 2] = cosr * kd[:, :, :, None]
    ropet[:, :, 3] = sinr * kd[:, :, :, None]
    ropet = ropet.reshape(16, 128, 1024).astype(f32)
    kk = np.arange(128)[:, None]
    qq = np.arange(128)[None, :]
    m = (kk <= qq).astype(f32)
    mask4 = np.repeat(m[:, None, :], 4, axis=1).reshape(128, 512).astype(f32)
    cst = np.zeros((128, 14, 128), np.float64)
    tp = np.arange(128)[:, None]
    t = np.arange(128)[None, :]
    for gq, w in enumerate((2, 4, 8, 16)):
        cst[:, gq, :] = ((tp <= t) & (tp > t - w)) / w - (tp == t)
        cst[:, 4 + gq, :] = ((tp - 128) > (t - w)) / w
        cst[:, 8 + gq, :] = ((tp <= t) & (tp > t - w)) / np.minimum(t + 1, w) - (tp == t)
    cst[:, 12, :] = np.eye(128)
    cst[:, 13, :] = 1.0
    return ropet, mask4, cst.astype(f32)


_PROGRAM = {}


def kernel(x, mem, g_mix, w_in, w_pool, pool_scale, w_a, g_ret, b_ret, w_r, g_mem, w_mem_kv, w_c, w_out,
           g_ffn, w_up, conv_w, conv_b, w_down, g_final, _dbg=None):
    f32 = np.float32
    x = np.asarray(x, f32)
    mem = np.asarray(mem, f32)
    ropet, mask4, cst = make_consts()
    gvec = np.ascontiguousarray(np.stack([np.asarray(g_mix, f32)[0], np.asarray(g_ffn, f32)[0],
                                          np.asarray(g_final, f32), np.asarray(g_mem, f32)[0]]))
    cw = np.asarray(conv_w, f32)[0]
    pp = np.concatenate([
        np.asarray(pool_scale, f32)[0].reshape(4, 128).T,
        np.asarray(g_ret, f32)[0].reshape(8, 128).T,
        np.asarray(b_ret, f32)[0].reshape(8, 128).T,
        cw[0].reshape(NF, 128).T, cw[1].reshape(NF, 128).T, cw[2].reshape(NF, 128).T,
        np.asarray(conv_b, f32)[0].reshape(NF, 128).T], axis=1)
    pp = np.ascontiguousarray(pp, dtype=f32)
    shared = {
        "w_in": np.ascontiguousarray(np.asarray(w_in, f32)[0]),
        "w_pool": np.ascontiguousarray(np.asarray(w_pool, f32)[0]),
        "w_a": np.ascontiguousarray(np.asarray(w_a, f32)[0]),
        "w_r": np.ascontiguousarray(np.asarray(w_r, f32)[0]),
        "w_mem_kv": np.ascontiguousarray(np.asarray(w_mem_kv, f32)[0]),
        "w_c": np.ascontiguousarray(np.asarray(w_c, f32)[0]),
        "w_out": np.ascontiguousarray(np.asarray(w_out, f32)[0]),
        "w_up": np.ascontiguousarray(np.asarray(w_up, f32)[0]),
        "w_down": np.ascontiguousarray(np.asarray(w_down, f32)[0]),
        "gvec": gvec, "pp": pp, "ropet": ropet, "mask4": mask4, "cst": cst,
    }
    in_maps = []
    for c in range(NCORES):
        m = dict(shared)
        m["x"] = np.ascontiguousarray(x[2 * c:2 * c + 2].reshape(2 * SEQ, D))
        m["mem"] = np.ascontiguousarray(mem[2 * c:2 * c + 2].reshape(2 * MEM, D))
        in_maps.append(m)
    key = tuple(sorted(_dbg.items())) if _dbg else None
    if key not in _PROGRAM:
        _PROGRAM[key] = build_program(_dbg)
    nc = _PROGRAM[key]
    res = run_bass_kernel_spmd(nc, in_maps, core_ids=list(range(NCORES)))
    out = np.concatenate([np.asarray(r["y"], f32).reshape(2, SEQ, D) for r in res.results], axis=0)
    if _dbg:
        return out, res.results
    return out
```
